# Optimizing a Trainium2 kernel written in Bass

```python
import math
import jax, jax.numpy as jnp
from jax import lax
import numpy as np

D_MODEL = 2048
BATCH = 8
SEQ = 2048
DEPTH = 2
DEC_BATCH = 16
DEC_SEQ = 16
PAST_LEN = 2048

CHUNK = 64
QBLOCK = 128
PLE_DIM = 256
MIX_WIDTH = D_MODEL
ATTN_WIDTH = MIX_WIDTH // 2
SSM_WIDTH = MIX_WIDTH - ATTN_WIDTH
HEAD_DIM = 128
N_HEADS = ATTN_WIDTH // HEAD_DIM
SSM_GROUP = 16
N_SSM_GROUPS = SSM_WIDTH // SSM_GROUP
SSM_STATE = 64
D_FF = ((8 * D_MODEL + 3 * 256 - 1) // (3 * 256)) * 256
IN_COLS = 3 * ATTN_WIDTH + N_HEADS + SSM_WIDTH
EPS = 1e-6
NEG_INF = -1e30

kernel_name = 'hybrid_fox_s5_stream_step'


def rms_norm(x, g):
    xf = x.astype(jnp.float32)
    y = xf * lax.rsqrt(jnp.mean(xf * xf, axis=-1, keepdims=True) + EPS)
    return (y * g.astype(jnp.float32)).astype(x.dtype)


def fox_attend(q, k, v, cq, ck, pos_q, pos_k):
    s = jnp.einsum('bqhd,bkhd->bhqk', q, k, preferred_element_type=jnp.float32) * (HEAD_DIM ** -0.5)
    bias = cq.astype(jnp.float32).transpose(0, 2, 1)[..., :, None] - ck.astype(jnp.float32).transpose(0, 2, 1)[..., None, :]
    s = jnp.where(pos_k[None, :] <= pos_q[:, None], s + bias, NEG_INF)
    p = jax.nn.softmax(s, axis=-1)
    return jnp.einsum('bhqk,bkhd->bqhd', p.astype(v.dtype), v)


def fox_prompt(q, k, v, c):
    b, t, h, d = q.shape
    nb = t // QBLOCK
    pos = jnp.arange(t)
    qb = q.reshape(b, nb, QBLOCK, h, d).transpose(1, 0, 2, 3, 4)
    cb = c.reshape(b, nb, QBLOCK, h).transpose(1, 0, 2, 3)
    pb = pos.reshape(nb, QBLOCK)

    def one_block(args):
        qi, ci, pi = args
        return fox_attend(qi, k, v, ci, c, pi, pos)

    out = lax.map(one_block, (qb, cb, pb))
    return out.transpose(1, 0, 2, 3, 4).reshape(b, t, h * d)


def s5_mixer(u, x0_re, x0_im, a_re, a_im, log_dt, b_re, b_im, c_re, c_im, d_skip, w_glu, b_glu):
    f32 = jnp.float32
    bsz, t, _ = u.shape
    ug = u.astype(f32).reshape(bsz, t, N_SSM_GROUPS, SSM_GROUP)
    ar, ai = a_re.astype(f32), a_im.astype(f32)
    dt = jnp.exp(log_dt.astype(f32))[:, None]
    mag = jnp.exp(ar * dt)
    ang = ai * dt
    abar_re, abar_im = mag * jnp.cos(ang), mag * jnp.sin(ang)
    den = ar * ar + ai * ai
    nr, ni = abar_re - 1.0, abar_im
    coef_re = (nr * ar + ni * ai) / den
    coef_im = (ni * ar - nr * ai) / den
    br, bi = b_re.astype(f32), b_im.astype(f32)
    bbar_re = coef_re[..., None] * br - coef_im[..., None] * bi
    bbar_im = coef_re[..., None] * bi + coef_im[..., None] * br
    bu_re = jnp.einsum('gpc,btgc->btgp', bbar_re, ug)
    bu_im = jnp.einsum('gpc,btgc->btgp', bbar_im, ug)
    x0r, x0i = x0_re.astype(f32), x0_im.astype(f32)
    bu_re = bu_re.at[:, 0].add(abar_re * x0r - abar_im * x0i)
    bu_im = bu_im.at[:, 0].add(abar_re * x0i + abar_im * x0r)
    a_r = jnp.broadcast_to(abar_re, bu_re.shape)
    a_i = jnp.broadcast_to(abar_im, bu_re.shape)

    def combine(e1, e2):
        a1r, a1i, b1r, b1i = e1
        a2r, a2i, b2r, b2i = e2
        return (a2r * a1r - a2i * a1i,
                a2r * a1i + a2i * a1r,
                a2r * b1r - a2i * b1i + b2r,
                a2r * b1i + a2i * b1r + b2i)

    _, _, xr, xi = lax.associative_scan(combine, (a_r, a_i, bu_re, bu_im), axis=1)
    y = (jnp.einsum('gcp,btgp->btgc', c_re.astype(f32), xr)
         - jnp.einsum('gcp,btgp->btgc', c_im.astype(f32), xi)
         + d_skip.astype(f32).reshape(N_SSM_GROUPS, SSM_GROUP) * ug)
    g = jax.nn.gelu(y.reshape(bsz, t, SSM_WIDTH))
    out = g * jax.nn.sigmoid(g @ w_glu.astype(f32) + b_glu.astype(f32))
    return out.astype(u.dtype), xr[:, -1], xi[:, -1]


def trunk_layer(x, p_i, k_past, v_past, logf_past, s_re, s_im,
                g_mix, w_in, b_f, g_q, g_k, a_re, a_im, log_dt, b_re, b_im, c_re, c_im,
                d_skip, w_glu, b_glu, g_attn_out, g_ssm_out, w_out, g_ffn, w_gate, w_up,
                w_down, g_ple, w_ple_gate, w_ple_proj):
    bsz, t, _ = x.shape
    a = rms_norm(x, g_mix)
    proj = a @ w_in
    q = proj[..., :ATTN_WIDTH].reshape(bsz, t, N_HEADS, HEAD_DIM)
    k = proj[..., ATTN_WIDTH:2 * ATTN_WIDTH].reshape(bsz, t, N_HEADS, HEAD_DIM)
    v = proj[..., 2 * ATTN_WIDTH:3 * ATTN_WIDTH].reshape(bsz, t, N_HEADS, HEAD_DIM)
    f_logit = proj[..., 3 * ATTN_WIDTH:3 * ATTN_WIDTH + N_HEADS]
    u = proj[..., 3 * ATTN_WIDTH + N_HEADS:]
    q = rms_norm(q, g_q)
    k = rms_norm(k, g_k)
    logf = jax.nn.log_sigmoid((f_logit + b_f).astype(jnp.float32))
    if k_past is None:
        c = jnp.cumsum(logf, axis=1)
        attn = fox_prompt(q, k, v, c)
    else:
        past = k_past.shape[1]
        k_all = jnp.concatenate([k_past.astype(k.dtype), k], axis=1)
        v_all = jnp.concatenate([v_past.astype(v.dtype), v], axis=1)
        c_all = jnp.cumsum(jnp.concatenate([logf_past.astype(jnp.float32), logf], axis=1), axis=1)
        pos_k = jnp.arange(past + t)
        pos_q = past + jnp.arange(t)
        attn = fox_attend(q, k_all, v_all, c_all[:, past:], c_all, pos_q, pos_k).reshape(bsz, t, ATTN_WIDTH)
    ssm, s_re_new, s_im_new = s5_mixer(u, s_re, s_im, a_re, a_im, log_dt, b_re, b_im,
                                       c_re, c_im, d_skip, w_glu, b_glu)
    merged = jnp.concatenate([rms_norm(attn, g_attn_out), rms_norm(ssm, g_ssm_out)], axis=-1)
    h = x + merged @ w_out
    f = rms_norm(h, g_ffn)
    h = h + (jax.nn.silu(f @ w_gate) * (f @ w_up)) @ w_down
    gate = jax.nn.sigmoid(rms_norm(h, g_ple) @ w_ple_gate)
    h = h + gate * (p_i @ w_ple_proj)
    return h, k, v, logf, s_re_new, s_im_new


def setup_inputs(seed: int = 0) -> dict:
    key = jax.random.key(seed)
    ks = iter(jax.random.split(key, 48))
    f32 = jnp.float32

    def nrm(shape, scale):
        return scale * jax.random.normal(next(ks), shape, f32)

    def gain(shape):
        return 1.0 + nrm(shape, 0.02)

    L, G, P = DEPTH, N_SSM_GROUPS, SSM_STATE
    return {
        'x_prompt': nrm((BATCH, SEQ, D_MODEL), 1.0),
        'x_sample': nrm((DEC_BATCH, DEC_SEQ, D_MODEL), 1.0),
        'cache_k': nrm((L, DEC_BATCH, PAST_LEN, N_HEADS, HEAD_DIM), 1.0),
        'cache_v': nrm((L, DEC_BATCH, PAST_LEN, N_HEADS, HEAD_DIM), 1.0),
        'cache_logf': jax.nn.log_sigmoid(3.0 + nrm((L, DEC_BATCH, PAST_LEN, N_HEADS), 1.0)),
        'state_ssm_re': nrm((L, DEC_BATCH, G, P), 0.1),
        'state_ssm_im': nrm((L, DEC_BATCH, G, P), 0.1),
        'p_prompt': nrm((L, BATCH, SEQ, PLE_DIM), 1.0),
        'p_sample': nrm((L, DEC_BATCH, DEC_SEQ, PLE_DIM), 1.0),
        'g_mix': gain((L, D_MODEL)),
        'w_in': nrm((L, D_MODEL, IN_COLS), D_MODEL ** -0.5),
        'b_f': 3.0 + nrm((L, N_HEADS), 0.5),
        'g_q': gain((L, HEAD_DIM)),
        'g_k': gain((L, HEAD_DIM)),
        'a_re': -0.5 + nrm((L, G, P), 0.01),
        'a_im': math.pi * jnp.arange(P, dtype=f32)[None, None, :] + nrm((L, G, P), 0.01),
        'log_dt': jax.random.uniform(next(ks), (L, G), f32, math.log(1e-3), math.log(1e-1)),
        'b_re': nrm((L, G, P, SSM_GROUP), (2 * SSM_GROUP) ** -0.5),
        'b_im': nrm((L, G, P, SSM_GROUP), (2 * SSM_GROUP) ** -0.5),
        'c_re': nrm((L, G, SSM_GROUP, P), (2 * P) ** -0.5),
        'c_im': nrm((L, G, SSM_GROUP, P), (2 * P) ** -0.5),
        'd_skip': nrm((L, SSM_WIDTH), 1.0),
        'w_glu': nrm((L, SSM_WIDTH, SSM_WIDTH), SSM_WIDTH ** -0.5),
        'b_glu': nrm((L, SSM_WIDTH), 0.02),
        'g_attn_out': gain((L, ATTN_WIDTH)),
        'g_ssm_out': gain((L, SSM_WIDTH)),
        'w_out': nrm((L, MIX_WIDTH, D_MODEL), MIX_WIDTH ** -0.5),
        'g_ffn': gain((L, D_MODEL)),
        'w_gate': nrm((L, D_MODEL, D_FF), D_MODEL ** -0.5),
        'w_up': nrm((L, D_MODEL, D_FF), D_MODEL ** -0.5),
        'w_down': nrm((L, D_FF, D_MODEL), D_FF ** -0.5),
        'g_ple': gain((L, D_MODEL)),
        'w_ple_gate': nrm((L, D_MODEL, D_MODEL), D_MODEL ** -0.5),
        'w_ple_proj': nrm((L, PLE_DIM, D_MODEL), PLE_DIM ** -0.5),
    }


def reference(x_prompt, x_sample, cache_k, cache_v, cache_logf, state_ssm_re, state_ssm_im,
              p_prompt, p_sample, g_mix, w_in, b_f, g_q, g_k, a_re, a_im, log_dt, b_re, b_im,
              c_re, c_im, d_skip, w_glu, b_glu, g_attn_out, g_ssm_out, w_out, g_ffn, w_gate,
              w_up, w_down, g_ple, w_ple_gate, w_ple_proj):
    assert x_sample.shape[1] <= CHUNK
    zero_state = jnp.zeros((x_prompt.shape[0], N_SSM_GROUPS, SSM_STATE), jnp.float32)
    hp, hs = x_prompt, x_sample
    kp, vp, fp, srp, sip = [], [], [], [], []
    ksm, vsm, fsm, srs, sis = [], [], [], [], []
    for i in range(DEPTH):
        w = (g_mix[i], w_in[i], b_f[i], g_q[i], g_k[i], a_re[i], a_im[i], log_dt[i], b_re[i],
             b_im[i], c_re[i], c_im[i], d_skip[i], w_glu[i], b_glu[i], g_attn_out[i],
             g_ssm_out[i], w_out[i], g_ffn[i], w_gate[i], w_up[i], w_down[i], g_ple[i],
             w_ple_gate[i], w_ple_proj[i])
        hp, k1, v1, f1, r1, m1 = trunk_layer(hp, p_prompt[i], None, None, None,
                                             zero_state, zero_state, *w)
        hs, k2, v2, f2, r2, m2 = trunk_layer(hs, p_sample[i], cache_k[i], cache_v[i], cache_logf[i],
                                             state_ssm_re[i], state_ssm_im[i], *w)
        kp.append(k1); vp.append(v1); fp.append(f1); srp.append(r1); sip.append(m1)
        ksm.append(k2); vsm.append(v2); fsm.append(f2); srs.append(r2); sis.append(m2)
    y_prompt, y_sample = hp, hs
    k_prompt, v_prompt, logf_prompt = jnp.stack(kp), jnp.stack(vp), jnp.stack(fp)
    ssm_re_prompt, ssm_im_prompt = jnp.stack(srp), jnp.stack(sip)
    k_sample, v_sample, logf_sample = jnp.stack(ksm), jnp.stack(vsm), jnp.stack(fsm)
    ssm_re_sample, ssm_im_sample = jnp.stack(srs), jnp.stack(sis)
    return (y_prompt, y_sample, k_prompt, v_prompt, logf_prompt, ssm_re_prompt, ssm_im_prompt,
            k_sample, v_sample, logf_sample, ssm_re_sample, ssm_im_sample)
```

```python
import math
import os
import numpy as np
from contextlib import ExitStack
import concourse.bass as bass
import concourse.mybir as mybir
from concourse.bass_utils import run_bass_kernel_spmd

F32 = mybir.dt.float32
BF16 = mybir.dt.bfloat16
I32 = mybir.dt.int32
AF = mybir.ActivationFunctionType
ALU = mybir.AluOpType

D = 2048
KT = 16
H = 8
DFF = 5632
FT = 44
INC = 4104
SEQ = 2048
TP = 256
NBT = TP // 128
TS = 32
EPS = 1e-6
TWO_PI = 2.0 * math.pi
NRING = 8


class Op:
    __slots__ = ("eng", "fn", "deps", "dma", "stream", "seq", "waits", "mark", "val", "snap")

    def __init__(self, eng, fn, deps, dma):
        self.eng = eng
        self.fn = fn
        self.deps = deps
        self.dma = dma
        self.mark = False
        self.waits = []


class Prog:
    def __init__(self, nc):
        self.nc = nc
        self.ops = []
        self.lastw = {}
        self.readers = {}
        self.cur = []

    def add(self, eng, fn, r=(), w=(), dma=False):
        self.cur.append((eng, fn, tuple(r), tuple(w), dma))

    def _resolve(self):
        for (eng, fn, r, w, dma) in self.cur:
            idx = len(self.ops)
            deps = set()
            for k in r:
                lw = self.lastw.get(k)
                if lw is not None:
                    deps.add(lw)
            for k in w:
                lw = self.lastw.get(k)
                if lw is not None:
                    deps.add(lw)
                rd = self.readers.get(k)
                if rd:
                    deps.update(rd.values())
            sk = (eng, idx) if dma else eng
            for k in r:
                self.readers.setdefault(k, {})[sk] = idx
            for k in w:
                self.lastw[k] = idx
                self.readers[k] = {}
            self.ops.append(Op(eng, fn, deps, dma))

    def finalize(self, es):
        nc = self.nc
        self._resolve()
        ops = self.ops
        engs = ["pe", "act", "dve", "pool", "sp"]
        cnt = {e: 0 for e in engs}
        dcnt = {e: 0 for e in engs}
        for o in ops:
            if o.dma:
                n = dcnt[o.eng]
                dcnt[o.eng] += 1
                o.stream = (o.eng, n % NRING)
                o.seq = n // NRING + 1
            else:
                cnt[o.eng] += 1
                o.stream = o.eng
                o.seq = cnt[o.eng]
        clock = {e: {} for e in engs}
        for o in ops:
            ck = clock[o.eng]
            need = {}
            if o.dma and o.seq > 1:
                need[o.stream] = (o.seq - 1, None)
            for j in o.deps:
                p = ops[j]
                if p.stream == "pe" and o.eng == "pe" and not o.dma:
                    continue
                cur = need.get(p.stream)
                if cur is None or cur[0] < p.seq:
                    need[p.stream] = (p.seq, j)
            for s, (q, j) in need.items():
                if ck.get(s, 0) >= q:
                    continue
                o.waits.append((s, q, j))
                ck[s] = q
                if j is not None:
                    ops[j].mark = True
                    for s2, q2 in ops[j].snap.items():
                        if ck.get(s2, 0) < q2:
                            ck[s2] = q2
            o.snap = dict(ck)
        rank = {e: 0 for e in engs}
        for o in ops:
            if o.dma:
                o.val = 16 * o.seq
            elif o.mark:
                rank[o.eng] += 1
                o.val = rank[o.eng]
        sems = {}
        for e in ["pe", "act", "dve", "pool"]:
            sems[e] = es.enter_context(nc.semaphore("c_" + e))
        for e in ["pool", "sp"]:
            for i in range(NRING):
                sems[(e, i)] = es.enter_context(nc.semaphore("d_%s%d" % (e, i)))
        streams = {e: [o for o in ops if o.eng == e] for e in engs}
        final = {}
        for o in ops:
            if o.dma:
                final[o.stream] = max(final.get(o.stream, 0), o.val)

        def emit(e, eng):
            for o in streams[e]:
                for (s, q, j) in o.waits:
                    if j is not None:
                        v = ops[j].val
                    else:
                        v = 16 * q
                    eng.wait_ge(sems[s], v)
                ins = o.fn(eng)
                if o.dma:
                    ins.then_inc(sems[o.stream], 16)
                elif o.mark:
                    ins.then_inc(sems[o.eng], 1)
            if e == "sp":
                for s, v in final.items():
                    eng.wait_ge(sems[s], v)

        block = es.enter_context(nc.Block())

        @block.tensor
        def _(e):
            emit("pe", e)

        @block.scalar
        def _(e):
            emit("act", e)

        @block.vector
        def _(e):
            emit("dve", e)

        @block.gpsimd
        def _(e):
            emit("pool", e)

        @block.sync
        def _(e):
            emit("sp", e)


def build(NL=2, NT=SEQ // TP, SAMPLE=True, UPTO=99):
    nc = bass.Bass("TRN2", target_bir_lowering=False)

    def din(name, shape):
        return nc.dram_tensor(name, list(shape), F32, kind="ExternalInput").ap()

    def dout(name, shape):
        return nc.dram_tensor(name, list(shape), F32, kind="ExternalOutput").ap()

    xp = din("xp", [SEQ, D])
    xs = din("xs", [TS, D])
    ck = din("ck", [2, 2, SEQ, 1024])
    cv = din("cv", [2, 2, SEQ, 1024])
    clf = din("clf", [2, 2, SEQ, 8])
    sre = din("sre", [2, 2, 64, 64])
    sim = din("sim", [2, 2, 64, 64])
    pp = din("pp", [2, SEQ, 256])
    psm = din("psm", [2, TS, 256])
    g_mix = din("g_mix", [2, D])
    w_in = din("w_in", [2, D, INC])
    b_f = din("b_f", [2, 8])
    g_q = din("g_q", [2, 128])
    g_k = din("g_k", [2, 128])
    a_re = din("a_re", [2, 64, 64])
    a_im = din("a_im", [2, 64, 64])
    log_dt = din("log_dt", [2, 64])
    b_re = din("b_re", [2, 64, 64, 16])
    b_im = din("b_im", [2, 64, 64, 16])
    c_re = din("c_re", [2, 64, 16, 64])
    c_im = din("c_im", [2, 64, 16, 64])
    d_skip = din("d_skip", [2, 1024])
    w_glu = din("w_glu", [2, 1024, 1024])
    b_glu = din("b_glu", [2, 1024])
    g_ao = din("g_attn_out", [2, 1024])
    g_so = din("g_ssm_out", [2, 1024])
    w_out = din("w_out", [2, D, D])
    g_ffn = din("g_ffn", [2, D])
    w_gate = din("w_gate", [2, D, DFF])
    w_up = din("w_up", [2, D, DFF])
    w_down = din("w_down", [2, DFF, D])
    g_ple = din("g_ple", [2, D])
    w_pg = din("w_ple_gate", [2, D, D])
    w_pp = din("w_ple_proj", [2, 256, D])

    yp = dout("yp", [SEQ, D])
    ys = dout("ys", [TS, D])
    kp = dout("kp", [2, SEQ, 1024])
    vp = dout("vp", [2, SEQ, 1024])
    lfp = dout("lfp", [2, SEQ, 8])
    srp = dout("srp", [2, 32, 128])
    sip = dout("sip", [2, 32, 128])
    kso = dout("kso", [2, TS, 1024])
    vso = dout("vso", [2, TS, 1024])
    lfs = dout("lfs", [2, TS, 8])
    srs = dout("srs", [2, 2, 32, 128])
    sis = dout("sis", [2, 2, 32, 128])
    hres = nc.dram_tensor("hres", [KT, 128, SEQ + TS], F32, kind="Internal").ap()
    wsc = nc.dram_tensor("wsc", [2, 56, 128, 8192], BF16, kind="Internal").ap()

    es = ExitStack()
    with es:
        def sb(name, shape, dt=F32):
            return es.enter_context(nc.sbuf_tensor(name, list(shape), dt))

        pr = Prog(nc)
        A = pr.add

        ident = sb("ident", [128, 128])
        ones_bf = sb("ones_bf", [128, 128], BF16)
        negmask = sb("negmask", [128, 128], BF16)
        identb = sb("identb", [128, 128], BF16)
        sel2 = sb("sel2", [40, 8, 128], BF16)
        tau = sb("tau", [128, TP])
        taus = sb("taus", [128, TS])
        KTb = sb("KTb", [128, H, SEQ], BF16)
        Vb = sb("Vb", [128, 16, 1024], BF16)
        negc = sb("negc", [128, 16, 8])
        vcol = sb("vcol", [128, 96])
        bfc = sb("bfc", [40, 1])
        wf2 = sb("wf2", [128, KT, 40], BF16)
        wppb = sb("wppb", [128, 2, 2, 512], BF16)
        magT = sb("magT", [128, 32])
        thn = sb("thn", [128, 32])
        cth = sb("cth", [128, 32])
        sth = sb("sth", [128, 32])
        winit = sb("winit", [128, 3, 2, 32])
        Bc = sb("Bc", [128, 8, 2, 128], BF16)
        Cc = sb("Cc", [128, 32, 2, 32], BF16)
        Xst = sb("Xst", [128, 3, 2, 32])
        ccar = sb("ccar", [40, 3])
        sm = sb("sm", [128, 24, 32])
        smi = sb("smi", [128, 32], I32)
        xT = sb("xT", [128, KT, TP])
        aT = sb("aT", [128, KT, TP], BF16)
        wsl = [sb("wsl%d" % i, [128, 8192], BF16) for i in range(2)]
        rstd = sb("rstd", [128, 3, TP])
        sqb = sb("sqb", [128, 2, TP], BF16)
        reg = sb("reg", [128, 8 * TP])
        uT = reg[:, :].rearrange("p (a t) -> p a t", t=TP)
        regb = sb("regb", [128, 22 * TP], BF16)
        ub = regb[:, 0:8 * TP].rearrange("p (a t) -> p a t", t=TP)
        qT = regb[:, 8 * TP:16 * TP].rearrange("p (a t) -> p a t", t=TP)
        Pt = regb[:, 16 * TP:19 * TP].rearrange("p (a t) -> p a t", t=TP)
        xrb = regb[:, 19 * TP:21 * TP].rearrange("p (a t) -> p a t", t=TP)
        actT = regb[:, :].rearrange("p (a t) -> p a t", t=TP)
        tmp = sb("tmp", [128, 11, TP])
        tmpi = sb("tmpi", [128, TP], I32)
        tmpB = sb("tmpB", [128, 2, TP])
        chl = sb("chl", [40, TP], BF16)
        stg = sb("stg", [128, 2, 1024])
        pT = sb("pT", [128, 2, TP], BF16)
        PS = [es.enter_context(nc.psum_tensor("ps%d" % i, [128, 512], F32)) for i in range(8)]

        rot = {"a": 0, "w": 0, "r": 0, "t": 0, "p": 0, "s": 0, "banks": [0, 1, 2, 3], "sbanks": [4, 5]}

        def bankA():
            lst = rot["banks"]
            rot["a"] = (rot["a"] + 1) % len(lst)
            return lst[rot["a"]]

        def nxt(name, n):
            rot[name] = (rot[name] + 1) % n
            return rot[name]

        def mm(b, out, lhsT, rhs, start, stop, r, tp=None, w=None):
            kw = {}
            if tp is not None:
                kw["tile_position"] = tp
            A("pe", lambda e: e.matmul(out, lhsT=lhsT, rhs=rhs, start=start, stop=stop, **kw),
              r=r, w=[("ps", b)] if w is None else w)

        def tr(b, out, in_, idn, r):
            A("pe", lambda e: e.transpose(out=out, in_=in_, identity=idn), r=r, w=[("ps", b)])

        def act(out, in_, func, r, w, bias=None, scale=None):
            kw = {}
            if bias is not None:
                kw["bias"] = bias
            if scale is not None:
                kw["scale"] = scale
            A("act", lambda e: e.activation(out=out, in_=in_, func=func, **kw), r=r, w=w)

        def tt(out, in0, in1, op, r, w, eng="dve"):
            A(eng, lambda e: e.tensor_tensor(out=out, in0=in0, in1=in1, op=op), r=r, w=w)

        def ts(out, in0, s1, s2, op0, op1, r, w, eng="dve"):
            A(eng, lambda e: e.tensor_scalar(out=out, in0=in0, scalar1=s1, scalar2=s2, op0=op0, op1=op1), r=r, w=w)

        def stt(out, in0, scalar, in1, op0, op1, r, w, eng="dve"):
            A(eng, lambda e: e.scalar_tensor_tensor(out=out, in0=in0, scalar=scalar, in1=in1, op0=op0, op1=op1), r=r, w=w)

        def cp(out, in_, r, w, eng="dve"):
            if eng == "act":
                A(eng, lambda e: e.activation(out=out, in_=in_, func=AF.Copy), r=r, w=w)
            else:
                A(eng, lambda e: e.tensor_copy(out=out, in_=in_), r=r, w=w)

        def ms(ap, val, w, eng="dve"):
            A(eng, lambda e: e.memset(ap, val), r=[], w=w)

        def dma(q, out, in_, r, w):
            A(q, lambda e: e.dma_start(out=out, in_=in_), r=r, w=w, dma=True)

        ms(ident[:], 0.0, ["ident"], eng="pool")
        A("pool", lambda e: e.affine_select(out=ident[:], in_=ident[:], compare_op=ALU.not_equal, fill=1.0,
                                            base=0, pattern=[[-1, 128]], channel_multiplier=1),
          r=["ident"], w=["ident"])
        ms(ones_bf[:], 1.0, ["ones_bf"])
        cp(identb[:], ident[:], ["ident"], ["identb"])
        ms(negmask[:], 0.0, ["negmask"], eng="pool")
        A("pool", lambda e: e.affine_select(out=negmask[:], in_=negmask[:], compare_op=ALU.is_ge, fill=-30000.0,
                                            base=0, pattern=[[1, 128]], channel_multiplier=-1),
          r=["negmask"], w=["negmask"])
        A("pool", lambda e: e.iota(tmpi[:], pattern=[[1, TP]], base=0, channel_multiplier=0), r=[], w=["tmpi"])
        cp(tau[:], tmpi[:], ["tmpi"], ["tau"])
        cp(taus[:, 0:16], tmpi[:, 0:16], ["tmpi"], ["tau"])
        cp(taus[:, 16:32], tmpi[:, 0:16], ["tmpi"], ["tau"])
        ms(sel2[:], 0.0, ["sel2"])
        for h in range(H):
            cp(sel2[0:8, h, :], ident[0:8, h:h + 1].to_broadcast([8, 128]), ["ident", "sel2"], ["sel2"])
            cp(sel2[32:40, h, :], ident[32:40, 32 + h:33 + h].to_broadcast([8, 128]), ["ident", "sel2"], ["sel2"])
        ms(Xst[:], 0.0, ["Xst"])
        ms(ccar[:], 0.0, ["ccar"])

        def rstd_from(src, skeys, rs, Tn, nfeat):
            act(rstd[:, rs, :Tn], src, AF.Ln, r=skeys + ["epsc"], w=[("rstd", rs)], bias=epsc[:, 0:1], scale=1.0 / nfeat)
            act(rstd[:, rs, :Tn], rstd[:, rs, :Tn], AF.Exp, r=[("rstd", rs)], w=[("rstd", rs)], scale=-0.5)

        def rms_from_sq(sq_list, Tn, nfeat, rkeys):
            sbl = rot["sbanks"]
            rot["r"] = (rot["r"] + 1) % len(sbl)
            b = sbl[rot["r"]]
            n = len(sq_list)
            for i, (ap, keys) in enumerate(sq_list):
                s = nxt("s", 2)
                act(sqb[:, s, :Tn], ap, AF.Square, r=keys, w=[("sqb", s)])
                mm(b, PS[b][:, :Tn], ones_bf[:], sqb[:, s, :Tn], i == 0, i == n - 1, r=["ones_bf", ("sqb", s)])
            rs = nxt("t", 3)
            rstd_from(PS[b][:, :Tn], [("ps", b)], rs, Tn, nfeat)
            return rs

        epsc = sb("epsc", [128, 1])
        ms(epsc[:], EPS, ["epsc"])

        def norm_x(l, Tn, gofs):
            rs = rms_from_sq([(xT[:, kt, :Tn], [("xT", kt)]) for kt in range(KT)], Tn, D, None)
            for kt in range(KT):
                stt(aT[:, kt, :Tn], xT[:, kt, :Tn], vcol[:, gofs + kt:gofs + kt + 1], rstd[:, rs, :Tn], ALU.mult, ALU.mult,
                    r=[("xT", kt), "vcol", ("rstd", rs)], w=[("aT", kt)])

        bidc = {}
        cur = {"l": 0, "first": True}

        def wblock(name, parts):
            s = nxt("w", 2)
            l = cur["l"]
            bid = bidc.setdefault(name, len(bidc))
            total = max(c0 + nk * ncol for (c0, nk, ncol, _) in parts)
            if cur["first"] or os.environ.get("NOWSC"):
                for pi_, (c0, nk, ncol, src3) in enumerate(parts):
                    view = wsl[s][:, c0:c0 + nk * ncol].rearrange("p (k n) -> p k n", n=ncol)
                    dma("pool", view, src3, r=[], w=[("w", s, pi_)])
                if not os.environ.get("NOWSC"):
                    dma("sp", wsc[l, bid, :, 0:total], wsl[s][:, 0:total], r=WK(s), w=[("wsc", l, bid)])
            else:
                half = total // 2
                dma("pool", wsl[s][:, 0:half], wsc[l, bid, :, 0:half], r=[("wsc", l, bid)], w=[("w", s, 0)])
                dma("pool", wsl[s][:, half:total], wsc[l, bid, :, half:total], r=[("wsc", l, bid)], w=[("w", s, 1)])
            return s

        def wload(name, src3, nk, ncol):
            step = nk // 4
            parts = [(k0 * ncol, step, ncol, src3[:, k0:k0 + step, :]) for k0 in range(0, nk, step)]
            s = wblock(name, parts)
            return s, wsl[s][:, 0:nk * ncol].rearrange("p (k n) -> p k n", n=ncol)

        def WK(s):
            return [("w", s, i) for i in range(4)]

        def wview(w2d, c0, ncol):
            return w2d.rearrange("(k p) n -> p k n", p=128)[:, :, c0:c0 + ncol]

        def layer_setup(l):
            rows = [(g_mix, 0, 16), (g_ffn, 16, 16), (g_ple, 32, 16), (g_ao, 48, 8), (g_so, 56, 8),
                    (d_skip, 64, 8), (b_glu, 72, 8), (g_q, 80, 1), (g_k, 81, 1)]
            st = stg[:, 0, 0:128]
            for (t, r0, n) in rows:
                dma("sp", stg[r0:r0 + n, 0, 0:128], t[l].rearrange("(r c) -> r c", c=128), r=[], w=["stg"])
            b = bankA()
            tr(b, PS[b][:, 0:82], st[0:82, :], ident[0:82, 0:82], r=["stg", "ident"])
            cp(vcol[:, 0:82], PS[b][:, 0:82], [("ps", b)], ["vcol"])
            ts(vcol[:, 82:83], vcol[:, 80:81], 128.0 ** -0.5, 0.0, ALU.mult, ALU.add, ["vcol"], ["vcol"])
            ms(bfc[:], 0.0, ["bfc"])
            dma("sp", bfc[0:8, :], b_f[l].rearrange("(r c) -> r c", c=1), r=[], w=["bfc"])
            dma("sp", bfc[32:40, :], b_f[l].rearrange("(r c) -> r c", c=1), r=[], w=["bfc"])
            ts(bfc[:], bfc[:], -1.0, 0.0, ALU.mult, ALU.add, ["bfc"], ["bfc"])
            ms(wf2[:], 0.0, ["wf2"])
            wfsrc = wview(w_in[l], 4096, 8)
            dma("pool", wf2[:, :, 0:8], wfsrc, r=[], w=["wf2"])
            dma("pool", wf2[:, :, 32:40], wfsrc, r=[], w=["wf2"])
            sA = stg[:, 1, 0:128]
            dma("sp", stg[0:32, 1, 0:128], a_re[l].rearrange("(gp gl) p -> gp (gl p)", gl=2), r=[], w=["stgA"])
            dma("sp", stg[32:64, 1, 0:128], a_im[l].rearrange("(gp gl) p -> gp (gl p)", gl=2), r=[], w=["stgA"])
            dma("sp", stg[64:96, 1, 128:130], log_dt[l].rearrange("(gp gl) -> gp gl", gl=2), r=[], w=["stgA"])
            for gl in range(2):
                cp(stg[64:96, 1, gl * 64:(gl + 1) * 64], stg[64:96, 1, 128 + gl:129 + gl].to_broadcast([32, 64]),
                   ["stgA"], ["stgA"])
            b = bankA()
            tr(b, PS[b][:, 0:96], sA[0:96, :], ident[0:96, 0:96], r=["stgA", "ident"])
            S = lambda i: sm[:, i, :]
            K = ["sm"]
            cp(sm[:, 0:3, :], PS[b][:, 0:96].rearrange("p (a g) -> p a g", g=32), [("ps", b)], K)
            act(S(3), S(2), AF.Exp, K, K)
            tt(S(4), S(0), S(3), ALU.mult, K, K)
            act(magT[:], S(4), AF.Exp, K, ["magT"])
            tt(S(5), S(1), S(3), ALU.mult, K, K)
            ts(thn[:], S(5), 1.0 / TWO_PI, 0.0, ALU.mult, ALU.add, K, ["thn"])
            cp(smi[:], thn[:], ["thn"], ["smi"])
            tt(S(6), thn[:], smi[:], ALU.subtract, ["thn", "smi"], K)
            act(S(7), S(6), AF.Sin, K, K, scale=TWO_PI)
            act(S(8), S(6), AF.Sin, K, K, scale=math.pi)
            tt(S(9), S(8), S(8), ALU.mult, K, K)
            ts(S(9), S(9), -2.0, 1.0, ALU.mult, ALU.add, K, K)
            cp(cth[:], S(9), K, ["cth"])
            cp(sth[:], S(7), K, ["sth"])
            tt(S(10), magT[:], S(9), ALU.mult, K + ["magT"], K)
            tt(S(11), magT[:], S(7), ALU.mult, K + ["magT"], K)
            ts(S(10), S(10), -1.0, 0.0, ALU.add, ALU.add, K, K)
            tt(S(12), S(0), S(0), ALU.mult, K, K)
            tt(S(13), S(1), S(1), ALU.mult, K, K)
            tt(S(12), S(12), S(13), ALU.add, K, K)
            A("dve", lambda e: e.reciprocal(out=S(12), in_=S(12)), r=K, w=K)
            tt(S(13), S(10), S(0), ALU.mult, K, K)
            tt(S(14), S(11), S(1), ALU.mult, K, K)
            tt(S(13), S(13), S(14), ALU.add, K, K)
            tt(S(13), S(13), S(12), ALU.mult, K, K)
            tt(S(14), S(11), S(0), ALU.mult, K, K)
            tt(S(15), S(10), S(1), ALU.mult, K, K)
            tt(S(14), S(14), S(15), ALU.subtract, K, K)
            tt(S(14), S(14), S(12), ALU.mult, K, K)
            tmpf = tmp[:, :, :].rearrange("p a t -> p (a t)")
            v3 = lambda i: tmpf[:, i * 512:(i + 1) * 512].rearrange("p (g c) -> p g c", c=16)
            br, bi, t2, t3, t4 = v3(0), v3(1), v3(2), v3(3), v3(4)
            TA = [("tmp", x) for x in range(10)]
            for gl in range(2):
                dma("sp", br[gl * 64:(gl + 1) * 64], b_re[l].rearrange("(gp gl) p c -> gl p gp c", gl=2)[gl], r=[], w=TA)
                dma("sp", bi[gl * 64:(gl + 1) * 64], b_im[l].rearrange("(gp gl) p c -> gl p gp c", gl=2)[gl], r=[], w=TA)
            crb = sm[:, 13:14, :].rearrange("p a g -> p g a").to_broadcast([128, 32, 16])
            cib = sm[:, 14:15, :].rearrange("p a g -> p g a").to_broadcast([128, 32, 16])
            KK = TA + ["sm"]
            tt(t2, br, crb, ALU.mult, KK, TA)
            tt(t3, bi, cib, ALU.mult, KK, TA)
            tt(t2, t2, t3, ALU.subtract, TA, TA)
            tt(t3, bi, crb, ALU.mult, KK, TA)
            tt(t4, br, cib, ALU.mult, KK, TA)
            tt(t3, t3, t4, ALU.add, TA, TA)
            RG = ["regS"] + [("uT", x) for x in range(8)]
            srcB = reg[:, 0:2048].rearrange("p (q r x) -> p q r x", r=2, x=128)
            ms(reg[:, 0:2048], 0.0, RG)
            for ri, tsrc in enumerate((t2, t3)):
                for gl in range(2):
                    dst = srcB[gl * 64:(gl + 1) * 64, :, ri, :].rearrange("p q (j x) -> p q j x", x=32)[:, :, :, gl * 16:gl * 16 + 16]
                    cp(dst, tsrc[gl * 64:(gl + 1) * 64].rearrange("p (q j) c -> p q j c", j=4), TA + RG, RG)
            for q in range(8):
                b = bankA()
                for ri in range(2):
                    tr(b, PS[b][:, ri * 128:(ri + 1) * 128], srcB[:, q, ri, :], ident[:], r=RG + ["ident"])
                cp(Bc[:, q, :, :], PS[b][:, 0:256].rearrange("p (r x) -> p r x", x=128), [("ps", b)], ["Bc"])
            stC = reg[0:32, 0:2048].rearrange("p (g x) -> p g x", x=128)
            for ri, csrc in enumerate((c_re, c_im)):
                for g0 in range(0, 32, 16):
                    ms(reg[0:32, 0:2048], 0.0, RG)
                    for gl in range(2):
                        dma("sp", stC[gl * 16:(gl + 1) * 16, :, gl * 64:(gl + 1) * 64],
                            csrc[l].rearrange("(gp gl) c p -> gl c gp p", gl=2)[gl][:, g0:g0 + 16, :], r=[], w=RG)
                    b = bankA()
                    for gg in range(16):
                        tr(b, PS[b][:, gg * 32:(gg + 1) * 32], stC[:, gg, :], ident[0:32, 0:32], r=RG + ["ident"])
                    src = PS[b][:, :].rearrange("p (g x) -> p g x", x=32)
                    if ri == 0:
                        cp(Cc[:, g0:g0 + 16, 0, :], src, [("ps", b)], ["Cc"])
                    else:
                        ts(Cc[:, g0:g0 + 16, 1, :], src, -1.0, 0.0, ALU.mult, ALU.add, [("ps", b)], ["Cc"])
            if SAMPLE:
                for si in range(2):
                    dma("sp", stg[si * 64:si * 64 + 32, 1, 0:128], sre[l, si].rearrange("(gp gl) p -> gp (gl p)", gl=2), r=[], w=["stgA"])
                    dma("sp", stg[si * 64 + 32:si * 64 + 64, 1, 0:128], sim[l, si].rearrange("(gp gl) p -> gp (gl p)", gl=2), r=[], w=["stgA"])
                b = bankA()
                tr(b, PS[b][:, 0:128], sA[:, :], ident[:], r=["stgA", "ident"])
                cp(Xst[:, 1:3, :, :], PS[b][:, 0:128].rearrange("p (s r g) -> p s r g", r=2, g=32), [("ps", b)], ["Xst"])
            ms(Xst[:, 0, :, :], 0.0, ["Xst"])
            ms(ccar[:], 0.0, ["ccar"])

        def do_tile(l, kind, ti):
            prompt = kind == "p"
            Tn = TP if prompt else TS
            nb = NBT if prompt else 1
            bw = 128 if prompt else TS
            tok0 = ti * TP if prompt else SEQ
            segs = [(0, TP, 0)] if prompt else [(0, 16, 1), (16, 16, 2)]
            T = lambda i: tmp[:, i, :Tn]
            TK = lambda *i: [("tmp", x) for x in i]
            TB = lambda i: tmpB[:, i, :Tn]
            TBK = lambda i: [("tmpB", i)]
            st = {}
            cur["l"] = l
            cur["first"] = prompt and ti == 0

            if l == 0:
                src = xp if prompt else xs
                for tb in range(nb):
                    r0 = ti * TP + tb * 128 if prompt else 0
                    for hf in range(2):
                        dma("sp", stg[0:bw, hf, :], src[r0:r0 + bw, hf * 1024:(hf + 1) * 1024], r=[], w=[("stg", hf)])
                    for g in range(4):
                        b = bankA()
                        for j in range(4):
                            kt = 4 * g + j
                            tr(b, PS[b][:, j * bw:(j + 1) * bw], stg[0:bw, kt // 8, (kt % 8) * 128:(kt % 8 + 1) * 128],
                               ident[0:bw, 0:bw], r=[("stg", kt // 8), "ident"])
                        cp(xT[:, 4 * g:4 * g + 4, tb * 128:tb * 128 + bw], PS[b][:, 0:4 * bw].rearrange("p (j t) -> p j t", t=bw),
                           [("ps", b)], [("xT", 4 * g + j) for j in range(4)], eng="act" if g % 2 else "dve")
            else:
                for kt in range(KT):
                    dma("sp", xT[:, kt, :Tn], hres[kt, :, tok0:tok0 + Tn], r=["hres"], w=[("xT", kt)])
            psrc = pp if prompt else psm
            for tb in range(nb):
                r0 = ti * TP + tb * 128 if prompt else 0
                dma("sp", stg[0:bw, 1, 0:256], psrc[l, r0:r0 + bw, :], r=[], w=[("stg", 1)])
                b = bankA()
                for j in range(2):
                    tr(b, PS[b][:, j * bw:(j + 1) * bw], stg[0:bw, 1, j * 128:(j + 1) * 128], ident[0:bw, 0:bw], r=[("stg", 1), "ident"])
                cp(pT[:, :, tb * 128:tb * 128 + bw], PS[b][:, 0:2 * bw].rearrange("p (j t) -> p j t", t=bw), [("ps", b)], ["pT"])

            norm_x(l, Tn, 0)

            if UPTO < 1:
                return
            b = bankA()
            for kt in range(KT):
                mm(b, PS[b][0:40, :Tn], wf2[:, kt, :], aT[:, kt, :Tn], kt == 0, kt == KT - 1, r=["wf2", ("aT", kt)])
            cfa = lambda i: tmp[0:40, i, :Tn]
            act(cfa(0), PS[b][0:40, :Tn], AF.Exp, [("ps", b), "bfc"], TK(0), bias=bfc[:, 0:1], scale=-1.0)
            act(cfa(0), cfa(0), AF.Ln, TK(0), TK(0), bias=onec[0:40, 0:1], scale=1.0)
            ts(cfa(1), cfa(0), -1.0, 0.0, ALU.mult, ALU.add, TK(0), TK(1))
            for (c0, n, slot) in segs:
                A("dve", lambda e, c0=c0, n=n, slot=slot: e.tensor_tensor_scan(
                    out=tmp[0:40, 2, c0:c0 + n], data0=onec[0:40, 0:1].to_broadcast([40, n]), data1=tmp[0:40, 1, c0:c0 + n],
                    initial=ccar[:, slot:slot + 1], op0=ALU.mult, op1=ALU.add), r=TK(1) + ["ccar"], w=TK(2))
                cp(ccar[:, slot:slot + 1], tmp[0:40, 2, c0 + n - 1:c0 + n], TK(2), ["ccar"])
            cp(chl[:, :Tn], cfa(2), TK(2), ["chl"])
            tt(cfa(3), cfa(2), chl[:, :Tn], ALU.subtract, TK(2) + ["chl"], TK(3))
            cp(chl[32:40, :Tn], tmp[32:40, 3, :Tn], TK(3), ["chl"])
            b = bankA()
            if prompt:
                for tb in range(nb):
                    tr(b, PS[b][:, tb * 16:tb * 16 + 8], tmp[0:8, 1, tb * 128:(tb + 1) * 128], ident[0:8, 0:8], r=TK(1) + ["ident"])
                    tr(b, PS[b][:, tb * 16 + 8:tb * 16 + 16], tmp[0:8, 2, tb * 128:(tb + 1) * 128], ident[0:8, 0:8], r=TK(2) + ["ident"])
                pv = PS[b][:, 0:nb * 16].rearrange("p (t x) -> p t x", x=16)
                lft = tmp[:, 9, 0:nb * 8].rearrange("p (t x) -> p t x", x=8)
                cp(lft, pv[:, :, 0:8], [("ps", b)], TK(9))
                ts(negc[:, ti * NBT:(ti + 1) * NBT, :], pv[:, :, 8:16], -1.0, 0.0, ALU.mult, ALU.add, [("ps", b)], [("negc", ti)])
                dma("sp", lfp[l, ti * TP:(ti + 1) * TP, :].rearrange("(t p) x -> p t x", p=128), lft, r=TK(9), w=[])
            else:
                for si in range(2):
                    tr(b, PS[b][0:16, si * 16:si * 16 + 8], tmp[0:8, 1, si * 16:(si + 1) * 16], ident[0:8, 0:8], r=TK(1) + ["ident"])
                    tr(b, PS[b][0:16, si * 16 + 8:si * 16 + 16], tmp[0:8, 2, si * 16:(si + 1) * 16], ident[0:8, 0:8], r=TK(2) + ["ident"])
                pv = PS[b][0:16, 0:32].rearrange("p (t x) -> p t x", x=16)
                lft = tmp[0:16, 9, 0:16].rearrange("p (t x) -> p t x", x=8)
                cp(lft, pv[:, :, 0:8], [("ps", b)], TK(9))
                ts(negcs[:, :, :], pv[:, :, 8:16], -1.0, 0.0, ALU.mult, ALU.add, [("ps", b)], ["negcs"])
                dma("sp", lfs[l].rearrange("(s p) x -> p s x", p=16), lft, r=TK(9), w=[])

            if UPTO < 2:
                return
            for ub_i in range(2):
                s, wv = wload(("u", ub_i), wview(w_in[l], 3072 + ub_i * 512, 512), KT, 512)
                for j in range(0 if os.environ.get("SKIPMM") else 4):
                    b = bankA()
                    for kt in range(KT):
                        mm(b, PS[b][:, :Tn], wv[:, kt, j * 128:(j + 1) * 128], aT[:, kt, :Tn], kt == 0, kt == KT - 1, r=WK(s) + [("aT", kt)])
                    c = ub_i * 4 + j
                    if os.environ.get("SKIPCP"):
                        continue
                    cp(uT[:, c, :Tn], PS[b][:, :Tn], [("ps", b)], [("uT", c)], eng="act")
                    cp(ub[:, c, :Tn], uT[:, c, :Tn], [("uT", c)], [("ub", c)])

            def part_qkv():
                for which in range(2):
                    for hb in range(2):
                        s, wv = wload(("qk", which, hb), wview(w_in[l], which * 1024 + hb * 512, 512), KT, 512)
                        for j in range(4):
                            h = hb * 4 + j
                            b = bankA()
                            for kt in range(KT):
                                mm(b, PS[b][:, :Tn], wv[:, kt, j * 128:(j + 1) * 128], aT[:, kt, :Tn], kt == 0, kt == KT - 1, r=WK(s) + [("aT", kt)])
                            rs = rms_from_sq([(PS[b][:, :Tn], [("ps", b)])], Tn, 128, None)
                            if which == 0:
                                stt(qT[:, h, :Tn], PS[b][:, :Tn], vcol[:, 82:83], rstd[:, rs, :Tn], ALU.mult, ALU.mult,
                                    r=[("ps", b), "vcol", ("rstd", rs)], w=[("qT", h)])
                            else:
                                ks = nxt("p", 2)
                                kk = TBK(ks)
                                stt(TB(ks), PS[b][:, :Tn], vcol[:, 81:82], rstd[:, rs, :Tn], ALU.mult, ALU.mult,
                                    r=[("ps", b), "vcol", ("rstd", rs)], w=kk)
                                if prompt:
                                    cp(KTb[:, h, ti * TP:(ti + 1) * TP], TB(ks), kk, [("KT", h, ti)], eng="act")
                                else:
                                    cp(kTs[:, h, :], TB(ks), kk, [("kTs", h)], eng="act")
                                b2 = bankA()
                                for tb in range(nb):
                                    tr(b2, PS[b2][0:bw, tb * 128:(tb + 1) * 128], tmpB[:, ks, tb * 128:tb * 128 + bw], ident[:], r=kk + ["ident"])
                                kslot = h % 2
                                cp(kst[0:bw, kslot, 0:nb, :], PS[b2][0:bw, 0:nb * 128].rearrange("p (t x) -> p t x", x=128),
                                   [("ps", b2)], [("kst", kslot)], eng="act")
                                if prompt:
                                    dma("sp", kp[l, ti * TP:(ti + 1) * TP, h * 128:(h + 1) * 128].rearrange("(t p) x -> p t x", p=128),
                                        kst[:, kslot, :, :], r=[("kst", kslot)], w=[])
                                else:
                                    dma("sp", kso[l, :, h * 128:(h + 1) * 128], kst[0:TS, kslot, 0, :], r=[("kst", kslot)], w=[])
                for vb_i in range(2):
                    s, wv = wload(("v", vb_i), wview(w_in[l], 2048 + vb_i * 512, 512), KT, 512)
                    vblocks = [(tb * 128, 128, tb) for tb in range(nb)] if prompt else [(0, 16, 0), (16, 16, 1)]
                    for (c0, n, tb) in vblocks:
                        b = bankA()
                        for kt in range(KT):
                            mm(b, PS[b][0:n, :], aT[:, kt, c0:c0 + n], wv[:, kt, :], kt == 0, kt == KT - 1, r=WK(s) + [("aT", kt)])
                        vs_ = nxt("p", 2)
                        cp(vst[0:n, vs_, :], PS[b][0:n, :], [("ps", b)], [("vst", vs_)], eng="act")
                        if prompt:
                            cp(Vb[:, ti * NBT + tb, vb_i * 512:(vb_i + 1) * 512], vst[:, vs_, :], [("vst", vs_)], [("V", ti)])
                            r0 = ti * TP + c0
                            dma("sp", vp[l, r0:r0 + 128, vb_i * 512:(vb_i + 1) * 512], vst[:, vs_, :], r=[("vst", vs_)], w=[])
                        else:
                            cp(vbs[:, tb, vb_i * 512:(vb_i + 1) * 512], vst[0:16, vs_, :], [("vst", vs_)], ["vbs"])
                            dma("sp", vso[l, c0:c0 + 16, vb_i * 512:(vb_i + 1) * 512], vst[0:16, vs_, :], r=[("vst", vs_)], w=[])

            def part_s5():
                tsrc = tau if prompt else taus
                tabs = [(1, 2), (9, 10)]

                def tables(gp):
                    sn_i, cs_i = tabs[gp % 2]
                    act(T(0), tsrc[:, :Tn], AF.Identity, ["tau", "thn"], TK(0), scale=thn[:, gp:gp + 1])
                    cp(tmpi[:, :Tn], T(0), TK(0), ["tmpi"])
                    tt(T(0), T(0), tmpi[:, :Tn], ALU.subtract, TK(0) + ["tmpi"], TK(0))
                    act(T(sn_i), T(0), AF.Sin, TK(0), TK(sn_i), scale=TWO_PI)
                    act(T(cs_i), T(0), AF.Sin, TK(0), TK(cs_i), scale=math.pi)
                    act(T(cs_i), T(cs_i), AF.Square, TK(cs_i), TK(cs_i))
                    act(T(cs_i), T(cs_i), AF.Identity, TK(cs_i) + ["onec"], TK(cs_i), bias=onec[:, 0:1], scale=-2.0)

                def bu(gp):
                    quad, j = gp // 4, gp % 4
                    bk = rot["banks"][gp % 2]
                    mm(bk, PS[bk][:, 0:Tn], Bc[32 * j:32 * j + 32, quad, 0, :], ub[32 * j:32 * j + 32, quad, :Tn], True, True,
                       r=["Bc", ("ub", quad)], tp=(32 * j, 0))
                    mm(bk, PS[bk][:, TP:TP + Tn], Bc[32 * j:32 * j + 32, quad, 1, :], ub[32 * j:32 * j + 32, quad, :Tn], True, True,
                       r=["Bc", ("ub", quad)], tp=(32 * j, 0))

                def body(gp):
                    quad, j = gp // 4, gp % 4
                    sn_i, cs_i = tabs[gp % 2]
                    bk = rot["banks"][gp % 2]
                    bre = PS[bk][:, 0:Tn]
                    bim = PS[bk][:, TP:TP + Tn]
                    BK = [("ps", bk)]
                    SK = ["smi16"]
                    inis = [(c0, n, slot, winit[:, slot, 0, gp:gp + 1], winit[:, slot, 1, gp:gp + 1]) for (c0, n, slot) in segs]
                    tt(T(3), T(cs_i), bre, ALU.mult, TK(cs_i) + BK, TK(3))
                    tt(T(5), T(sn_i), bim, ALU.mult, TK(sn_i) + BK, TK(5))
                    tt(T(4), T(cs_i), bim, ALU.mult, TK(cs_i) + BK, TK(4))
                    tt(T(8), T(sn_i), bre, ALU.mult, TK(sn_i) + BK, TK(8))
                    tt(T(3), T(3), T(5), ALU.add, TK(3, 5), TK(3))
                    tt(T(4), T(4), T(8), ALU.subtract, TK(4, 8), TK(4))
                    for (c0, n, slot, i0, i1) in inis:
                        for (dst, srcw, ini) in ((6, 3, i0), (7, 4, i1)):
                            A("dve", lambda e, dst=dst, srcw=srcw, ini=ini, c0=c0, n=n, gp=gp: e.tensor_tensor_scan(
                                out=tmp[:, dst, c0:c0 + n], data0=magT[:, gp:gp + 1].to_broadcast([128, n]),
                                data1=tmp[:, srcw, c0:c0 + n], initial=ini, op0=ALU.mult, op1=ALU.add),
                              r=TK(srcw) + ["winit", "magT"], w=TK(dst))
                    tt(T(8), T(cs_i), T(6), ALU.mult, TK(cs_i, 6), TK(8))
                    tt(T(5), T(sn_i), T(7), ALU.mult, TK(sn_i, 7), TK(5))
                    tt(T(3), T(cs_i), T(7), ALU.mult, TK(cs_i, 7), TK(3))
                    tt(T(4), T(sn_i), T(6), ALU.mult, TK(sn_i, 6), TK(4))
                    tt(T(8), T(8), T(5), ALU.subtract, TK(8, 5), TK(8))
                    tt(T(3), T(3), T(4), ALU.add, TK(3, 4), TK(3))
                    cp(xrb[:, 0, :Tn], T(8), TK(8), ["xrb0"], eng="act")
                    cp(xrb[:, 1, :Tn], T(3), TK(3), ["xrb1"], eng="act")
                    for (c0, n, slot, i0, i1) in inis:
                        cp(Xst[:, slot, 0, gp:gp + 1], tmp[:, 8, c0 + n - 1:c0 + n], TK(8), ["Xst"])
                        cp(Xst[:, slot, 1, gp:gp + 1], tmp[:, 3, c0 + n - 1:c0 + n], TK(3), ["Xst"])
                    by = 5
                    mm(by, PS[by][32 * j:32 * j + 32, :Tn], Cc[:, gp, 0, :], xrb[:, 0, :Tn], True, False, r=["Cc", "xrb0"], tp=(0, 32 * j))
                    mm(by, PS[by][32 * j:32 * j + 32, :Tn], Cc[:, gp, 1, :], xrb[:, 1, :Tn], False, True, r=["Cc", "xrb1"], tp=(0, 32 * j))
                    if j == 3:
                        stt(uT[:, quad, :Tn], uT[:, quad, :Tn], vcol[:, 64 + quad:65 + quad], PS[by][:, :Tn], ALU.mult, ALU.add,
                            r=[("uT", quad), "vcol", ("ps", by)], w=[("uT", quad)])
                        act(uT[:, quad, :Tn], uT[:, quad, :Tn], AF.Gelu_apprx_tanh, [("uT", quad)], [("uT", quad)])
                        cp(ub[:, quad, :Tn], uT[:, quad, :Tn], [("uT", quad)], [("ub", quad)])

                for (c0, n, slot) in segs:
                    xr_ = Xst[:, slot, 0, :]
                    xi_ = Xst[:, slot, 1, :]
                    wr0 = winit[:, slot, 0, :]
                    wi0 = winit[:, slot, 1, :]
                    WI = ["winit"]
                    tt(wr0, cth[:], xr_, ALU.mult, ["cth", "Xst"], WI)
                    tt(sm[:, 18, :], sth[:], xi_, ALU.mult, ["sth", "Xst"], ["smi18"])
                    tt(wi0, cth[:], xi_, ALU.mult, ["cth", "Xst"], WI)
                    tt(sm[:, 19, :], sth[:], xr_, ALU.mult, ["sth", "Xst"], ["smi19"])
                    tt(wr0, wr0, sm[:, 18, :], ALU.subtract, WI + ["smi18"], WI)
                    tt(wi0, wi0, sm[:, 19, :], ALU.add, WI + ["smi19"], WI)
                tables(0)
                bu(0)
                for gp in range(32):
                    if gp + 1 < 32:
                        tables(gp + 1)
                        bu(gp + 1)
                    body(gp)
            def part_glu():
                for nb2 in range(2):
                    s, wv = wload(("glu", nb2), wview(w_glu[l], nb2 * 512, 512), 8, 512)
                    for j in range(4):
                        n = nb2 * 4 + j
                        b = bankA()
                        for kq in range(8):
                            mm(b, PS[b][:, :Tn], wv[:, kq, j * 128:(j + 1) * 128], ub[:, kq, :Tn], kq == 0, kq == 7, r=WK(s) + [("ub", kq)])
                        act(T(0), PS[b][:, :Tn], AF.Sigmoid, [("ps", b), "vcol"], TK(0), bias=vcol[:, 72 + n:73 + n], scale=1.0)
                        tt(T(9), uT[:, n, :Tn], T(0), ALU.mult, [("uT", n)] + TK(0), TK(9))
                        s2 = nxt("s", 2)
                        act(sqb[:, s2, :Tn], T(9), AF.Square, TK(9), [("sqb", s2)])
                        mm(5, PS[5][:, :Tn], ones_bf[:], sqb[:, s2, :Tn], n == 0, n == 7, r=["ones_bf", ("sqb", s2)])
                        ts(mT(8 + n, Tn), T(9), vcol[:, 56 + n:57 + n], 0.0, ALU.mult, ALU.add, TK(9) + ["vcol"], [("aT", 8 + n)])
                rs_s = st.setdefault('x', {}).__setitem__('s', nxt("t", 3)) or st['x']['s']
                rstd_from(PS[5][:, :Tn], [("ps", 5)], rs_s, Tn, 1024)

            def part_attn():
                if prompt:
                    nkb = NBT * ti + NBT
                    for h in range(H):
                        bo = 6
                        br_ = 7
                        for kb in range(nkb):
                            diag = kb >= NBT * ti
                            qs = (kb - NBT * ti) * 128 if diag else 0
                            bs = bankA()
                            ktile = kb // NBT
                            mm(bs, PS[bs][:, qs:TP], KTb[:, h, kb * 128:(kb + 1) * 128], qT[:, h, qs:TP], True, False, r=[("KT", h, ktile), ("qT", h)])
                            mm(bs, PS[bs][:, qs:TP], sel2[:, h, :], chl[:, qs:TP], False, not diag, r=["sel2", "chl"])
                            if diag:
                                mm(bs, PS[bs][:, qs:qs + 128], identb[:], negmask[:], False, True, r=["identb", "negmask"])
                            pi = nxt("p", 3)
                            act(Pt[:, pi, qs:TP], PS[bs][:, qs:TP], AF.Exp, [("ps", bs), ("negc", ktile)], [("Pt", pi)], bias=negc[:, kb, h:h + 1], scale=1.0)
                            mm(bo, PS[bo][:, qs:TP], Vb[:, kb, h * 128:(h + 1) * 128], Pt[:, pi, qs:TP], kb == 0, kb == nkb - 1, r=[("V", ktile), ("Pt", pi)])
                            mm(br_, PS[br_][:, qs:TP], ones_bf[:], Pt[:, pi, qs:TP], kb == 0, kb == nkb - 1, r=["ones_bf", ("Pt", pi)])
                        attn_epilogue(l, h, Tn, bo, br_)
                else:
                    sample_attention(l)
                rs_a = st.setdefault('x', {}).__setitem__('a', nxt("t", 3)) or st['x']['a']
                rstd_from(PS[4][:, :Tn], [("ps", 4)], rs_a, Tn, 1024)

            if prompt and not os.environ.get("NOFORK"):
                main_l = pr.cur
                lb = []
                pr.cur = lb
                rot["banks"] = [2, 3]
                rot["sbanks"] = [4]
                part_qkv()
                part_attn()
                la = []
                pr.cur = la
                rot["banks"] = [0, 1]
                part_s5()
                rot["banks"] = [0, 1, 2, 3]
                rot["sbanks"] = [4, 5]
                pr.cur = main_l
                na, nb_ = len(la), len(lb)
                ia = ib = 0
                while ia < na or ib < nb_:
                    if ib >= nb_ or (ia < na and ia * nb_ <= ib * na):
                        main_l.append(la[ia]); ia += 1
                    else:
                        main_l.append(lb[ib]); ib += 1
                part_glu()
            else:
                part_qkv()
                part_s5()
                part_glu()
                part_attn()
            rs_a = st['x']['a']
            rs_s = st['x']['s']
            for nb4 in range(4):
                s, wv = wload(("wo", nb4), wview(w_out[l], nb4 * 512, 512), KT, 512)
                for j in range(4):
                    n = nb4 * 4 + j
                    b1 = bankA()
                    for kt in range(8):
                        mm(b1, PS[b1][:, :Tn], wv[:, kt, j * 128:(j + 1) * 128], mT(kt, Tn), kt == 0, kt == 7, r=WK(s) + [("aT", kt)])
                    b2 = bankA()
                    for kt in range(8, 16):
                        mm(b2, PS[b2][:, :Tn], wv[:, kt, j * 128:(j + 1) * 128], mT(kt, Tn), kt == 8, kt == 15, r=WK(s) + [("aT", kt)])
                    tt(T(0), PS[b1][:, :Tn], rstd[:, rs_a, :Tn], ALU.mult, [("ps", b1), ("rstd", rs_a)], TK(0))
                    tt(xT[:, n, :Tn], xT[:, n, :Tn], T(0), ALU.add, [("xT", n)] + TK(0), [("xT", n)])
                    tt(T(1), PS[b2][:, :Tn], rstd[:, rs_s, :Tn], ALU.mult, [("ps", b2), ("rstd", rs_s)], TK(1))
                    tt(xT[:, n, :Tn], xT[:, n, :Tn], T(1), ALU.add, [("xT", n)] + TK(1), [("xT", n)])

            if UPTO < 9:
                return
            norm_x(l, Tn, 16)
            akey = lambda jj: ("ub", jj) if jj < 8 else ("qT", jj - 8) if jj < 16 else ("Pt", jj - 16) if jj < 19 else "xrb0" if jj == 19 else "xrb1" if jj == 20 else "act21"
            for half in range(2):
                for blk in range(11):
                    c0 = (half * 11 + blk) * 256
                    gsrc = wview(w_gate[l], c0, 256)
                    usrc = wview(w_up[l], c0, 256)
                    s = wblock(("gu", half, blk), [(0, 8, 256, gsrc[:, 0:8, :]), (2048, 8, 256, gsrc[:, 8:16, :]),
                                                   (4096, 8, 256, usrc[:, 0:8, :]), (6144, 8, 256, usrc[:, 8:16, :])])
                    gv = wsl[s][:, 0:4096].rearrange("p (k n) -> p k n", n=256)
                    uv = wsl[s][:, 4096:8192].rearrange("p (k n) -> p k n", n=256)
                    for j in range(2):
                        jj = blk * 2 + j
                        bg = bankA()
                        for kt in range(KT):
                            mm(bg, PS[bg][:, :Tn], gv[:, kt, j * 128:(j + 1) * 128], aT[:, kt, :Tn], kt == 0, kt == KT - 1, r=WK(s) + [("aT", kt)])
                        bu_ = bankA()
                        for kt in range(KT):
                            mm(bu_, PS[bu_][:, :Tn], uv[:, kt, j * 128:(j + 1) * 128], aT[:, kt, :Tn], kt == 0, kt == KT - 1, r=WK(s) + [("aT", kt)])
                        t_ = nxt("p", 2)
                        act(T(t_), PS[bg][:, :Tn], AF.Silu, [("ps", bg)], TK(t_))
                        tt(actT[:, jj, :Tn], T(t_), PS[bu_][:, :Tn], ALU.mult, TK(t_) + [("ps", bu_)], [akey(jj)])
                for np_ in range(8):
                    src = w_down[l].rearrange("(k p) n -> p k n", p=128)[:, half * 22:(half + 1) * 22, np_ * 256:(np_ + 1) * 256]
                    s = wblock(("dn", half, np_), [(0, 11, 256, src[:, 0:11, :]), (11 * 256, 11, 256, src[:, 11:22, :])])
                    dv = wsl[s][:, 0:22 * 256].rearrange("p (k n) -> p k n", n=256)
                    for j in range(2):
                        n = np_ * 2 + j
                        b = bankA()
                        for jj in range(22):
                            mm(b, PS[b][:, :Tn], dv[:, jj, j * 128:(j + 1) * 128], actT[:, jj, :Tn], jj == 0, jj == 21, r=WK(s) + [akey(jj)])
                        tt(xT[:, n, :Tn], xT[:, n, :Tn], PS[b][:, :Tn], ALU.add, [("xT", n), ("ps", b)], [("xT", n)])

            if UPTO < 10:
                return
            norm_x(l, Tn, 32)
            for nb4 in range(4):
                s, wv = wload(("pg", nb4), wview(w_pg[l], nb4 * 512, 512), KT, 512)
                wps = nb4 % 2
                dma("pool", wppb[:, wps, :, :], w_pp[l].rearrange("(k p) n -> p k n", p=128)[:, :, nb4 * 512:(nb4 + 1) * 512], r=[], w=[("wpp", wps)])
                for j in range(4):
                    n = nb4 * 4 + j
                    bg = bankA()
                    for kt in range(KT):
                        mm(bg, PS[bg][:, :Tn], wv[:, kt, j * 128:(j + 1) * 128], aT[:, kt, :Tn], kt == 0, kt == KT - 1, r=WK(s) + [("aT", kt)])
                    bp = bankA()
                    for k2 in range(2):
                        mm(bp, PS[bp][:, :Tn], wppb[:, wps, k2, j * 128:(j + 1) * 128], pT[:, k2, :Tn], k2 == 0, k2 == 1, r=[("wpp", wps), "pT"])
                    t_ = nxt("p", 2)
                    act(T(t_), PS[bg][:, :Tn], AF.Sigmoid, [("ps", bg)], TK(t_))
                    tt(T(t_), T(t_), PS[bp][:, :Tn], ALU.mult, TK(t_) + [("ps", bp)], TK(t_))
                    tt(xT[:, n, :Tn], xT[:, n, :Tn], T(t_), ALU.add, [("xT", n)] + TK(t_), [("xT", n)])

            if UPTO < 11:
                return
            if l == 0 and NL > 1:
                for kt in range(KT):
                    dma("sp", hres[kt, :, tok0:tok0 + Tn], xT[:, kt, :Tn], r=[("xT", kt)], w=["hres"])
            else:
                dst = yp if prompt else ys
                for tb in range(nb):
                    for g in range(4):
                        b = bankA()
                        for j in range(4):
                            kt = 4 * g + j
                            tr(b, PS[b][0:bw, j * 128:(j + 1) * 128], xT[:, kt, tb * 128:tb * 128 + bw], ident[:], r=[("xT", kt), "ident"])
                        hf = g // 2
                        cp(stg[0:bw, hf, (g % 2) * 512:(g % 2 + 1) * 512], PS[b][0:bw, :], [("ps", b)], [("stg", hf)], eng="act" if g % 2 else "dve")
                    r0 = ti * TP + tb * 128 if prompt else 0
                    for hf in range(2):
                        dma("sp", dst[r0:r0 + bw, hf * 1024:(hf + 1) * 1024], stg[0:bw, hf, :], r=[("stg", hf)], w=[])

        def mT(kt, Tn):
            return aT[:, kt, :Tn]

        onec = sb("onec", [128, 1])
        ms(onec[:], 1.0, ["onec"])
        kst = sb("kst", [128, 2, NBT, 128])
        vst = sb("vst", [128, 2, 512])
        negcs = sb("negcs", [16, 2, 8])
        negcp = sb("negcp", [128, 16, 8])
        kTs = sb("kTs", [128, H, TS], BF16)
        vbs = sb("vbs", [16, 2, 1024], BF16)
        kc = sb("kc", [128, 1, 1024], BF16)
        vc = sb("vc", [128, 1, 1024], BF16)
        kTc = sb("kTc", [128, H, 128], BF16)
        cl_tok = sb("cl_tok", [128, 16, 8])

        def attn_epilogue(l, h, Tn, bo, br_):
            act(tmpB[:, 0, :Tn], PS[br_][:, :Tn], AF.Ln, r=[("ps", br_)], w=[("tmpB", 0)])
            act(tmpB[:, 0, :Tn], tmpB[:, 0, :Tn], AF.Exp, r=[("tmpB", 0)], w=[("tmpB", 0)], scale=-1.0)
            tt(tmpB[:, 1, :Tn], PS[bo][:, :Tn], tmpB[:, 0, :Tn], ALU.mult, [("ps", bo), ("tmpB", 0)], [("tmpB", 1)])
            s2 = nxt("s", 2)
            act(sqb[:, s2, :Tn], tmpB[:, 1, :Tn], AF.Square, [("tmpB", 1)], [("sqb", s2)])
            mm(4, PS[4][:, :Tn], ones_bf[:], sqb[:, s2, :Tn], h == 0, h == H - 1, r=["ones_bf", ("sqb", s2)])
            ts(mT(h, Tn), tmpB[:, 1, :Tn], vcol[:, 48 + h:49 + h], 0.0, ALU.mult, ALU.add, [("tmpB", 1), "vcol"], [("aT", h)])

        def sample_attention(l):
            Tn = TS
            for si in range(2):
                q0 = si * 16
                for hh in range(2):
                    dma("sp", cl_tok[:, hh * 8:(hh + 1) * 8, :], clf[l, si, hh * 1024:(hh + 1) * 1024, :].rearrange("(t p) x -> p t x", p=128), r=[], w=["cl_tok"])
                cpast = tmp[0:8, 0:8, :]
                cflat = lambda a, b_: tmp[0:8, a:b_, :]
                PK = [("tmp", x) for x in range(8)]
                for g in range(4):
                    b = bankA()
                    for j in range(4):
                        kb = g * 4 + j
                        tr(b, PS[b][0:8, j * 128:(j + 1) * 128], cl_tok[:, kb, :], ident[:], r=["cl_tok", "ident"])
                    cp(tmp[0:8, 2 * g:2 * g + 2, :], PS[b][0:8, :].rearrange("p (a t) -> p a t", t=TP), [("ps", b)], PK)
                for a in range(8):
                    ini = 0.0 if a == 0 else tmp[0:8, a - 1, TP - 1:TP]
                    A("dve", lambda e, a=a, ini=ini: e.tensor_tensor_scan(
                        out=tmp[0:8, a, :], data0=onec[0:8, 0:1].to_broadcast([8, TP]), data1=tmp[0:8, a, :],
                        initial=ini, op0=ALU.mult, op1=ALU.add), r=PK + ["onec"], w=PK)
                cp(sm[0:8, 17, 0:1], tmp[0:8, 7, TP - 1:TP], PK, ["smi17"])
                ts(tmp[0:8, 0:8, :], tmp[0:8, 0:8, :], -1.0, sm[0:8, 17, 0:1], ALU.mult, ALU.add, PK + ["smi17"], PK)
                b = bankA()
                for kb in range(16):
                    tr(b, PS[b][:, kb * 8:(kb + 1) * 8], tmp[0:8, kb // 2, (kb % 2) * 128:(kb % 2 + 1) * 128], ident[0:8, 0:8], r=PK + ["ident"])
                cp(negcp[:, :, :], PS[b][:, 0:128].rearrange("p (k x) -> p k x", x=8), [("ps", b)], ["negcp"])
                bo = 6
                br_ = 7
                for kb in range(17):
                    past = kb < 16
                    np_ = 128 if past else 16
                    if past:
                        ks = nxt("w", 2)
                        kcs = 0
                        dma("pool", kc[:, kcs, :], ck[l, si, kb * 128:(kb + 1) * 128, :], r=[], w=[("kc", kcs)])
                        dma("pool", vc[:, kcs, :], cv[l, si, kb * 128:(kb + 1) * 128, :], r=[], w=[("vc", kcs)])
                        for g in range(2):
                            b = bankA()
                            for j in range(4):
                                h = g * 4 + j
                                mm(b, PS[b][:, j * 128:(j + 1) * 128], kc[:, kcs, h * 128:(h + 1) * 128], identb[:], True, True, r=[("kc", kcs), "identb"])
                            cp(kTc[:, g * 4:g * 4 + 4, :], PS[b][:, :].rearrange("p (j t) -> p j t", t=128), [("ps", b)], [("kTc", g)], eng="act" if g else "dve")
                    bs = bankA()
                    for h in range(H):
                        if past:
                            mm(bs, PS[bs][:, h * 16:(h + 1) * 16], kTc[:, h, :], qT[:, h, q0:q0 + 16], True, False, r=[("kTc", h // 4), ("qT", h)])
                            mm(bs, PS[bs][:, h * 16:(h + 1) * 16], sel2[:, h, :], chl[:, q0:q0 + 16], False, True, r=["sel2", "chl"])
                        else:
                            mm(bs, PS[bs][0:16, h * 16:(h + 1) * 16], kTs[:, h, q0:q0 + 16], qT[:, h, q0:q0 + 16], True, False, r=[("kTs", h), ("qT", h)])
                            mm(bs, PS[bs][0:16, h * 16:(h + 1) * 16], sel2[:, h, 0:16], chl[:, q0:q0 + 16], False, False, r=["sel2", "chl"])
                            mm(bs, PS[bs][0:16, h * 16:(h + 1) * 16], identb[0:16, 0:16], negmask[0:16, 0:16], False, True, r=["identb", "negmask"])
                    pi = nxt("p", 3)
                    for h in range(H):
                        bias = negcp[:, kb, h:h + 1] if past else negcs[:, si, h:h + 1]
                        act(Pt[0:np_, pi, h * 16:(h + 1) * 16], PS[bs][0:np_, h * 16:(h + 1) * 16], AF.Exp, [("ps", bs), "negcp", "negcs"], [("Pt", pi)], bias=bias, scale=1.0)
                    for h in range(H):
                        vl = vc[:, 0, h * 128:(h + 1) * 128] if past else vbs[:, si, h * 128:(h + 1) * 128]
                        mm(bo, PS[bo][:, h * 16:(h + 1) * 16], vl, Pt[0:np_, pi, h * 16:(h + 1) * 16], kb == 0, kb == 16,
                           r=[("vc", 0), "vbs", ("Pt", pi)])
                    mm(br_, PS[br_][:, 0:128], ones_bf[0:np_, :], Pt[0:np_, pi, 0:128], kb == 0, kb == 16, r=["ones_bf", ("Pt", pi)])
                A("dve", lambda e: e.reciprocal(out=tmp[:, 8, 0:128], in_=PS[br_][:, 0:128]), r=[("ps", br_)], w=[("tmp", 8)])
                tt(tmp[:, 9, 0:128], PS[bo][:, 0:128], tmp[:, 8, 0:128], ALU.mult, [("ps", bo), ("tmp", 8)], [("tmp", 9)])
                s2 = nxt("s", 2)
                act(sqb[:, s2, 0:128], tmp[:, 9, 0:128], AF.Square, [("tmp", 9)], [("sqb", s2)])
                for h in range(H):
                    mm(4, PS[4][:, q0:q0 + 16], ones_bf[:], sqb[:, s2, h * 16:(h + 1) * 16], h == 0, h == H - 1, r=["ones_bf", ("sqb", s2)])
                    ts(aT[:, h, q0:q0 + 16], tmp[:, 9, h * 16:(h + 1) * 16], vcol[:, 48 + h:49 + h], 0.0, ALU.mult, ALU.add, [("tmp", 9), "vcol"], [("aT", h)])

        for l in range(NL):
            layer_setup(l)
            for ti in range(NT):
                do_tile(l, "p", ti)
            if SAMPLE:
                do_tile(l, "s", 0)
            outs = [(0, 0, srp[l]), (0, 1, sip[l])]
            if SAMPLE:
                outs += [(1, 0, srs[l, 0]), (1, 1, sis[l, 0]), (2, 0, srs[l, 1]), (2, 1, sis[l, 1])]
            for oi, (slot, ri, dst) in enumerate(outs):
                b = bankA()
                tr(b, PS[b][0:32, 0:128], Xst[:, slot, ri, :], ident[:], r=["Xst", "ident"])
                cp(stg[0:32, 0, oi * 128:(oi + 1) * 128], PS[b][0:32, 0:128], [("ps", b)], [("stg", 0)])
                dma("sp", dst, stg[0:32, 0, oi * 128:(oi + 1) * 128], r=[("stg", 0)], w=[])
        print('SBUF remaining', nc.sbuf_bytes_remaining)
        pr.finalize(es)
    return nc


_CACHE = {}
CFG = {}


def kernel(**inp):
    f = lambda a: np.ascontiguousarray(a, dtype=np.float32)
    nc = _CACHE.get("nc")
    if nc is None:
        nc = build(**{k: v for k, v in CFG.items() if k != 'NCORES'})
        _CACHE["nc"] = nc
    wnames = ["g_mix", "w_in", "b_f", "g_q", "g_k", "a_re", "a_im", "log_dt", "b_re", "b_im", "c_re", "c_im", "d_skip",
              "w_glu", "b_glu", "g_attn_out", "g_ssm_out", "w_out", "g_ffn", "w_gate", "w_up", "w_down", "g_ple",
              "w_ple_gate", "w_ple_proj"]
    shared = {n: f(inp[n]) for n in wnames}
    wi = shared["w_in"]
    shared["w_in"] = np.ascontiguousarray(np.concatenate([wi[..., :3072], wi[..., 3080:], wi[..., 3072:3080]], axis=-1))
    in_maps = []
    NCO = CFG.get('NCORES', 8)
    for c in range(NCO):
        m = dict(shared)
        m["xp"] = f(inp["x_prompt"][c])
        m["xs"] = f(inp["x_sample"][2 * c:2 * c + 2].reshape(TS, D))
        m["ck"] = f(inp["cache_k"][:, 2 * c:2 * c + 2].reshape(2, 2, SEQ, 1024))
        m["cv"] = f(inp["cache_v"][:, 2 * c:2 * c + 2].reshape(2, 2, SEQ, 1024))
        m["clf"] = f(inp["cache_logf"][:, 2 * c:2 * c + 2])
        m["sre"] = f(inp["state_ssm_re"][:, 2 * c:2 * c + 2])
        m["sim"] = f(inp["state_ssm_im"][:, 2 * c:2 * c + 2])
        m["pp"] = f(inp["p_prompt"][:, c])
        m["psm"] = f(inp["p_sample"][:, 2 * c:2 * c + 2].reshape(2, TS, 256))
        in_maps.append(m)
    res = run_bass_kernel_spmd(nc, in_maps, core_ids=list(range(NCO)))
    R = list(res.results)
    while len(R) < 8:
        R.append(R[0])
    cat = lambda k: np.stack([r[k] for r in R])
    y_prompt = cat("yp")
    y_sample = cat("ys").reshape(16, 16, D)
    k_prompt = cat("kp").transpose(1, 0, 2, 3).reshape(2, 8, SEQ, 8, 128)
    v_prompt = cat("vp").transpose(1, 0, 2, 3).reshape(2, 8, SEQ, 8, 128)
    logf_prompt = cat("lfp").transpose(1, 0, 2, 3)
    ssm_re_prompt = cat("srp").transpose(1, 0, 2, 3).reshape(2, 8, 64, 64)
    ssm_im_prompt = cat("sip").transpose(1, 0, 2, 3).reshape(2, 8, 64, 64)
    k_sample = cat("kso").transpose(1, 0, 2, 3).reshape(2, 16, 16, 8, 128)
    v_sample = cat("vso").transpose(1, 0, 2, 3).reshape(2, 16, 16, 8, 128)
    logf_sample = cat("lfs").transpose(1, 0, 2, 3).reshape(2, 16, 16, 8)
    ssm_re_sample = cat("srs").transpose(1, 0, 2, 3, 4).reshape(2, 16, 64, 64)
    ssm_im_sample = cat("sis").transpose(1, 0, 2, 3, 4).reshape(2, 16, 64, 64)
    return (y_prompt, y_sample, k_prompt, v_prompt, logf_prompt, ssm_re_prompt, ssm_im_prompt,
            k_sample, v_sample, logf_sample, ssm_re_sample, ssm_im_sample)
```

```python
import math
import os
import numpy as np
from contextlib import ExitStack
import concourse.bass as bass
import concourse.mybir as mybir
from concourse.bass_utils import run_bass_kernel_spmd

F32 = mybir.dt.float32
BF16 = mybir.dt.bfloat16
I32 = mybir.dt.int32
AF = mybir.ActivationFunctionType
ALU = mybir.AluOpType

D = 2048
KT = 16
H = 8
DFF = 5632
FT = 44
INC = 4104
SEQ = 2048
TP = 256
NBT = TP // 128
TS = 32
EPS = 1e-6
TWO_PI = 2.0 * math.pi
NRING = 8


class Op:
    __slots__ = ("eng", "fn", "deps", "dma", "stream", "seq", "waits", "mark", "val", "snap")

    def __init__(self, eng, fn, deps, dma):
        self.eng = eng
        self.fn = fn
        self.deps = deps
        self.dma = dma
        self.mark = False
        self.waits = []


class Prog:
    def __init__(self, nc):
        self.nc = nc
        self.ops = []
        self.lastw = {}
        self.readers = {}
        self.cur = []

    def add(self, eng, fn, r=(), w=(), dma=False):
        self.cur.append((eng, fn, tuple(r), tuple(w), dma))

    def _resolve(self):
        for (eng, fn, r, w, dma) in self.cur:
            idx = len(self.ops)
            deps = set()
            for k in r:
                lw = self.lastw.get(k)
                if lw is not None:
                    deps.add(lw)
            for k in w:
                lw = self.lastw.get(k)
                if lw is not None:
                    deps.add(lw)
                rd = self.readers.get(k)
                if rd:
                    deps.update(rd.values())
            sk = (eng, idx) if dma else eng
            for k in r:
                self.readers.setdefault(k, {})[sk] = idx
            for k in w:
                self.lastw[k] = idx
                self.readers[k] = {}
            self.ops.append(Op(eng, fn, deps, dma))

    def finalize(self, es):
        nc = self.nc
        self._resolve()
        ops = self.ops
        engs = ["pe", "act", "dve", "pool", "sp"]
        cnt = {e: 0 for e in engs}
        dcnt = {e: 0 for e in engs}
        for o in ops:
            if o.dma:
                n = dcnt[o.eng]
                dcnt[o.eng] += 1
                o.stream = (o.eng, n % NRING)
                o.seq = n // NRING + 1
            else:
                cnt[o.eng] += 1
                o.stream = o.eng
                o.seq = cnt[o.eng]
        clock = {e: {} for e in engs}
        for o in ops:
            ck = clock[o.eng]
            need = {}
            if o.dma and o.seq > 1:
                need[o.stream] = (o.seq - 1, None)
            for j in o.deps:
                p = ops[j]
                if p.stream == "pe" and o.eng == "pe" and not o.dma:
                    continue
                cur = need.get(p.stream)
                if cur is None or cur[0] < p.seq:
                    need[p.stream] = (p.seq, j)
            for s, (q, j) in need.items():
                if ck.get(s, 0) >= q:
                    continue
                o.waits.append((s, q, j))
                ck[s] = q
                if j is not None:
                    ops[j].mark = True
                    for s2, q2 in ops[j].snap.items():
                        if ck.get(s2, 0) < q2:
                            ck[s2] = q2
            o.snap = dict(ck)
        rank = {e: 0 for e in engs}
        for o in ops:
            if o.dma:
                o.val = 16 * o.seq
            elif o.mark:
                rank[o.eng] += 1
                o.val = rank[o.eng]
        sems = {}
        for e in ["pe", "act", "dve", "pool"]:
            sems[e] = es.enter_context(nc.semaphore("c_" + e))
        for e in ["pool", "sp"]:
            for i in range(NRING):
                sems[(e, i)] = es.enter_context(nc.semaphore("d_%s%d" % (e, i)))
        streams = {e: [o for o in ops if o.eng == e] for e in engs}
        final = {}
        for o in ops:
            if o.dma:
                final[o.stream] = max(final.get(o.stream, 0), o.val)

        def emit(e, eng):
            for o in streams[e]:
                for (s, q, j) in o.waits:
                    if j is not None:
                        v = ops[j].val
                    else:
                        v = 16 * q
                    eng.wait_ge(sems[s], v)
                ins = o.fn(eng)
                if o.dma:
                    ins.then_inc(sems[o.stream], 16)
                elif o.mark:
                    ins.then_inc(sems[o.eng], 1)
            if e == "sp":
                for s, v in final.items():
                    eng.wait_ge(sems[s], v)

        block = es.enter_context(nc.Block())

        @block.tensor
        def _(e):
            emit("pe", e)

        @block.scalar
        def _(e):
            emit("act", e)

        @block.vector
        def _(e):
            emit("dve", e)

        @block.gpsimd
        def _(e):
            emit("pool", e)

        @block.sync
        def _(e):
            emit("sp", e)


def build(NL=2, NT=SEQ // TP, SAMPLE=True, UPTO=99):
    nc = bass.Bass("TRN2", target_bir_lowering=False)

    def din(name, shape):
        return nc.dram_tensor(name, list(shape), F32, kind="ExternalInput").ap()

    def dout(name, shape):
        return nc.dram_tensor(name, list(shape), F32, kind="ExternalOutput").ap()

    xp = din("xp", [SEQ, D])
    xs = din("xs", [TS, D])
    ck = din("ck", [2, 2, SEQ, 1024])
    cv = din("cv", [2, 2, SEQ, 1024])
    clf = din("clf", [2, 2, SEQ, 8])
    sre = din("sre", [2, 2, 64, 64])
    sim = din("sim", [2, 2, 64, 64])
    pp = din("pp", [2, SEQ, 256])
    psm = din("psm", [2, TS, 256])
    g_mix = din("g_mix", [2, D])
    w_in = din("w_in", [2, D, INC])
    b_f = din("b_f", [2, 8])
    g_q = din("g_q", [2, 128])
    g_k = din("g_k", [2, 128])
    a_re = din("a_re", [2, 64, 64])
    a_im = din("a_im", [2, 64, 64])
    log_dt = din("log_dt", [2, 64])
    b_re = din("b_re", [2, 64, 64, 16])
    b_im = din("b_im", [2, 64, 64, 16])
    c_re = din("c_re", [2, 64, 16, 64])
    c_im = din("c_im", [2, 64, 16, 64])
    d_skip = din("d_skip", [2, 1024])
    w_glu = din("w_glu", [2, 1024, 1024])
    b_glu = din("b_glu", [2, 1024])
    g_ao = din("g_attn_out", [2, 1024])
    g_so = din("g_ssm_out", [2, 1024])
    w_out = din("w_out", [2, D, D])
    g_ffn = din("g_ffn", [2, D])
    w_gate = din("w_gate", [2, D, DFF])
    w_up = din("w_up", [2, D, DFF])
    w_down = din("w_down", [2, DFF, D])
    g_ple = din("g_ple", [2, D])
    w_pg = din("w_ple_gate", [2, D, D])
    w_pp = din("w_ple_proj", [2, 256, D])

    yp = dout("yp", [SEQ, D])
    ys = dout("ys", [TS, D])
    kp = dout("kp", [2, SEQ, 1024])
    vp = dout("vp", [2, SEQ, 1024])
    lfp = dout("lfp", [2, SEQ, 8])
    srp = dout("srp", [2, 32, 128])
    sip = dout("sip", [2, 32, 128])
    kso = dout("kso", [2, TS, 1024])
    vso = dout("vso", [2, TS, 1024])
    lfs = dout("lfs", [2, TS, 8])
    srs = dout("srs", [2, 2, 32, 128])
    sis = dout("sis", [2, 2, 32, 128])
    hres = nc.dram_tensor("hres", [KT, 128, SEQ + TS], F32, kind="Internal").ap()
    wsc = nc.dram_tensor("wsc", [2, 56, 128, 8192], BF16, kind="Internal").ap()
    tabsc = nc.dram_tensor("tabsc", [32, 128, 2, TP], F32, kind="Internal").ap()

    es = ExitStack()
    with es:
        def sb(name, shape, dt=F32):
            return es.enter_context(nc.sbuf_tensor(name, list(shape), dt))

        pr = Prog(nc)
        A = pr.add

        ident = sb("ident", [128, 128])
        ones_bf = sb("ones_bf", [128, 128], BF16)
        negmask = sb("negmask", [128, 128], BF16)
        identb = sb("identb", [128, 128], BF16)
        sel2 = sb("sel2", [40, 8, 128], BF16)
        tau = sb("tau", [128, TP])
        taus = sb("taus", [128, TS])
        KTb = sb("KTb", [128, H, SEQ], BF16)
        Vb = sb("Vb", [128, 16, 1024], BF16)
        negc = sb("negc", [128, 16, 8])
        vcol = sb("vcol", [128, 96])
        bfc = sb("bfc", [40, 1])
        wf2 = sb("wf2", [128, KT, 40], BF16)
        wppb = sb("wppb", [128, 2, 2, 512], BF16)
        magT = sb("magT", [128, 32])
        thn = sb("thn", [128, 32])
        cth = sb("cth", [128, 32])
        sth = sb("sth", [128, 32])
        winit = sb("winit", [128, 3, 2, 32])
        Bc = sb("Bc", [128, 8, 2, 128], BF16)
        Cc = sb("Cc", [128, 32, 2, 32], BF16)
        Xst = sb("Xst", [128, 3, 2, 32])
        ccar = sb("ccar", [40, 3])
        sm = sb("sm", [128, 24, 32])
        smi = sb("smi", [128, 32], I32)
        xT = sb("xT", [128, KT, TP])
        aT = sb("aT", [128, KT, TP], BF16)
        wsl = [sb("wsl%d" % i, [128, 8192], BF16) for i in range(2)]
        rstd = sb("rstd", [128, 3, TP])
        sqb = sb("sqb", [128, 2, TP], BF16)
        reg = sb("reg", [128, 8 * TP])
        uT = reg[:, :].rearrange("p (a t) -> p a t", t=TP)
        regb = sb("regb", [128, 22 * TP], BF16)
        ub = regb[:, 0:8 * TP].rearrange("p (a t) -> p a t", t=TP)
        qT = regb[:, 8 * TP:16 * TP].rearrange("p (a t) -> p a t", t=TP)
        Pt = regb[:, 16 * TP:19 * TP].rearrange("p (a t) -> p a t", t=TP)
        xrb = regb[:, 19 * TP:21 * TP].rearrange("p (a t) -> p a t", t=TP)
        actT = regb[:, :].rearrange("p (a t) -> p a t", t=TP)
        tmp = sb("tmp", [128, 11, TP])
        tmpi = sb("tmpi", [128, TP], I32)
        tmpB = sb("tmpB", [128, 2, TP])
        chl = sb("chl", [40, TP], BF16)
        stg = sb("stg", [128, 2, 1024])
        pT = sb("pT", [128, 2, TP], BF16)
        PS = [es.enter_context(nc.psum_tensor("ps%d" % i, [128, 512], F32)) for i in range(8)]

        rot = {"a": 0, "w": 0, "r": 0, "t": 0, "p": 0, "s": 0, "banks": [0, 1, 2, 3], "sbanks": [4, 5]}

        def bankA():
            lst = rot["banks"]
            rot["a"] = (rot["a"] + 1) % len(lst)
            return lst[rot["a"]]

        def nxt(name, n):
            rot[name] = (rot[name] + 1) % n
            return rot[name]

        def mm(b, out, lhsT, rhs, start, stop, r, tp=None, w=None):
            kw = {}
            if tp is not None:
                kw["tile_position"] = tp
            A("pe", lambda e: e.matmul(out, lhsT=lhsT, rhs=rhs, start=start, stop=stop, **kw),
              r=r, w=[("ps", b)] if w is None else w)

        def tr(b, out, in_, idn, r):
            A("pe", lambda e: e.transpose(out=out, in_=in_, identity=idn), r=r, w=[("ps", b)])

        def act(out, in_, func, r, w, bias=None, scale=None):
            kw = {}
            if bias is not None:
                kw["bias"] = bias
            if scale is not None:
                kw["scale"] = scale
            A("act", lambda e: e.activation(out=out, in_=in_, func=func, **kw), r=r, w=w)

        def tt(out, in0, in1, op, r, w, eng="dve"):
            A(eng, lambda e: e.tensor_tensor(out=out, in0=in0, in1=in1, op=op), r=r, w=w)

        def ts(out, in0, s1, s2, op0, op1, r, w, eng="dve"):
            A(eng, lambda e: e.tensor_scalar(out=out, in0=in0, scalar1=s1, scalar2=s2, op0=op0, op1=op1), r=r, w=w)

        def stt(out, in0, scalar, in1, op0, op1, r, w, eng="dve"):
            A(eng, lambda e: e.scalar_tensor_tensor(out=out, in0=in0, scalar=scalar, in1=in1, op0=op0, op1=op1), r=r, w=w)

        def cp(out, in_, r, w, eng="dve"):
            if eng == "act":
                A(eng, lambda e: e.activation(out=out, in_=in_, func=AF.Copy), r=r, w=w)
            else:
                A(eng, lambda e: e.tensor_copy(out=out, in_=in_), r=r, w=w)

        def ms(ap, val, w, eng="dve"):
            A(eng, lambda e: e.memset(ap, val), r=[], w=w)

        def dma(q, out, in_, r, w):
            A(q, lambda e: e.dma_start(out=out, in_=in_), r=r, w=w, dma=True)

        ms(ident[:], 0.0, ["ident"], eng="pool")
        A("pool", lambda e: e.affine_select(out=ident[:], in_=ident[:], compare_op=ALU.not_equal, fill=1.0,
                                            base=0, pattern=[[-1, 128]], channel_multiplier=1),
          r=["ident"], w=["ident"])
        ms(ones_bf[:], 1.0, ["ones_bf"])
        cp(identb[:], ident[:], ["ident"], ["identb"])
        ms(negmask[:], 0.0, ["negmask"], eng="pool")
        A("pool", lambda e: e.affine_select(out=negmask[:], in_=negmask[:], compare_op=ALU.is_ge, fill=-30000.0,
                                            base=0, pattern=[[1, 128]], channel_multiplier=-1),
          r=["negmask"], w=["negmask"])
        A("pool", lambda e: e.iota(tmpi[:], pattern=[[1, TP]], base=0, channel_multiplier=0), r=[], w=["tmpi"])
        cp(tau[:], tmpi[:], ["tmpi"], ["tau"])
        cp(taus[:, 0:16], tmpi[:, 0:16], ["tmpi"], ["tau"])
        cp(taus[:, 16:32], tmpi[:, 0:16], ["tmpi"], ["tau"])
        ms(sel2[:], 0.0, ["sel2"])
        for h in range(H):
            cp(sel2[0:8, h, :], ident[0:8, h:h + 1].to_broadcast([8, 128]), ["ident", "sel2"], ["sel2"])
            cp(sel2[32:40, h, :], ident[32:40, 32 + h:33 + h].to_broadcast([8, 128]), ["ident", "sel2"], ["sel2"])
        ms(Xst[:], 0.0, ["Xst"])
        ms(ccar[:], 0.0, ["ccar"])

        def rstd_from(src, skeys, rs, Tn, nfeat):
            act(rstd[:, rs, :Tn], src, AF.Ln, r=skeys + ["epsc"], w=[("rstd", rs)], bias=epsc[:, 0:1], scale=1.0 / nfeat)
            act(rstd[:, rs, :Tn], rstd[:, rs, :Tn], AF.Exp, r=[("rstd", rs)], w=[("rstd", rs)], scale=-0.5)

        def rms_from_sq(sq_list, Tn, nfeat, rkeys):
            sbl = rot["sbanks"]
            rot["r"] = (rot["r"] + 1) % len(sbl)
            b = sbl[rot["r"]]
            n = len(sq_list)
            for i, (ap, keys) in enumerate(sq_list):
                s = nxt("s", 2)
                act(sqb[:, s, :Tn], ap, AF.Square, r=keys, w=[("sqb", s)])
                mm(b, PS[b][:, :Tn], ones_bf[:], sqb[:, s, :Tn], i == 0, i == n - 1, r=["ones_bf", ("sqb", s)])
            rs = nxt("t", 3)
            rstd_from(PS[b][:, :Tn], [("ps", b)], rs, Tn, nfeat)
            return rs

        epsc = sb("epsc", [128, 1])
        ms(epsc[:], EPS, ["epsc"])

        def norm_x(l, Tn, gofs):
            rs = rms_from_sq([(xT[:, kt, :Tn], [("xT", kt)]) for kt in range(KT)], Tn, D, None)
            for kt in range(KT):
                stt(aT[:, kt, :Tn], xT[:, kt, :Tn], vcol[:, gofs + kt:gofs + kt + 1], rstd[:, rs, :Tn], ALU.mult, ALU.mult,
                    r=[("xT", kt), "vcol", ("rstd", rs)], w=[("aT", kt)])

        bidc = {}
        cur = {"l": 0, "first": True}

        def wblock(name, parts):
            s = nxt("w", 2)
            l = cur["l"]
            bid = bidc.setdefault(name, len(bidc))
            total = max(c0 + nk * ncol for (c0, nk, ncol, _) in parts)
            if cur["first"] or os.environ.get("NOWSC"):
                for pi_, (c0, nk, ncol, src3) in enumerate(parts):
                    view = wsl[s][:, c0:c0 + nk * ncol].rearrange("p (k n) -> p k n", n=ncol)
                    dma("pool", view, src3, r=[], w=[("w", s, pi_)])
                if not os.environ.get("NOWSC"):
                    dma("sp", wsc[l, bid, :, 0:total], wsl[s][:, 0:total], r=WK(s), w=[("wsc", l, bid)])
            else:
                half = total // 2
                dma("pool", wsl[s][:, 0:half], wsc[l, bid, :, 0:half], r=[("wsc", l, bid)], w=[("w", s, 0)])
                dma("pool", wsl[s][:, half:total], wsc[l, bid, :, half:total], r=[("wsc", l, bid)], w=[("w", s, 1)])
            return s

        def wload(name, src3, nk, ncol):
            step = nk // 4
            parts = [(k0 * ncol, step, ncol, src3[:, k0:k0 + step, :]) for k0 in range(0, nk, step)]
            s = wblock(name, parts)
            return s, wsl[s][:, 0:nk * ncol].rearrange("p (k n) -> p k n", n=ncol)

        def WK(s):
            return [("w", s, i) for i in range(4)]

        def wview(w2d, c0, ncol):
            return w2d.rearrange("(k p) n -> p k n", p=128)[:, :, c0:c0 + ncol]

        def layer_setup(l):
            rows = [(g_mix, 0, 16), (g_ffn, 16, 16), (g_ple, 32, 16), (g_ao, 48, 8), (g_so, 56, 8),
                    (d_skip, 64, 8), (b_glu, 72, 8), (g_q, 80, 1), (g_k, 81, 1)]
            st = stg[:, 0, 0:128]
            for (t, r0, n) in rows:
                dma("sp", stg[r0:r0 + n, 0, 0:128], t[l].rearrange("(r c) -> r c", c=128), r=[], w=["stg"])
            b = bankA()
            tr(b, PS[b][:, 0:82], st[0:82, :], ident[0:82, 0:82], r=["stg", "ident"])
            cp(vcol[:, 0:82], PS[b][:, 0:82], [("ps", b)], ["vcol"])
            ts(vcol[:, 82:83], vcol[:, 80:81], 128.0 ** -0.5, 0.0, ALU.mult, ALU.add, ["vcol"], ["vcol"])
            ms(bfc[:], 0.0, ["bfc"])
            dma("sp", bfc[0:8, :], b_f[l].rearrange("(r c) -> r c", c=1), r=[], w=["bfc"])
            dma("sp", bfc[32:40, :], b_f[l].rearrange("(r c) -> r c", c=1), r=[], w=["bfc"])
            ts(bfc[:], bfc[:], -1.0, 0.0, ALU.mult, ALU.add, ["bfc"], ["bfc"])
            ms(wf2[:], 0.0, ["wf2"])
            wfsrc = wview(w_in[l], 4096, 8)
            dma("pool", wf2[:, :, 0:8], wfsrc, r=[], w=["wf2"])
            dma("pool", wf2[:, :, 32:40], wfsrc, r=[], w=["wf2"])
            sA = stg[:, 1, 0:128]
            dma("sp", stg[0:32, 1, 0:128], a_re[l].rearrange("(gp gl) p -> gp (gl p)", gl=2), r=[], w=["stgA"])
            dma("sp", stg[32:64, 1, 0:128], a_im[l].rearrange("(gp gl) p -> gp (gl p)", gl=2), r=[], w=["stgA"])
            dma("sp", stg[64:96, 1, 128:130], log_dt[l].rearrange("(gp gl) -> gp gl", gl=2), r=[], w=["stgA"])
            for gl in range(2):
                cp(stg[64:96, 1, gl * 64:(gl + 1) * 64], stg[64:96, 1, 128 + gl:129 + gl].to_broadcast([32, 64]),
                   ["stgA"], ["stgA"])
            b = bankA()
            tr(b, PS[b][:, 0:96], sA[0:96, :], ident[0:96, 0:96], r=["stgA", "ident"])
            S = lambda i: sm[:, i, :]
            K = ["sm"]
            cp(sm[:, 0:3, :], PS[b][:, 0:96].rearrange("p (a g) -> p a g", g=32), [("ps", b)], K)
            act(S(3), S(2), AF.Exp, K, K)
            tt(S(4), S(0), S(3), ALU.mult, K, K)
            act(magT[:], S(4), AF.Exp, K, ["magT"])
            tt(S(5), S(1), S(3), ALU.mult, K, K)
            ts(thn[:], S(5), 1.0 / TWO_PI, 0.0, ALU.mult, ALU.add, K, ["thn"])
            cp(smi[:], thn[:], ["thn"], ["smi"])
            tt(S(6), thn[:], smi[:], ALU.subtract, ["thn", "smi"], K)
            act(S(7), S(6), AF.Sin, K, K, scale=TWO_PI)
            act(S(8), S(6), AF.Sin, K, K, scale=math.pi)
            tt(S(9), S(8), S(8), ALU.mult, K, K)
            ts(S(9), S(9), -2.0, 1.0, ALU.mult, ALU.add, K, K)
            cp(cth[:], S(9), K, ["cth"])
            cp(sth[:], S(7), K, ["sth"])
            tt(S(10), magT[:], S(9), ALU.mult, K + ["magT"], K)
            tt(S(11), magT[:], S(7), ALU.mult, K + ["magT"], K)
            ts(S(10), S(10), -1.0, 0.0, ALU.add, ALU.add, K, K)
            tt(S(12), S(0), S(0), ALU.mult, K, K)
            tt(S(13), S(1), S(1), ALU.mult, K, K)
            tt(S(12), S(12), S(13), ALU.add, K, K)
            A("dve", lambda e: e.reciprocal(out=S(12), in_=S(12)), r=K, w=K)
            tt(S(13), S(10), S(0), ALU.mult, K, K)
            tt(S(14), S(11), S(1), ALU.mult, K, K)
            tt(S(13), S(13), S(14), ALU.add, K, K)
            tt(S(13), S(13), S(12), ALU.mult, K, K)
            tt(S(14), S(11), S(0), ALU.mult, K, K)
            tt(S(15), S(10), S(1), ALU.mult, K, K)
            tt(S(14), S(14), S(15), ALU.subtract, K, K)
            tt(S(14), S(14), S(12), ALU.mult, K, K)
            tmpf = tmp[:, :, :].rearrange("p a t -> p (a t)")
            v3 = lambda i: tmpf[:, i * 512:(i + 1) * 512].rearrange("p (g c) -> p g c", c=16)
            br, bi, t2, t3, t4 = v3(0), v3(1), v3(2), v3(3), v3(4)
            TA = [("tmp", x) for x in range(10)]
            for gl in range(2):
                dma("sp", br[gl * 64:(gl + 1) * 64], b_re[l].rearrange("(gp gl) p c -> gl p gp c", gl=2)[gl], r=[], w=TA)
                dma("sp", bi[gl * 64:(gl + 1) * 64], b_im[l].rearrange("(gp gl) p c -> gl p gp c", gl=2)[gl], r=[], w=TA)
            crb = sm[:, 13:14, :].rearrange("p a g -> p g a").to_broadcast([128, 32, 16])
            cib = sm[:, 14:15, :].rearrange("p a g -> p g a").to_broadcast([128, 32, 16])
            KK = TA + ["sm"]
            tt(t2, br, crb, ALU.mult, KK, TA)
            tt(t3, bi, cib, ALU.mult, KK, TA)
            tt(t2, t2, t3, ALU.subtract, TA, TA)
            tt(t3, bi, crb, ALU.mult, KK, TA)
            tt(t4, br, cib, ALU.mult, KK, TA)
            tt(t3, t3, t4, ALU.add, TA, TA)
            RG = ["regS"] + [("uT", x) for x in range(8)]
            srcB = reg[:, 0:2048].rearrange("p (q r x) -> p q r x", r=2, x=128)
            ms(reg[:, 0:2048], 0.0, RG)
            for ri, tsrc in enumerate((t2, t3)):
                for gl in range(2):
                    dst = srcB[gl * 64:(gl + 1) * 64, :, ri, :].rearrange("p q (j x) -> p q j x", x=32)[:, :, :, gl * 16:gl * 16 + 16]
                    cp(dst, tsrc[gl * 64:(gl + 1) * 64].rearrange("p (q j) c -> p q j c", j=4), TA + RG, RG)
            for q in range(8):
                b = bankA()
                for ri in range(2):
                    tr(b, PS[b][:, ri * 128:(ri + 1) * 128], srcB[:, q, ri, :], ident[:], r=RG + ["ident"])
                cp(Bc[:, q, :, :], PS[b][:, 0:256].rearrange("p (r x) -> p r x", x=128), [("ps", b)], ["Bc"])
            stC = reg[0:32, 0:2048].rearrange("p (g x) -> p g x", x=128)
            for ri, csrc in enumerate((c_re, c_im)):
                for g0 in range(0, 32, 16):
                    ms(reg[0:32, 0:2048], 0.0, RG)
                    for gl in range(2):
                        dma("sp", stC[gl * 16:(gl + 1) * 16, :, gl * 64:(gl + 1) * 64],
                            csrc[l].rearrange("(gp gl) c p -> gl c gp p", gl=2)[gl][:, g0:g0 + 16, :], r=[], w=RG)
                    b = bankA()
                    for gg in range(16):
                        tr(b, PS[b][:, gg * 32:(gg + 1) * 32], stC[:, gg, :], ident[0:32, 0:32], r=RG + ["ident"])
                    src = PS[b][:, :].rearrange("p (g x) -> p g x", x=32)
                    if ri == 0:
                        cp(Cc[:, g0:g0 + 16, 0, :], src, [("ps", b)], ["Cc"])
                    else:
                        ts(Cc[:, g0:g0 + 16, 1, :], src, -1.0, 0.0, ALU.mult, ALU.add, [("ps", b)], ["Cc"])
            if SAMPLE:
                for si in range(2):
                    dma("sp", stg[si * 64:si * 64 + 32, 1, 0:128], sre[l, si].rearrange("(gp gl) p -> gp (gl p)", gl=2), r=[], w=["stgA"])
                    dma("sp", stg[si * 64 + 32:si * 64 + 64, 1, 0:128], sim[l, si].rearrange("(gp gl) p -> gp (gl p)", gl=2), r=[], w=["stgA"])
                b = bankA()
                tr(b, PS[b][:, 0:128], sA[:, :], ident[:], r=["stgA", "ident"])
                cp(Xst[:, 1:3, :, :], PS[b][:, 0:128].rearrange("p (s r g) -> p s r g", r=2, g=32), [("ps", b)], ["Xst"])
            ms(Xst[:, 0, :, :], 0.0, ["Xst"])
            ms(ccar[:], 0.0, ["ccar"])

        def do_tile(l, kind, ti):
            prompt = kind == "p"
            Tn = TP if prompt else TS
            nb = NBT if prompt else 1
            bw = 128 if prompt else TS
            tok0 = ti * TP if prompt else SEQ
            segs = [(0, TP, 0)] if prompt else [(0, 16, 1), (16, 16, 2)]
            T = lambda i: tmp[:, i, :Tn]
            TK = lambda *i: [("tmp", x) for x in i]
            TB = lambda i: tmpB[:, i, :Tn]
            TBK = lambda i: [("tmpB", i)]
            st = {}
            cur["l"] = l
            cur["first"] = prompt and ti == 0

            if l == 0:
                src = xp if prompt else xs
                for tb in range(nb):
                    r0 = ti * TP + tb * 128 if prompt else 0
                    for hf in range(2):
                        dma("sp", stg[0:bw, hf, :], src[r0:r0 + bw, hf * 1024:(hf + 1) * 1024], r=[], w=[("stg", hf)])
                    for g in range(4):
                        b = bankA()
                        for j in range(4):
                            kt = 4 * g + j
                            tr(b, PS[b][:, j * bw:(j + 1) * bw], stg[0:bw, kt // 8, (kt % 8) * 128:(kt % 8 + 1) * 128],
                               ident[0:bw, 0:bw], r=[("stg", kt // 8), "ident"])
                        cp(xT[:, 4 * g:4 * g + 4, tb * 128:tb * 128 + bw], PS[b][:, 0:4 * bw].rearrange("p (j t) -> p j t", t=bw),
                           [("ps", b)], [("xT", 4 * g + j) for j in range(4)], eng="act" if g % 2 else "dve")
            else:
                for kt in range(KT):
                    dma("sp", xT[:, kt, :Tn], hres[kt, :, tok0:tok0 + Tn], r=["hres"], w=[("xT", kt)])
            psrc = pp if prompt else psm
            for tb in range(nb):
                r0 = ti * TP + tb * 128 if prompt else 0
                dma("sp", stg[0:bw, 1, 0:256], psrc[l, r0:r0 + bw, :], r=[], w=[("stg", 1)])
                b = bankA()
                for j in range(2):
                    tr(b, PS[b][:, j * bw:(j + 1) * bw], stg[0:bw, 1, j * 128:(j + 1) * 128], ident[0:bw, 0:bw], r=[("stg", 1), "ident"])
                cp(pT[:, :, tb * 128:tb * 128 + bw], PS[b][:, 0:2 * bw].rearrange("p (j t) -> p j t", t=bw), [("ps", b)], ["pT"])

            norm_x(l, Tn, 0)

            if UPTO < 1:
                return
            b = bankA()
            for kt in range(KT):
                mm(b, PS[b][0:40, :Tn], wf2[:, kt, :], aT[:, kt, :Tn], kt == 0, kt == KT - 1, r=["wf2", ("aT", kt)])
            cfa = lambda i: tmp[0:40, i, :Tn]
            act(cfa(0), PS[b][0:40, :Tn], AF.Exp, [("ps", b), "bfc"], TK(0), bias=bfc[:, 0:1], scale=-1.0)
            act(cfa(0), cfa(0), AF.Ln, TK(0), TK(0), bias=onec[0:40, 0:1], scale=1.0)
            ts(cfa(1), cfa(0), -1.0, 0.0, ALU.mult, ALU.add, TK(0), TK(1))
            for (c0, n, slot) in segs:
                A("dve", lambda e, c0=c0, n=n, slot=slot: e.tensor_tensor_scan(
                    out=tmp[0:40, 2, c0:c0 + n], data0=onec[0:40, 0:1].to_broadcast([40, n]), data1=tmp[0:40, 1, c0:c0 + n],
                    initial=ccar[:, slot:slot + 1], op0=ALU.mult, op1=ALU.add), r=TK(1) + ["ccar"], w=TK(2))
                cp(ccar[:, slot:slot + 1], tmp[0:40, 2, c0 + n - 1:c0 + n], TK(2), ["ccar"])
            cp(chl[:, :Tn], cfa(2), TK(2), ["chl"])
            tt(cfa(3), cfa(2), chl[:, :Tn], ALU.subtract, TK(2) + ["chl"], TK(3))
            cp(chl[32:40, :Tn], tmp[32:40, 3, :Tn], TK(3), ["chl"])
            b = bankA()
            if prompt:
                for tb in range(nb):
                    tr(b, PS[b][:, tb * 16:tb * 16 + 8], tmp[0:8, 1, tb * 128:(tb + 1) * 128], ident[0:8, 0:8], r=TK(1) + ["ident"])
                    tr(b, PS[b][:, tb * 16 + 8:tb * 16 + 16], tmp[0:8, 2, tb * 128:(tb + 1) * 128], ident[0:8, 0:8], r=TK(2) + ["ident"])
                pv = PS[b][:, 0:nb * 16].rearrange("p (t x) -> p t x", x=16)
                lft = tmp[:, 9, 0:nb * 8].rearrange("p (t x) -> p t x", x=8)
                cp(lft, pv[:, :, 0:8], [("ps", b)], TK(9))
                ts(negc[:, ti * NBT:(ti + 1) * NBT, :], pv[:, :, 8:16], -1.0, 0.0, ALU.mult, ALU.add, [("ps", b)], [("negc", ti)])
                dma("sp", lfp[l, ti * TP:(ti + 1) * TP, :].rearrange("(t p) x -> p t x", p=128), lft, r=TK(9), w=[])
            else:
                for si in range(2):
                    tr(b, PS[b][0:16, si * 16:si * 16 + 8], tmp[0:8, 1, si * 16:(si + 1) * 16], ident[0:8, 0:8], r=TK(1) + ["ident"])
                    tr(b, PS[b][0:16, si * 16 + 8:si * 16 + 16], tmp[0:8, 2, si * 16:(si + 1) * 16], ident[0:8, 0:8], r=TK(2) + ["ident"])
                pv = PS[b][0:16, 0:32].rearrange("p (t x) -> p t x", x=16)
                lft = tmp[0:16, 9, 0:16].rearrange("p (t x) -> p t x", x=8)
                cp(lft, pv[:, :, 0:8], [("ps", b)], TK(9))
                ts(negcs[:, :, :], pv[:, :, 8:16], -1.0, 0.0, ALU.mult, ALU.add, [("ps", b)], ["negcs"])
                dma("sp", lfs[l].rearrange("(s p) x -> p s x", p=16), lft, r=TK(9), w=[])

            if UPTO < 2:
                return
            for ub_i in range(2):
                s, wv = wload(("u", ub_i), wview(w_in[l], 3072 + ub_i * 512, 512), KT, 512)
                for j in range(0 if os.environ.get("SKIPMM") else 4):
                    b = bankA()
                    for kt in range(KT):
                        mm(b, PS[b][:, :Tn], wv[:, kt, j * 128:(j + 1) * 128], aT[:, kt, :Tn], kt == 0, kt == KT - 1, r=WK(s) + [("aT", kt)])
                    c = ub_i * 4 + j
                    if os.environ.get("SKIPCP"):
                        continue
                    cp(uT[:, c, :Tn], PS[b][:, :Tn], [("ps", b)], [("uT", c)], eng="act")
                    cp(ub[:, c, :Tn], uT[:, c, :Tn], [("uT", c)], [("ub", c)])

            def part_qkv():
                for which in range(2):
                    for hb in range(2):
                        s, wv = wload(("qk", which, hb), wview(w_in[l], which * 1024 + hb * 512, 512), KT, 512)
                        for j in range(4):
                            h = hb * 4 + j
                            b = bankA()
                            for kt in range(KT):
                                mm(b, PS[b][:, :Tn], wv[:, kt, j * 128:(j + 1) * 128], aT[:, kt, :Tn], kt == 0, kt == KT - 1, r=WK(s) + [("aT", kt)])
                            rs = rms_from_sq([(PS[b][:, :Tn], [("ps", b)])], Tn, 128, None)
                            if which == 0:
                                stt(qT[:, h, :Tn], PS[b][:, :Tn], vcol[:, 82:83], rstd[:, rs, :Tn], ALU.mult, ALU.mult,
                                    r=[("ps", b), "vcol", ("rstd", rs)], w=[("qT", h)])
                            else:
                                ks = nxt("p", 2)
                                kk = TBK(ks)
                                stt(TB(ks), PS[b][:, :Tn], vcol[:, 81:82], rstd[:, rs, :Tn], ALU.mult, ALU.mult,
                                    r=[("ps", b), "vcol", ("rstd", rs)], w=kk)
                                if prompt:
                                    cp(KTb[:, h, ti * TP:(ti + 1) * TP], TB(ks), kk, [("KT", h, ti)], eng="act")
                                else:
                                    cp(kTs[:, h, :], TB(ks), kk, [("kTs", h)], eng="act")
                                b2 = bankA()
                                for tb in range(nb):
                                    tr(b2, PS[b2][0:bw, tb * 128:(tb + 1) * 128], tmpB[:, ks, tb * 128:tb * 128 + bw], ident[:], r=kk + ["ident"])
                                kslot = h % 2
                                cp(kst[0:bw, kslot, 0:nb, :], PS[b2][0:bw, 0:nb * 128].rearrange("p (t x) -> p t x", x=128),
                                   [("ps", b2)], [("kst", kslot)], eng="act")
                                if prompt:
                                    dma("sp", kp[l, ti * TP:(ti + 1) * TP, h * 128:(h + 1) * 128].rearrange("(t p) x -> p t x", p=128),
                                        kst[:, kslot, :, :], r=[("kst", kslot)], w=[])
                                else:
                                    dma("sp", kso[l, :, h * 128:(h + 1) * 128], kst[0:TS, kslot, 0, :], r=[("kst", kslot)], w=[])
                for vb_i in range(2):
                    s, wv = wload(("v", vb_i), wview(w_in[l], 2048 + vb_i * 512, 512), KT, 512)
                    vblocks = [(tb * 128, 128, tb) for tb in range(nb)] if prompt else [(0, 16, 0), (16, 16, 1)]
                    for (c0, n, tb) in vblocks:
                        b = bankA()
                        for kt in range(KT):
                            mm(b, PS[b][0:n, :], aT[:, kt, c0:c0 + n], wv[:, kt, :], kt == 0, kt == KT - 1, r=WK(s) + [("aT", kt)])
                        vs_ = nxt("p", 2)
                        cp(vst[0:n, vs_, :], PS[b][0:n, :], [("ps", b)], [("vst", vs_)], eng="act")
                        if prompt:
                            cp(Vb[:, ti * NBT + tb, vb_i * 512:(vb_i + 1) * 512], vst[:, vs_, :], [("vst", vs_)], [("V", ti)])
                            r0 = ti * TP + c0
                            dma("sp", vp[l, r0:r0 + 128, vb_i * 512:(vb_i + 1) * 512], vst[:, vs_, :], r=[("vst", vs_)], w=[])
                        else:
                            cp(vbs[:, tb, vb_i * 512:(vb_i + 1) * 512], vst[0:16, vs_, :], [("vst", vs_)], ["vbs"])
                            dma("sp", vso[l, c0:c0 + 16, vb_i * 512:(vb_i + 1) * 512], vst[0:16, vs_, :], r=[("vst", vs_)], w=[])

            def part_s5():
                tsrc = tau if prompt else taus
                tabs = [(1, 2), (9, 10)]

                def tables(gp):
                    sn_i, cs_i = tabs[gp % 2]
                    if not cur["first"] and not os.environ.get("NOTABC"):
                        if prompt:
                            dma("sp", tmp[:, sn_i:sn_i + 2, :], tabsc[gp], r=[("tabsc", gp)], w=TK(sn_i, cs_i))
                        else:
                            for c0 in (0, 16):
                                dma("sp", tmp[:, sn_i:sn_i + 2, c0:c0 + 16], tabsc[gp, :, :, 0:16], r=[("tabsc", gp)], w=TK(sn_i, cs_i))
                        return
                    act(T(0), tsrc[:, :Tn], AF.Identity, ["tau", "thn"], TK(0), scale=thn[:, gp:gp + 1])
                    cp(tmpi[:, :Tn], T(0), TK(0), ["tmpi"])
                    tt(T(0), T(0), tmpi[:, :Tn], ALU.subtract, TK(0) + ["tmpi"], TK(0))
                    act(T(sn_i), T(0), AF.Sin, TK(0), TK(sn_i), scale=TWO_PI)
                    act(T(cs_i), T(0), AF.Sin, TK(0), TK(cs_i), scale=math.pi)
                    act(T(cs_i), T(cs_i), AF.Square, TK(cs_i), TK(cs_i))
                    act(T(cs_i), T(cs_i), AF.Identity, TK(cs_i) + ["onec"], TK(cs_i), bias=onec[:, 0:1], scale=-2.0)
                    if cur["first"] and not os.environ.get("NOTABC"):
                        dma("sp", tabsc[gp], tmp[:, sn_i:sn_i + 2, :], r=TK(sn_i, cs_i), w=[("tabsc", gp)])

                def bu(gp):
                    quad, j = gp // 4, gp % 4
                    bk = rot["banks"][gp % 2]
                    mm(bk, PS[bk][:, 0:Tn], Bc[32 * j:32 * j + 32, quad, 0, :], ub[32 * j:32 * j + 32, quad, :Tn], True, True,
                       r=["Bc", ("ub", quad)], tp=(32 * j, 0))
                    mm(bk, PS[bk][:, TP:TP + Tn], Bc[32 * j:32 * j + 32, quad, 1, :], ub[32 * j:32 * j + 32, quad, :Tn], True, True,
                       r=["Bc", ("ub", quad)], tp=(32 * j, 0))

                def body(gp):
                    quad, j = gp // 4, gp % 4
                    sn_i, cs_i = tabs[gp % 2]
                    bk = rot["banks"][gp % 2]
                    bre = PS[bk][:, 0:Tn]
                    bim = PS[bk][:, TP:TP + Tn]
                    BK = [("ps", bk)]
                    SK = ["smi16"]
                    inis = [(c0, n, slot, winit[:, slot, 0, gp:gp + 1], winit[:, slot, 1, gp:gp + 1]) for (c0, n, slot) in segs]
                    tt(T(3), T(cs_i), bre, ALU.mult, TK(cs_i) + BK, TK(3))
                    tt(T(5), T(sn_i), bim, ALU.mult, TK(sn_i) + BK, TK(5))
                    tt(T(4), T(cs_i), bim, ALU.mult, TK(cs_i) + BK, TK(4))
                    tt(T(8), T(sn_i), bre, ALU.mult, TK(sn_i) + BK, TK(8))
                    tt(T(3), T(3), T(5), ALU.add, TK(3, 5), TK(3))
                    tt(T(4), T(4), T(8), ALU.subtract, TK(4, 8), TK(4))
                    for (c0, n, slot, i0, i1) in inis:
                        for (dst, srcw, ini) in ((6, 3, i0), (7, 4, i1)):
                            A("dve", lambda e, dst=dst, srcw=srcw, ini=ini, c0=c0, n=n, gp=gp: e.tensor_tensor_scan(
                                out=tmp[:, dst, c0:c0 + n], data0=magT[:, gp:gp + 1].to_broadcast([128, n]),
                                data1=tmp[:, srcw, c0:c0 + n], initial=ini, op0=ALU.mult, op1=ALU.add),
                              r=TK(srcw) + ["winit", "magT"], w=TK(dst))
                    tt(T(8), T(cs_i), T(6), ALU.mult, TK(cs_i, 6), TK(8))
                    tt(T(5), T(sn_i), T(7), ALU.mult, TK(sn_i, 7), TK(5))
                    tt(T(3), T(cs_i), T(7), ALU.mult, TK(cs_i, 7), TK(3))
                    tt(T(4), T(sn_i), T(6), ALU.mult, TK(sn_i, 6), TK(4))
                    tt(T(8), T(8), T(5), ALU.subtract, TK(8, 5), TK(8))
                    tt(T(3), T(3), T(4), ALU.add, TK(3, 4), TK(3))
                    cp(xrb[:, 0, :Tn], T(8), TK(8), ["xrb0"], eng="act")
                    cp(xrb[:, 1, :Tn], T(3), TK(3), ["xrb1"], eng="act")
                    for (c0, n, slot, i0, i1) in inis:
                        cp(Xst[:, slot, 0, gp:gp + 1], tmp[:, 8, c0 + n - 1:c0 + n], TK(8), ["Xst"])
                        cp(Xst[:, slot, 1, gp:gp + 1], tmp[:, 3, c0 + n - 1:c0 + n], TK(3), ["Xst"])
                    by = 5
                    mm(by, PS[by][32 * j:32 * j + 32, :Tn], Cc[:, gp, 0, :], xrb[:, 0, :Tn], True, False, r=["Cc", "xrb0"], tp=(0, 32 * j))
                    mm(by, PS[by][32 * j:32 * j + 32, :Tn], Cc[:, gp, 1, :], xrb[:, 1, :Tn], False, True, r=["Cc", "xrb1"], tp=(0, 32 * j))
                    if j == 3:
                        stt(uT[:, quad, :Tn], uT[:, quad, :Tn], vcol[:, 64 + quad:65 + quad], PS[by][:, :Tn], ALU.mult, ALU.add,
                            r=[("uT", quad), "vcol", ("ps", by)], w=[("uT", quad)])
                        act(uT[:, quad, :Tn], uT[:, quad, :Tn], AF.Gelu_apprx_tanh, [("uT", quad)], [("uT", quad)])
                        cp(ub[:, quad, :Tn], uT[:, quad, :Tn], [("uT", quad)], [("ub", quad)])

                for (c0, n, slot) in segs:
                    xr_ = Xst[:, slot, 0, :]
                    xi_ = Xst[:, slot, 1, :]
                    wr0 = winit[:, slot, 0, :]
                    wi0 = winit[:, slot, 1, :]
                    WI = ["winit"]
                    tt(wr0, cth[:], xr_, ALU.mult, ["cth", "Xst"], WI)
                    tt(sm[:, 18, :], sth[:], xi_, ALU.mult, ["sth", "Xst"], ["smi18"])
                    tt(wi0, cth[:], xi_, ALU.mult, ["cth", "Xst"], WI)
                    tt(sm[:, 19, :], sth[:], xr_, ALU.mult, ["sth", "Xst"], ["smi19"])
                    tt(wr0, wr0, sm[:, 18, :], ALU.subtract, WI + ["smi18"], WI)
                    tt(wi0, wi0, sm[:, 19, :], ALU.add, WI + ["smi19"], WI)
                tables(0)
                bu(0)
                for gp in range(32):
                    if gp + 1 < 32:
                        tables(gp + 1)
                        bu(gp + 1)
                    body(gp)
            def part_glu():
                for nb2 in range(2):
                    s, wv = wload(("glu", nb2), wview(w_glu[l], nb2 * 512, 512), 8, 512)
                    for j in range(4):
                        n = nb2 * 4 + j
                        b = bankA()
                        for kq in range(8):
                            mm(b, PS[b][:, :Tn], wv[:, kq, j * 128:(j + 1) * 128], ub[:, kq, :Tn], kq == 0, kq == 7, r=WK(s) + [("ub", kq)])
                        act(T(0), PS[b][:, :Tn], AF.Sigmoid, [("ps", b), "vcol"], TK(0), bias=vcol[:, 72 + n:73 + n], scale=1.0)
                        tt(T(9), uT[:, n, :Tn], T(0), ALU.mult, [("uT", n)] + TK(0), TK(9))
                        s2 = nxt("s", 2)
                        act(sqb[:, s2, :Tn], T(9), AF.Square, TK(9), [("sqb", s2)])
                        mm(5, PS[5][:, :Tn], ones_bf[:], sqb[:, s2, :Tn], n == 0, n == 7, r=["ones_bf", ("sqb", s2)])
                        ts(mT(8 + n, Tn), T(9), vcol[:, 56 + n:57 + n], 0.0, ALU.mult, ALU.add, TK(9) + ["vcol"], [("aT", 8 + n)])
                rs_s = st.setdefault('x', {}).__setitem__('s', nxt("t", 3)) or st['x']['s']
                rstd_from(PS[5][:, :Tn], [("ps", 5)], rs_s, Tn, 1024)

            def part_attn():
                if prompt:
                    nkb = NBT * ti + NBT
                    for h in range(H):
                        bo = 6
                        br_ = 7
                        for kb in range(nkb):
                            diag = kb >= NBT * ti
                            qs = (kb - NBT * ti) * 128 if diag else 0
                            bs = bankA()
                            ktile = kb // NBT
                            mm(bs, PS[bs][:, qs:TP], KTb[:, h, kb * 128:(kb + 1) * 128], qT[:, h, qs:TP], True, False, r=[("KT", h, ktile), ("qT", h)])
                            mm(bs, PS[bs][:, qs:TP], sel2[:, h, :], chl[:, qs:TP], False, not diag, r=["sel2", "chl"])
                            if diag:
                                mm(bs, PS[bs][:, qs:qs + 128], identb[:], negmask[:], False, True, r=["identb", "negmask"])
                            pi = nxt("p", 3)
                            act(Pt[:, pi, qs:TP], PS[bs][:, qs:TP], AF.Exp, [("ps", bs), ("negc", ktile)], [("Pt", pi)], bias=negc[:, kb, h:h + 1], scale=1.0)
                            mm(bo, PS[bo][:, qs:TP], Vb[:, kb, h * 128:(h + 1) * 128], Pt[:, pi, qs:TP], kb == 0, kb == nkb - 1, r=[("V", ktile), ("Pt", pi)])
                            mm(br_, PS[br_][:, qs:TP], ones_bf[:], Pt[:, pi, qs:TP], kb == 0, kb == nkb - 1, r=["ones_bf", ("Pt", pi)])
                        attn_epilogue(l, h, Tn, bo, br_)
                else:
                    sample_attention(l)
                rs_a = st.setdefault('x', {}).__setitem__('a', nxt("t", 3)) or st['x']['a']
                rstd_from(PS[4][:, :Tn], [("ps", 4)], rs_a, Tn, 1024)

            if prompt and not os.environ.get("NOFORK"):
                main_l = pr.cur
                lb = []
                pr.cur = lb
                rot["banks"] = [2, 3]
                rot["sbanks"] = [4]
                part_qkv()
                part_attn()
                la = []
                pr.cur = la
                rot["banks"] = [0, 1]
                part_s5()
                rot["banks"] = [0, 1, 2, 3]
                rot["sbanks"] = [4, 5]
                pr.cur = main_l
                na, nb_ = len(la), len(lb)
                ia = ib = 0
                while ia < na or ib < nb_:
                    if ib >= nb_ or (ia < na and ia * nb_ <= ib * na):
                        main_l.append(la[ia]); ia += 1
                    else:
                        main_l.append(lb[ib]); ib += 1
                part_glu()
            else:
                part_qkv()
                part_s5()
                part_glu()
                part_attn()
            rs_a = st['x']['a']
            rs_s = st['x']['s']
            for nb4 in range(4):
                s, wv = wload(("wo", nb4), wview(w_out[l], nb4 * 512, 512), KT, 512)
                for j in range(4):
                    n = nb4 * 4 + j
                    b1 = bankA()
                    for kt in range(8):
                        mm(b1, PS[b1][:, :Tn], wv[:, kt, j * 128:(j + 1) * 128], mT(kt, Tn), kt == 0, kt == 7, r=WK(s) + [("aT", kt)])
                    b2 = bankA()
                    for kt in range(8, 16):
                        mm(b2, PS[b2][:, :Tn], wv[:, kt, j * 128:(j + 1) * 128], mT(kt, Tn), kt == 8, kt == 15, r=WK(s) + [("aT", kt)])
                    tt(T(0), PS[b1][:, :Tn], rstd[:, rs_a, :Tn], ALU.mult, [("ps", b1), ("rstd", rs_a)], TK(0))
                    tt(xT[:, n, :Tn], xT[:, n, :Tn], T(0), ALU.add, [("xT", n)] + TK(0), [("xT", n)])
                    tt(T(1), PS[b2][:, :Tn], rstd[:, rs_s, :Tn], ALU.mult, [("ps", b2), ("rstd", rs_s)], TK(1))
                    tt(xT[:, n, :Tn], xT[:, n, :Tn], T(1), ALU.add, [("xT", n)] + TK(1), [("xT", n)])

            if UPTO < 9:
                return
            norm_x(l, Tn, 16)
            akey = lambda jj: ("ub", jj) if jj < 8 else ("qT", jj - 8) if jj < 16 else ("Pt", jj - 16) if jj < 19 else "xrb0" if jj == 19 else "xrb1" if jj == 20 else "act21"
            for half in range(2):
                for blk in range(11):
                    c0 = (half * 11 + blk) * 256
                    gsrc = wview(w_gate[l], c0, 256)
                    usrc = wview(w_up[l], c0, 256)
                    s = wblock(("gu", half, blk), [(0, 8, 256, gsrc[:, 0:8, :]), (2048, 8, 256, gsrc[:, 8:16, :]),
                                                   (4096, 8, 256, usrc[:, 0:8, :]), (6144, 8, 256, usrc[:, 8:16, :])])
                    gv = wsl[s][:, 0:4096].rearrange("p (k n) -> p k n", n=256)
                    uv = wsl[s][:, 4096:8192].rearrange("p (k n) -> p k n", n=256)
                    for j in range(2):
                        jj = blk * 2 + j
                        bg = bankA()
                        for kt in range(KT):
                            mm(bg, PS[bg][:, :Tn], gv[:, kt, j * 128:(j + 1) * 128], aT[:, kt, :Tn], kt == 0, kt == KT - 1, r=WK(s) + [("aT", kt)])
                        bu_ = bankA()
                        for kt in range(KT):
                            mm(bu_, PS[bu_][:, :Tn], uv[:, kt, j * 128:(j + 1) * 128], aT[:, kt, :Tn], kt == 0, kt == KT - 1, r=WK(s) + [("aT", kt)])
                        t_ = nxt("p", 2)
                        act(T(t_), PS[bg][:, :Tn], AF.Silu, [("ps", bg)], TK(t_))
                        tt(actT[:, jj, :Tn], T(t_), PS[bu_][:, :Tn], ALU.mult, TK(t_) + [("ps", bu_)], [akey(jj)])
                for np_ in range(8):
                    src = w_down[l].rearrange("(k p) n -> p k n", p=128)[:, half * 22:(half + 1) * 22, np_ * 256:(np_ + 1) * 256]
                    s = wblock(("dn", half, np_), [(0, 11, 256, src[:, 0:11, :]), (11 * 256, 11, 256, src[:, 11:22, :])])
                    dv = wsl[s][:, 0:22 * 256].rearrange("p (k n) -> p k n", n=256)
                    for j in range(2):
                        n = np_ * 2 + j
                        b = bankA()
                        for jj in range(22):
                            mm(b, PS[b][:, :Tn], dv[:, jj, j * 128:(j + 1) * 128], actT[:, jj, :Tn], jj == 0, jj == 21, r=WK(s) + [akey(jj)])
                        tt(xT[:, n, :Tn], xT[:, n, :Tn], PS[b][:, :Tn], ALU.add, [("xT", n), ("ps", b)], [("xT", n)])

            if UPTO < 10:
                return
            norm_x(l, Tn, 32)
            for nb4 in range(4):
                s, wv = wload(("pg", nb4), wview(w_pg[l], nb4 * 512, 512), KT, 512)
                wps = nb4 % 2
                dma("pool", wppb[:, wps, :, :], w_pp[l].rearrange("(k p) n -> p k n", p=128)[:, :, nb4 * 512:(nb4 + 1) * 512], r=[], w=[("wpp", wps)])
                for j in range(4):
                    n = nb4 * 4 + j
                    bg = bankA()
                    for kt in range(KT):
                        mm(bg, PS[bg][:, :Tn], wv[:, kt, j * 128:(j + 1) * 128], aT[:, kt, :Tn], kt == 0, kt == KT - 1, r=WK(s) + [("aT", kt)])
                    bp = bankA()
                    for k2 in range(2):
                        mm(bp, PS[bp][:, :Tn], wppb[:, wps, k2, j * 128:(j + 1) * 128], pT[:, k2, :Tn], k2 == 0, k2 == 1, r=[("wpp", wps), "pT"])
                    t_ = nxt("p", 2)
                    act(T(t_), PS[bg][:, :Tn], AF.Sigmoid, [("ps", bg)], TK(t_))
                    tt(T(t_), T(t_), PS[bp][:, :Tn], ALU.mult, TK(t_) + [("ps", bp)], TK(t_))
                    tt(xT[:, n, :Tn], xT[:, n, :Tn], T(t_), ALU.add, [("xT", n)] + TK(t_), [("xT", n)])

            if UPTO < 11:
                return
            if l == 0 and NL > 1:
                for kt in range(KT):
                    dma("sp", hres[kt, :, tok0:tok0 + Tn], xT[:, kt, :Tn], r=[("xT", kt)], w=["hres"])
            else:
                dst = yp if prompt else ys
                for tb in range(nb):
                    for g in range(4):
                        b = bankA()
                        for j in range(4):
                            kt = 4 * g + j
                            tr(b, PS[b][0:bw, j * 128:(j + 1) * 128], xT[:, kt, tb * 128:tb * 128 + bw], ident[:], r=[("xT", kt), "ident"])
                        hf = g // 2
                        cp(stg[0:bw, hf, (g % 2) * 512:(g % 2 + 1) * 512], PS[b][0:bw, :], [("ps", b)], [("stg", hf)], eng="act" if g % 2 else "dve")
                    r0 = ti * TP + tb * 128 if prompt else 0
                    for hf in range(2):
                        dma("sp", dst[r0:r0 + bw, hf * 1024:(hf + 1) * 1024], stg[0:bw, hf, :], r=[("stg", hf)], w=[])

        def mT(kt, Tn):
            return aT[:, kt, :Tn]

        onec = sb("onec", [128, 1])
        ms(onec[:], 1.0, ["onec"])
        kst = sb("kst", [128, 2, NBT, 128])
        vst = sb("vst", [128, 2, 512])
        negcs = sb("negcs", [16, 2, 8])
        negcp = sb("negcp", [128, 16, 8])
        kTs = sb("kTs", [128, H, TS], BF16)
        vbs = sb("vbs", [16, 2, 1024], BF16)
        kc = sb("kc", [128, 1, 1024], BF16)
        vc = sb("vc", [128, 1, 1024], BF16)
        kTc = sb("kTc", [128, H, 128], BF16)
        cl_tok = sb("cl_tok", [128, 16, 8])

        def attn_epilogue(l, h, Tn, bo, br_):
            act(tmpB[:, 0, :Tn], PS[br_][:, :Tn], AF.Ln, r=[("ps", br_)], w=[("tmpB", 0)])
            act(tmpB[:, 0, :Tn], tmpB[:, 0, :Tn], AF.Exp, r=[("tmpB", 0)], w=[("tmpB", 0)], scale=-1.0)
            tt(tmpB[:, 1, :Tn], PS[bo][:, :Tn], tmpB[:, 0, :Tn], ALU.mult, [("ps", bo), ("tmpB", 0)], [("tmpB", 1)])
            s2 = nxt("s", 2)
            act(sqb[:, s2, :Tn], tmpB[:, 1, :Tn], AF.Square, [("tmpB", 1)], [("sqb", s2)])
            mm(4, PS[4][:, :Tn], ones_bf[:], sqb[:, s2, :Tn], h == 0, h == H - 1, r=["ones_bf", ("sqb", s2)])
            ts(mT(h, Tn), tmpB[:, 1, :Tn], vcol[:, 48 + h:49 + h], 0.0, ALU.mult, ALU.add, [("tmpB", 1), "vcol"], [("aT", h)])

        def sample_attention(l):
            Tn = TS
            for si in range(2):
                q0 = si * 16
                for hh in range(2):
                    dma("sp", cl_tok[:, hh * 8:(hh + 1) * 8, :], clf[l, si, hh * 1024:(hh + 1) * 1024, :].rearrange("(t p) x -> p t x", p=128), r=[], w=["cl_tok"])
                cpast = tmp[0:8, 0:8, :]
                cflat = lambda a, b_: tmp[0:8, a:b_, :]
                PK = [("tmp", x) for x in range(8)]
                for g in range(4):
                    b = bankA()
                    for j in range(4):
                        kb = g * 4 + j
                        tr(b, PS[b][0:8, j * 128:(j + 1) * 128], cl_tok[:, kb, :], ident[:], r=["cl_tok", "ident"])
                    cp(tmp[0:8, 2 * g:2 * g + 2, :], PS[b][0:8, :].rearrange("p (a t) -> p a t", t=TP), [("ps", b)], PK)
                for a in range(8):
                    ini = 0.0 if a == 0 else tmp[0:8, a - 1, TP - 1:TP]
                    A("dve", lambda e, a=a, ini=ini: e.tensor_tensor_scan(
                        out=tmp[0:8, a, :], data0=onec[0:8, 0:1].to_broadcast([8, TP]), data1=tmp[0:8, a, :],
                        initial=ini, op0=ALU.mult, op1=ALU.add), r=PK + ["onec"], w=PK)
                cp(sm[0:8, 17, 0:1], tmp[0:8, 7, TP - 1:TP], PK, ["smi17"])
                ts(tmp[0:8, 0:8, :], tmp[0:8, 0:8, :], -1.0, sm[0:8, 17, 0:1], ALU.mult, ALU.add, PK + ["smi17"], PK)
                b = bankA()
                for kb in range(16):
                    tr(b, PS[b][:, kb * 8:(kb + 1) * 8], tmp[0:8, kb // 2, (kb % 2) * 128:(kb % 2 + 1) * 128], ident[0:8, 0:8], r=PK + ["ident"])
                cp(negcp[:, :, :], PS[b][:, 0:128].rearrange("p (k x) -> p k x", x=8), [("ps", b)], ["negcp"])
                bo = 6
                br_ = 7
                for kb in range(17):
                    past = kb < 16
                    np_ = 128 if past else 16
                    if past:
                        ks = nxt("w", 2)
                        kcs = 0
                        dma("pool", kc[:, kcs, :], ck[l, si, kb * 128:(kb + 1) * 128, :], r=[], w=[("kc", kcs)])
                        dma("pool", vc[:, kcs, :], cv[l, si, kb * 128:(kb + 1) * 128, :], r=[], w=[("vc", kcs)])
                        for g in range(2):
                            b = bankA()
                            for j in range(4):
                                h = g * 4 + j
                                mm(b, PS[b][:, j * 128:(j + 1) * 128], kc[:, kcs, h * 128:(h + 1) * 128], identb[:], True, True, r=[("kc", kcs), "identb"])
                            cp(kTc[:, g * 4:g * 4 + 4, :], PS[b][:, :].rearrange("p (j t) -> p j t", t=128), [("ps", b)], [("kTc", g)], eng="act" if g else "dve")
                    bs = bankA()
                    for h in range(H):
                        if past:
                            mm(bs, PS[bs][:, h * 16:(h + 1) * 16], kTc[:, h, :], qT[:, h, q0:q0 + 16], True, False, r=[("kTc", h // 4), ("qT", h)])
                            mm(bs, PS[bs][:, h * 16:(h + 1) * 16], sel2[:, h, :], chl[:, q0:q0 + 16], False, True, r=["sel2", "chl"])
                        else:
                            mm(bs, PS[bs][0:16, h * 16:(h + 1) * 16], kTs[:, h, q0:q0 + 16], qT[:, h, q0:q0 + 16], True, False, r=[("kTs", h), ("qT", h)])
                            mm(bs, PS[bs][0:16, h * 16:(h + 1) * 16], sel2[:, h, 0:16], chl[:, q0:q0 + 16], False, False, r=["sel2", "chl"])
                            mm(bs, PS[bs][0:16, h * 16:(h + 1) * 16], identb[0:16, 0:16], negmask[0:16, 0:16], False, True, r=["identb", "negmask"])
                    pi = nxt("p", 3)
                    for h in range(H):
                        bias = negcp[:, kb, h:h + 1] if past else negcs[:, si, h:h + 1]
                        act(Pt[0:np_, pi, h * 16:(h + 1) * 16], PS[bs][0:np_, h * 16:(h + 1) * 16], AF.Exp, [("ps", bs), "negcp", "negcs"], [("Pt", pi)], bias=bias, scale=1.0)
                    for h in range(H):
                        vl = vc[:, 0, h * 128:(h + 1) * 128] if past else vbs[:, si, h * 128:(h + 1) * 128]
                        mm(bo, PS[bo][:, h * 16:(h + 1) * 16], vl, Pt[0:np_, pi, h * 16:(h + 1) * 16], kb == 0, kb == 16,
                           r=[("vc", 0), "vbs", ("Pt", pi)])
                    mm(br_, PS[br_][:, 0:128], ones_bf[0:np_, :], Pt[0:np_, pi, 0:128], kb == 0, kb == 16, r=["ones_bf", ("Pt", pi)])
                A("dve", lambda e: e.reciprocal(out=tmp[:, 8, 0:128], in_=PS[br_][:, 0:128]), r=[("ps", br_)], w=[("tmp", 8)])
                tt(tmp[:, 9, 0:128], PS[bo][:, 0:128], tmp[:, 8, 0:128], ALU.mult, [("ps", bo), ("tmp", 8)], [("tmp", 9)])
                s2 = nxt("s", 2)
                act(sqb[:, s2, 0:128], tmp[:, 9, 0:128], AF.Square, [("tmp", 9)], [("sqb", s2)])
                for h in range(H):
                    mm(4, PS[4][:, q0:q0 + 16], ones_bf[:], sqb[:, s2, h * 16:(h + 1) * 16], h == 0, h == H - 1, r=["ones_bf", ("sqb", s2)])
                    ts(aT[:, h, q0:q0 + 16], tmp[:, 9, h * 16:(h + 1) * 16], vcol[:, 48 + h:49 + h], 0.0, ALU.mult, ALU.add, [("tmp", 9), "vcol"], [("aT", h)])

        for l in range(NL):
            layer_setup(l)
            for ti in range(NT):
                do_tile(l, "p", ti)
            if SAMPLE:
                do_tile(l, "s", 0)
            outs = [(0, 0, srp[l]), (0, 1, sip[l])]
            if SAMPLE:
                outs += [(1, 0, srs[l, 0]), (1, 1, sis[l, 0]), (2, 0, srs[l, 1]), (2, 1, sis[l, 1])]
            for oi, (slot, ri, dst) in enumerate(outs):
                b = bankA()
                tr(b, PS[b][0:32, 0:128], Xst[:, slot, ri, :], ident[:], r=["Xst", "ident"])
                cp(stg[0:32, 0, oi * 128:(oi + 1) * 128], PS[b][0:32, 0:128], [("ps", b)], [("stg", 0)])
                dma("sp", dst, stg[0:32, 0, oi * 128:(oi + 1) * 128], r=[("stg", 0)], w=[])
        print('SBUF remaining', nc.sbuf_bytes_remaining)
        pr.finalize(es)
    return nc


_CACHE = {}
CFG = {}


def kernel(**inp):
    f = lambda a: np.ascontiguousarray(a, dtype=np.float32)
    nc = _CACHE.get("nc")
    if nc is None:
        nc = build(**{k: v for k, v in CFG.items() if k != 'NCORES'})
        _CACHE["nc"] = nc
    wnames = ["g_mix", "w_in", "b_f", "g_q", "g_k", "a_re", "a_im", "log_dt", "b_re", "b_im", "c_re", "c_im", "d_skip",
              "w_glu", "b_glu", "g_attn_out", "g_ssm_out", "w_out", "g_ffn", "w_gate", "w_up", "w_down", "g_ple",
              "w_ple_gate", "w_ple_proj"]
    shared = {n: f(inp[n]) for n in wnames}
    wi = shared["w_in"]
    shared["w_in"] = np.ascontiguousarray(np.concatenate([wi[..., :3072], wi[..., 3080:], wi[..., 3072:3080]], axis=-1))
    in_maps = []
    NCO = CFG.get('NCORES', 8)
    for c in range(NCO):
        m = dict(shared)
        m["xp"] = f(inp["x_prompt"][c])
        m["xs"] = f(inp["x_sample"][2 * c:2 * c + 2].reshape(TS, D))
        m["ck"] = f(inp["cache_k"][:, 2 * c:2 * c + 2].reshape(2, 2, SEQ, 1024))
        m["cv"] = f(inp["cache_v"][:, 2 * c:2 * c + 2].reshape(2, 2, SEQ, 1024))
        m["clf"] = f(inp["cache_logf"][:, 2 * c:2 * c + 2])
        m["sre"] = f(inp["state_ssm_re"][:, 2 * c:2 * c + 2])
        m["sim"] = f(inp["state_ssm_im"][:, 2 * c:2 * c + 2])
        m["pp"] = f(inp["p_prompt"][:, c])
        m["psm"] = f(inp["p_sample"][:, 2 * c:2 * c + 2].reshape(2, TS, 256))
        in_maps.append(m)
    res = run_bass_kernel_spmd(nc, in_maps, core_ids=list(range(NCO)))
    R = list(res.results)
    while len(R) < 8:
        R.append(R[0])
    cat = lambda k: np.stack([r[k] for r in R])
    y_prompt = cat("yp")
    y_sample = cat("ys").reshape(16, 16, D)
    k_prompt = cat("kp").transpose(1, 0, 2, 3).reshape(2, 8, SEQ, 8, 128)
    v_prompt = cat("vp").transpose(1, 0, 2, 3).reshape(2, 8, SEQ, 8, 128)
    logf_prompt = cat("lfp").transpose(1, 0, 2, 3)
    ssm_re_prompt = cat("srp").transpose(1, 0, 2, 3).reshape(2, 8, 64, 64)
    ssm_im_prompt = cat("sip").transpose(1, 0, 2, 3).reshape(2, 8, 64, 64)
    k_sample = cat("kso").transpose(1, 0, 2, 3).reshape(2, 16, 16, 8, 128)
    v_sample = cat("vso").transpose(1, 0, 2, 3).reshape(2, 16, 16, 8, 128)
    logf_sample = cat("lfs").transpose(1, 0, 2, 3).reshape(2, 16, 16, 8)
    ssm_re_sample = cat("srs").transpose(1, 0, 2, 3, 4).reshape(2, 16, 64, 64)
    ssm_im_sample = cat("sis").transpose(1, 0, 2, 3, 4).reshape(2, 16, 64, 64)
    return (y_prompt, y_sample, k_prompt, v_prompt, logf_prompt, ssm_re_prompt, ssm_im_prompt,
            k_sample, v_sample, logf_sample, ssm_re_sample, ssm_im_sample)
```

```python
import math
import os
import numpy as np
from contextlib import ExitStack
import concourse.bass as bass
import concourse.mybir as mybir
from concourse.bass_utils import run_bass_kernel_spmd

F32 = mybir.dt.float32
BF16 = mybir.dt.bfloat16
I32 = mybir.dt.int32
AF = mybir.ActivationFunctionType
ALU = mybir.AluOpType

D = 2048
KT = 16
H = 8
DFF = 5632
FT = 44
INC = 4104
SEQ = 2048
TP = 256
NBT = TP // 128
TS = 32
EPS = 1e-6
TWO_PI = 2.0 * math.pi
NRING = 8


class Op:
    __slots__ = ("eng", "fn", "deps", "dma", "stream", "seq", "waits", "mark", "val", "snap")

    def __init__(self, eng, fn, deps, dma):
        self.eng = eng
        self.fn = fn
        self.deps = deps
        self.dma = dma
        self.mark = False
        self.waits = []


class Prog:
    def __init__(self, nc):
        self.nc = nc
        self.ops = []
        self.lastw = {}
        self.readers = {}
        self.cur = []

    def add(self, eng, fn, r=(), w=(), dma=False):
        self.cur.append((eng, fn, tuple(r), tuple(w), dma))

    def _resolve(self):
        for (eng, fn, r, w, dma) in self.cur:
            idx = len(self.ops)
            deps = set()
            for k in r:
                lw = self.lastw.get(k)
                if lw is not None:
                    deps.add(lw)
            for k in w:
                lw = self.lastw.get(k)
                if lw is not None:
                    deps.add(lw)
                rd = self.readers.get(k)
                if rd:
                    deps.update(rd.values())
            sk = (eng, idx) if dma else eng
            for k in r:
                self.readers.setdefault(k, {})[sk] = idx
            for k in w:
                self.lastw[k] = idx
                self.readers[k] = {}
            self.ops.append(Op(eng, fn, deps, dma))

    def finalize(self, es):
        nc = self.nc
        self._resolve()
        ops = self.ops
        engs = ["pe", "act", "dve", "pool", "sp"]
        cnt = {e: 0 for e in engs}
        dcnt = {e: 0 for e in engs}
        for o in ops:
            if o.dma:
                n = dcnt[o.eng]
                dcnt[o.eng] += 1
                o.stream = (o.eng, n % NRING)
                o.seq = n // NRING + 1
            else:
                cnt[o.eng] += 1
                o.stream = o.eng
                o.seq = cnt[o.eng]
        clock = {e: {} for e in engs}
        for o in ops:
            ck = clock[o.eng]
            need = {}
            if o.dma and o.seq > 1:
                need[o.stream] = (o.seq - 1, None)
            for j in o.deps:
                p = ops[j]
                if p.stream == "pe" and o.eng == "pe" and not o.dma:
                    continue
                cur = need.get(p.stream)
                if cur is None or cur[0] < p.seq:
                    need[p.stream] = (p.seq, j)
            for s, (q, j) in need.items():
                if ck.get(s, 0) >= q:
                    continue
                o.waits.append((s, q, j))
                ck[s] = q
                if j is not None:
                    ops[j].mark = True
                    for s2, q2 in ops[j].snap.items():
                        if ck.get(s2, 0) < q2:
                            ck[s2] = q2
            o.snap = dict(ck)
        rank = {e: 0 for e in engs}
        for o in ops:
            if o.dma:
                o.val = 16 * o.seq
            elif o.mark:
                rank[o.eng] += 1
                o.val = rank[o.eng]
        sems = {}
        for e in ["pe", "act", "dve", "pool"]:
            sems[e] = es.enter_context(nc.semaphore("c_" + e))
        for e in ["pool", "sp"]:
            for i in range(NRING):
                sems[(e, i)] = es.enter_context(nc.semaphore("d_%s%d" % (e, i)))
        streams = {e: [o for o in ops if o.eng == e] for e in engs}
        final = {}
        for o in ops:
            if o.dma:
                final[o.stream] = max(final.get(o.stream, 0), o.val)

        def emit(e, eng):
            for o in streams[e]:
                for (s, q, j) in o.waits:
                    if j is not None:
                        v = ops[j].val
                    else:
                        v = 16 * q
                    eng.wait_ge(sems[s], v)
                ins = o.fn(eng)
                if o.dma:
                    ins.then_inc(sems[o.stream], 16)
                elif o.mark:
                    ins.then_inc(sems[o.eng], 1)
            if e == "sp":
                for s, v in final.items():
                    eng.wait_ge(sems[s], v)

        block = es.enter_context(nc.Block())

        @block.tensor
        def _(e):
            emit("pe", e)

        @block.scalar
        def _(e):
            emit("act", e)

        @block.vector
        def _(e):
            emit("dve", e)

        @block.gpsimd
        def _(e):
            emit("pool", e)

        @block.sync
        def _(e):
            emit("sp", e)


def build(NL=2, NT=SEQ // TP, SAMPLE=True, UPTO=99):
    nc = bass.Bass("TRN2", target_bir_lowering=False)

    def din(name, shape):
        return nc.dram_tensor(name, list(shape), F32, kind="ExternalInput").ap()

    def dout(name, shape):
        return nc.dram_tensor(name, list(shape), F32, kind="ExternalOutput").ap()

    xp = din("xp", [SEQ, D])
    xs = din("xs", [TS, D])
    ck = din("ck", [2, 2, SEQ, 1024])
    cv = din("cv", [2, 2, SEQ, 1024])
    clf = din("clf", [2, 2, SEQ, 8])
    sre = din("sre", [2, 2, 64, 64])
    sim = din("sim", [2, 2, 64, 64])
    pp = din("pp", [2, SEQ, 256])
    psm = din("psm", [2, TS, 256])
    g_mix = din("g_mix", [2, D])
    w_in = din("w_in", [2, D, INC])
    b_f = din("b_f", [2, 8])
    g_q = din("g_q", [2, 128])
    g_k = din("g_k", [2, 128])
    a_re = din("a_re", [2, 64, 64])
    a_im = din("a_im", [2, 64, 64])
    log_dt = din("log_dt", [2, 64])
    b_re = din("b_re", [2, 64, 64, 16])
    b_im = din("b_im", [2, 64, 64, 16])
    c_re = din("c_re", [2, 64, 16, 64])
    c_im = din("c_im", [2, 64, 16, 64])
    d_skip = din("d_skip", [2, 1024])
    w_glu = din("w_glu", [2, 1024, 1024])
    b_glu = din("b_glu", [2, 1024])
    g_ao = din("g_attn_out", [2, 1024])
    g_so = din("g_ssm_out", [2, 1024])
    w_out = din("w_out", [2, D, D])
    g_ffn = din("g_ffn", [2, D])
    w_gate = din("w_gate", [2, D, DFF])
    w_up = din("w_up", [2, D, DFF])
    w_down = din("w_down", [2, DFF, D])
    g_ple = din("g_ple", [2, D])
    w_pg = din("w_ple_gate", [2, D, D])
    w_pp = din("w_ple_proj", [2, 256, D])

    yp = dout("yp", [SEQ, D])
    ys = dout("ys", [TS, D])
    kp = dout("kp", [2, SEQ, 1024])
    vp = dout("vp", [2, SEQ, 1024])
    lfp = dout("lfp", [2, SEQ, 8])
    srp = dout("srp", [2, 32, 128])
    sip = dout("sip", [2, 32, 128])
    kso = dout("kso", [2, TS, 1024])
    vso = dout("vso", [2, TS, 1024])
    lfs = dout("lfs", [2, TS, 8])
    srs = dout("srs", [2, 2, 32, 128])
    sis = dout("sis", [2, 2, 32, 128])
    hres = nc.dram_tensor("hres", [KT, 128, SEQ + TS], F32, kind="Internal").ap()
    wsc = nc.dram_tensor("wsc", [2, 56, 128, 8192], BF16, kind="Internal").ap()
    tabsc = nc.dram_tensor("tabsc", [32, 128, 2, TP], F32, kind="Internal").ap()

    es = ExitStack()
    with es:
        def sb(name, shape, dt=F32):
            return es.enter_context(nc.sbuf_tensor(name, list(shape), dt))

        pr = Prog(nc)
        A = pr.add

        ident = sb("ident", [128, 128])
        ones_bf = sb("ones_bf", [128, 128], BF16)
        negmask = sb("negmask", [128, 128], BF16)
        identb = sb("identb", [128, 128], BF16)
        sel2 = sb("sel2", [40, 8, 128], BF16)
        tau = sb("tau", [128, TP])
        taus = sb("taus", [128, TS])
        KTb = sb("KTb", [128, H, SEQ], BF16)
        Vb = sb("Vb", [128, 16, 1024], BF16)
        negc = sb("negc", [128, 16, 8])
        vcol = sb("vcol", [128, 96])
        bfc = sb("bfc", [40, 1])
        wf2 = sb("wf2", [128, KT, 40], BF16)
        wppb = sb("wppb", [128, 2, 2, 512], BF16)
        magT = sb("magT", [128, 32])
        thn = sb("thn", [128, 32])
        cth = sb("cth", [128, 32])
        sth = sb("sth", [128, 32])
        winit = sb("winit", [128, 3, 2, 32])
        Bc = sb("Bc", [128, 8, 2, 128], BF16)
        Cc = sb("Cc", [128, 32, 2, 32], BF16)
        Xst = sb("Xst", [128, 3, 2, 32])
        ccar = sb("ccar", [40, 3])
        sm = sb("sm", [128, 24, 32])
        smi = sb("smi", [128, 32], I32)
        xT = sb("xT", [128, KT, TP])
        aT = sb("aT", [128, KT, TP], BF16)
        wsl = [sb("wsl%d" % i, [128, 8192], BF16) for i in range(2)]
        rstd = sb("rstd", [128, 3, TP])
        sqb = sb("sqb", [128, 2, TP], BF16)
        reg = sb("reg", [128, 8 * TP])
        uT = reg[:, :].rearrange("p (a t) -> p a t", t=TP)
        regb = sb("regb", [128, 22 * TP], BF16)
        ub = regb[:, 0:8 * TP].rearrange("p (a t) -> p a t", t=TP)
        qT = regb[:, 8 * TP:16 * TP].rearrange("p (a t) -> p a t", t=TP)
        Pt = regb[:, 16 * TP:19 * TP].rearrange("p (a t) -> p a t", t=TP)
        xrb = regb[:, 19 * TP:21 * TP].rearrange("p (a t) -> p a t", t=TP)
        actT = regb[:, :].rearrange("p (a t) -> p a t", t=TP)
        tmp = sb("tmp", [128, 11, TP])
        tmpi = sb("tmpi", [128, TP], I32)
        tmpB = sb("tmpB", [128, 2, TP])
        chl = sb("chl", [40, TP], BF16)
        stg = sb("stg", [128, 2, 1024])
        pT = sb("pT", [128, 2, TP], BF16)
        PS = [es.enter_context(nc.psum_tensor("ps%d" % i, [128, 512], F32)) for i in range(8)]

        rot = {"a": 0, "w": 0, "r": 0, "t": 0, "p": 0, "s": 0, "banks": [0, 1, 2, 3], "sbanks": [4, 5]}

        def bankA():
            lst = rot["banks"]
            rot["a"] = (rot["a"] + 1) % len(lst)
            return lst[rot["a"]]

        def nxt(name, n):
            rot[name] = (rot[name] + 1) % n
            return rot[name]

        def mm(b, out, lhsT, rhs, start, stop, r, tp=None, w=None):
            kw = {}
            if tp is not None:
                kw["tile_position"] = tp
            A("pe", lambda e: e.matmul(out, lhsT=lhsT, rhs=rhs, start=start, stop=stop, **kw),
              r=r, w=[("ps", b)] if w is None else w)

        def tr(b, out, in_, idn, r):
            A("pe", lambda e: e.transpose(out=out, in_=in_, identity=idn), r=r, w=[("ps", b)])

        def act(out, in_, func, r, w, bias=None, scale=None):
            kw = {}
            if bias is not None:
                kw["bias"] = bias
            if scale is not None:
                kw["scale"] = scale
            A("act", lambda e: e.activation(out=out, in_=in_, func=func, **kw), r=r, w=w)

        def tt(out, in0, in1, op, r, w, eng="dve"):
            A(eng, lambda e: e.tensor_tensor(out=out, in0=in0, in1=in1, op=op), r=r, w=w)

        def ts(out, in0, s1, s2, op0, op1, r, w, eng="dve"):
            A(eng, lambda e: e.tensor_scalar(out=out, in0=in0, scalar1=s1, scalar2=s2, op0=op0, op1=op1), r=r, w=w)

        def stt(out, in0, scalar, in1, op0, op1, r, w, eng="dve"):
            A(eng, lambda e: e.scalar_tensor_tensor(out=out, in0=in0, scalar=scalar, in1=in1, op0=op0, op1=op1), r=r, w=w)

        def cp(out, in_, r, w, eng="dve"):
            if eng == "act":
                A(eng, lambda e: e.activation(out=out, in_=in_, func=AF.Copy), r=r, w=w)
            else:
                A(eng, lambda e: e.tensor_copy(out=out, in_=in_), r=r, w=w)

        def ms(ap, val, w, eng="dve"):
            A(eng, lambda e: e.memset(ap, val), r=[], w=w)

        def dma(q, out, in_, r, w):
            A(q, lambda e: e.dma_start(out=out, in_=in_), r=r, w=w, dma=True)

        ms(ident[:], 0.0, ["ident"], eng="pool")
        A("pool", lambda e: e.affine_select(out=ident[:], in_=ident[:], compare_op=ALU.not_equal, fill=1.0,
                                            base=0, pattern=[[-1, 128]], channel_multiplier=1),
          r=["ident"], w=["ident"])
        ms(ones_bf[:], 1.0, ["ones_bf"])
        cp(identb[:], ident[:], ["ident"], ["identb"])
        ms(negmask[:], 0.0, ["negmask"], eng="pool")
        A("pool", lambda e: e.affine_select(out=negmask[:], in_=negmask[:], compare_op=ALU.is_ge, fill=-30000.0,
                                            base=0, pattern=[[1, 128]], channel_multiplier=-1),
          r=["negmask"], w=["negmask"])
        A("pool", lambda e: e.iota(tmpi[:], pattern=[[1, TP]], base=0, channel_multiplier=0), r=[], w=["tmpi"])
        cp(tau[:], tmpi[:], ["tmpi"], ["tau"])
        cp(taus[:, 0:16], tmpi[:, 0:16], ["tmpi"], ["tau"])
        cp(taus[:, 16:32], tmpi[:, 0:16], ["tmpi"], ["tau"])
        ms(sel2[:], 0.0, ["sel2"])
        for h in range(H):
            cp(sel2[0:8, h, :], ident[0:8, h:h + 1].to_broadcast([8, 128]), ["ident", "sel2"], ["sel2"])
            cp(sel2[32:40, h, :], ident[32:40, 32 + h:33 + h].to_broadcast([8, 128]), ["ident", "sel2"], ["sel2"])
        ms(Xst[:], 0.0, ["Xst"])
        ms(ccar[:], 0.0, ["ccar"])

        def rstd_from(src, skeys, rs, Tn, nfeat):
            act(rstd[:, rs, :Tn], src, AF.Ln, r=skeys + ["epsc"], w=[("rstd", rs)], bias=epsc[:, 0:1], scale=1.0 / nfeat)
            act(rstd[:, rs, :Tn], rstd[:, rs, :Tn], AF.Exp, r=[("rstd", rs)], w=[("rstd", rs)], scale=-0.5)

        def rms_from_sq(sq_list, Tn, nfeat, rkeys):
            sbl = rot["sbanks"]
            rot["r"] = (rot["r"] + 1) % len(sbl)
            b = sbl[rot["r"]]
            n = len(sq_list)
            for i, (ap, keys) in enumerate(sq_list):
                s = nxt("s", 2)
                act(sqb[:, s, :Tn], ap, AF.Square, r=keys, w=[("sqb", s)])
                mm(b, PS[b][:, :Tn], ones_bf[:], sqb[:, s, :Tn], i == 0, i == n - 1, r=["ones_bf", ("sqb", s)])
            rs = nxt("t", 3)
            rstd_from(PS[b][:, :Tn], [("ps", b)], rs, Tn, nfeat)
            return rs

        epsc = sb("epsc", [128, 1])
        ms(epsc[:], EPS, ["epsc"])

        def norm_x(l, Tn, gofs):
            rs = rms_from_sq([(xT[:, kt, :Tn], [("xT", kt)]) for kt in range(KT)], Tn, D, None)
            for kt in range(KT):
                stt(aT[:, kt, :Tn], xT[:, kt, :Tn], vcol[:, gofs + kt:gofs + kt + 1], rstd[:, rs, :Tn], ALU.mult, ALU.mult,
                    r=[("xT", kt), "vcol", ("rstd", rs)], w=[("aT", kt)])

        bidc = {}
        cur = {"l": 0, "first": True}

        def wblock(name, parts):
            s = nxt("w", 2)
            l = cur["l"]
            bid = bidc.setdefault(name, len(bidc))
            total = max(c0 + nk * ncol for (c0, nk, ncol, _) in parts)
            if cur["first"] or os.environ.get("NOWSC"):
                for pi_, (c0, nk, ncol, src3) in enumerate(parts):
                    view = wsl[s][:, c0:c0 + nk * ncol].rearrange("p (k n) -> p k n", n=ncol)
                    dma("pool", view, src3, r=[], w=[("w", s, pi_)])
                if not os.environ.get("NOWSC"):
                    dma("sp", wsc[l, bid, :, 0:total], wsl[s][:, 0:total], r=WK(s), w=[("wsc", l, bid)])
            else:
                half = total // 2
                dma("pool", wsl[s][:, 0:half], wsc[l, bid, :, 0:half], r=[("wsc", l, bid)], w=[("w", s, 0)])
                dma("pool", wsl[s][:, half:total], wsc[l, bid, :, half:total], r=[("wsc", l, bid)], w=[("w", s, 1)])
            return s

        def wload(name, src3, nk, ncol):
            step = nk // 4
            parts = [(k0 * ncol, step, ncol, src3[:, k0:k0 + step, :]) for k0 in range(0, nk, step)]
            s = wblock(name, parts)
            return s, wsl[s][:, 0:nk * ncol].rearrange("p (k n) -> p k n", n=ncol)

        def WK(s):
            return [("w", s, i) for i in range(4)]

        def wview(w2d, c0, ncol):
            return w2d.rearrange("(k p) n -> p k n", p=128)[:, :, c0:c0 + ncol]

        def layer_setup(l):
            rows = [(g_mix, 0, 16), (g_ffn, 16, 16), (g_ple, 32, 16), (g_ao, 48, 8), (g_so, 56, 8),
                    (d_skip, 64, 8), (b_glu, 72, 8), (g_q, 80, 1), (g_k, 81, 1)]
            st = stg[:, 0, 0:128]
            for (t, r0, n) in rows:
                dma("sp", stg[r0:r0 + n, 0, 0:128], t[l].rearrange("(r c) -> r c", c=128), r=[], w=["stg"])
            b = bankA()
            tr(b, PS[b][:, 0:82], st[0:82, :], ident[0:82, 0:82], r=["stg", "ident"])
            cp(vcol[:, 0:82], PS[b][:, 0:82], [("ps", b)], ["vcol"])
            ts(vcol[:, 82:83], vcol[:, 80:81], 128.0 ** -0.5, 0.0, ALU.mult, ALU.add, ["vcol"], ["vcol"])
            ms(bfc[:], 0.0, ["bfc"])
            dma("sp", bfc[0:8, :], b_f[l].rearrange("(r c) -> r c", c=1), r=[], w=["bfc"])
            dma("sp", bfc[32:40, :], b_f[l].rearrange("(r c) -> r c", c=1), r=[], w=["bfc"])
            ts(bfc[:], bfc[:], -1.0, 0.0, ALU.mult, ALU.add, ["bfc"], ["bfc"])
            ms(wf2[:], 0.0, ["wf2"])
            wfsrc = wview(w_in[l], 4096, 8)
            dma("pool", wf2[:, :, 0:8], wfsrc, r=[], w=["wf2"])
            dma("pool", wf2[:, :, 32:40], wfsrc, r=[], w=["wf2"])
            sA = stg[:, 1, 0:128]
            dma("sp", stg[0:32, 1, 0:128], a_re[l].rearrange("(gp gl) p -> gp (gl p)", gl=2), r=[], w=["stgA"])
            dma("sp", stg[32:64, 1, 0:128], a_im[l].rearrange("(gp gl) p -> gp (gl p)", gl=2), r=[], w=["stgA"])
            dma("sp", stg[64:96, 1, 128:130], log_dt[l].rearrange("(gp gl) -> gp gl", gl=2), r=[], w=["stgA"])
            for gl in range(2):
                cp(stg[64:96, 1, gl * 64:(gl + 1) * 64], stg[64:96, 1, 128 + gl:129 + gl].to_broadcast([32, 64]),
                   ["stgA"], ["stgA"])
            b = bankA()
            tr(b, PS[b][:, 0:96], sA[0:96, :], ident[0:96, 0:96], r=["stgA", "ident"])
            S = lambda i: sm[:, i, :]
            K = ["sm"]
            cp(sm[:, 0:3, :], PS[b][:, 0:96].rearrange("p (a g) -> p a g", g=32), [("ps", b)], K)
            act(S(3), S(2), AF.Exp, K, K)
            tt(S(4), S(0), S(3), ALU.mult, K, K)
            act(magT[:], S(4), AF.Exp, K, ["magT"])
            tt(S(5), S(1), S(3), ALU.mult, K, K)
            ts(thn[:], S(5), 1.0 / TWO_PI, 0.0, ALU.mult, ALU.add, K, ["thn"])
            cp(smi[:], thn[:], ["thn"], ["smi"])
            tt(S(6), thn[:], smi[:], ALU.subtract, ["thn", "smi"], K)
            act(S(7), S(6), AF.Sin, K, K, scale=TWO_PI)
            act(S(8), S(6), AF.Sin, K, K, scale=math.pi)
            tt(S(9), S(8), S(8), ALU.mult, K, K)
            ts(S(9), S(9), -2.0, 1.0, ALU.mult, ALU.add, K, K)
            cp(cth[:], S(9), K, ["cth"])
            cp(sth[:], S(7), K, ["sth"])
            tt(S(10), magT[:], S(9), ALU.mult, K + ["magT"], K)
            tt(S(11), magT[:], S(7), ALU.mult, K + ["magT"], K)
            ts(S(10), S(10), -1.0, 0.0, ALU.add, ALU.add, K, K)
            tt(S(12), S(0), S(0), ALU.mult, K, K)
            tt(S(13), S(1), S(1), ALU.mult, K, K)
            tt(S(12), S(12), S(13), ALU.add, K, K)
            A("dve", lambda e: e.reciprocal(out=S(12), in_=S(12)), r=K, w=K)
            tt(S(13), S(10), S(0), ALU.mult, K, K)
            tt(S(14), S(11), S(1), ALU.mult, K, K)
            tt(S(13), S(13), S(14), ALU.add, K, K)
            tt(S(13), S(13), S(12), ALU.mult, K, K)
            tt(S(14), S(11), S(0), ALU.mult, K, K)
            tt(S(15), S(10), S(1), ALU.mult, K, K)
            tt(S(14), S(14), S(15), ALU.subtract, K, K)
            tt(S(14), S(14), S(12), ALU.mult, K, K)
            tmpf = tmp[:, :, :].rearrange("p a t -> p (a t)")
            v3 = lambda i: tmpf[:, i * 512:(i + 1) * 512].rearrange("p (g c) -> p g c", c=16)
            br, bi, t2, t3, t4 = v3(0), v3(1), v3(2), v3(3), v3(4)
            TA = [("tmp", x) for x in range(10)]
            for gl in range(2):
                dma("sp", br[gl * 64:(gl + 1) * 64], b_re[l].rearrange("(gp gl) p c -> gl p gp c", gl=2)[gl], r=[], w=TA)
                dma("sp", bi[gl * 64:(gl + 1) * 64], b_im[l].rearrange("(gp gl) p c -> gl p gp c", gl=2)[gl], r=[], w=TA)
            crb = sm[:, 13:14, :].rearrange("p a g -> p g a").to_broadcast([128, 32, 16])
            cib = sm[:, 14:15, :].rearrange("p a g -> p g a").to_broadcast([128, 32, 16])
            KK = TA + ["sm"]
            tt(t2, br, crb, ALU.mult, KK, TA)
            tt(t3, bi, cib, ALU.mult, KK, TA)
            tt(t2, t2, t3, ALU.subtract, TA, TA)
            tt(t3, bi, crb, ALU.mult, KK, TA)
            tt(t4, br, cib, ALU.mult, KK, TA)
            tt(t3, t3, t4, ALU.add, TA, TA)
            RG = ["regS"] + [("uT", x) for x in range(8)]
            srcB = reg[:, 0:2048].rearrange("p (q r x) -> p q r x", r=2, x=128)
            ms(reg[:, 0:2048], 0.0, RG)
            for ri, tsrc in enumerate((t2, t3)):
                for gl in range(2):
                    dst = srcB[gl * 64:(gl + 1) * 64, :, ri, :].rearrange("p q (j x) -> p q j x", x=32)[:, :, :, gl * 16:gl * 16 + 16]
                    cp(dst, tsrc[gl * 64:(gl + 1) * 64].rearrange("p (q j) c -> p q j c", j=4), TA + RG, RG)
            for q in range(8):
                b = bankA()
                for ri in range(2):
                    tr(b, PS[b][:, ri * 128:(ri + 1) * 128], srcB[:, q, ri, :], ident[:], r=RG + ["ident"])
                cp(Bc[:, q, :, :], PS[b][:, 0:256].rearrange("p (r x) -> p r x", x=128), [("ps", b)], ["Bc"])
            stC = reg[0:32, 0:2048].rearrange("p (g x) -> p g x", x=128)
            for ri, csrc in enumerate((c_re, c_im)):
                for g0 in range(0, 32, 16):
                    ms(reg[0:32, 0:2048], 0.0, RG)
                    for gl in range(2):
                        dma("sp", stC[gl * 16:(gl + 1) * 16, :, gl * 64:(gl + 1) * 64],
                            csrc[l].rearrange("(gp gl) c p -> gl c gp p", gl=2)[gl][:, g0:g0 + 16, :], r=[], w=RG)
                    b = bankA()
                    for gg in range(16):
                        tr(b, PS[b][:, gg * 32:(gg + 1) * 32], stC[:, gg, :], ident[0:32, 0:32], r=RG + ["ident"])
                    src = PS[b][:, :].rearrange("p (g x) -> p g x", x=32)
                    if ri == 0:
                        cp(Cc[:, g0:g0 + 16, 0, :], src, [("ps", b)], ["Cc"])
                    else:
                        ts(Cc[:, g0:g0 + 16, 1, :], src, -1.0, 0.0, ALU.mult, ALU.add, [("ps", b)], ["Cc"])
            if SAMPLE:
                for si in range(2):
                    dma("sp", stg[si * 64:si * 64 + 32, 1, 0:128], sre[l, si].rearrange("(gp gl) p -> gp (gl p)", gl=2), r=[], w=["stgA"])
                    dma("sp", stg[si * 64 + 32:si * 64 + 64, 1, 0:128], sim[l, si].rearrange("(gp gl) p -> gp (gl p)", gl=2), r=[], w=["stgA"])
                b = bankA()
                tr(b, PS[b][:, 0:128], sA[:, :], ident[:], r=["stgA", "ident"])
                cp(Xst[:, 1:3, :, :], PS[b][:, 0:128].rearrange("p (s r g) -> p s r g", r=2, g=32), [("ps", b)], ["Xst"])
            ms(Xst[:, 0, :, :], 0.0, ["Xst"])
            ms(ccar[:], 0.0, ["ccar"])

        def do_tile(l, kind, ti):
            prompt = kind == "p"
            Tn = TP if prompt else TS
            nb = NBT if prompt else 1
            bw = 128 if prompt else TS
            tok0 = ti * TP if prompt else SEQ
            segs = [(0, TP, 0)] if prompt else [(0, 16, 1), (16, 16, 2)]
            T = lambda i: tmp[:, i, :Tn]
            TK = lambda *i: [("tmp", x) for x in i]
            TB = lambda i: tmpB[:, i, :Tn]
            TBK = lambda i: [("tmpB", i)]
            st = {}
            cur["l"] = l
            cur["first"] = prompt and ti == 0

            if l == 0:
                src = xp if prompt else xs
                for tb in range(nb):
                    r0 = ti * TP + tb * 128 if prompt else 0
                    for hf in range(2):
                        dma("sp", stg[0:bw, hf, :], src[r0:r0 + bw, hf * 1024:(hf + 1) * 1024], r=[], w=[("stg", hf)])
                    for g in range(4):
                        b = bankA()
                        for j in range(4):
                            kt = 4 * g + j
                            tr(b, PS[b][:, j * bw:(j + 1) * bw], stg[0:bw, kt // 8, (kt % 8) * 128:(kt % 8 + 1) * 128],
                               ident[0:bw, 0:bw], r=[("stg", kt // 8), "ident"])
                        cp(xT[:, 4 * g:4 * g + 4, tb * 128:tb * 128 + bw], PS[b][:, 0:4 * bw].rearrange("p (j t) -> p j t", t=bw),
                           [("ps", b)], [("xT", 4 * g + j) for j in range(4)], eng="act" if g % 2 else "dve")
            else:
                for kt in range(KT):
                    dma("sp", xT[:, kt, :Tn], hres[kt, :, tok0:tok0 + Tn], r=["hres"], w=[("xT", kt)])
            psrc = pp if prompt else psm
            for tb in range(nb):
                r0 = ti * TP + tb * 128 if prompt else 0
                dma("sp", stg[0:bw, 1, 0:256], psrc[l, r0:r0 + bw, :], r=[], w=[("stg", 1)])
                b = bankA()
                for j in range(2):
                    tr(b, PS[b][:, j * bw:(j + 1) * bw], stg[0:bw, 1, j * 128:(j + 1) * 128], ident[0:bw, 0:bw], r=[("stg", 1), "ident"])
                cp(pT[:, :, tb * 128:tb * 128 + bw], PS[b][:, 0:2 * bw].rearrange("p (j t) -> p j t", t=bw), [("ps", b)], ["pT"])

            norm_x(l, Tn, 0)

            if UPTO < 1:
                return
            b = bankA()
            for kt in range(KT):
                mm(b, PS[b][0:40, :Tn], wf2[:, kt, :], aT[:, kt, :Tn], kt == 0, kt == KT - 1, r=["wf2", ("aT", kt)])
            cfa = lambda i: tmp[0:40, i, :Tn]
            act(cfa(0), PS[b][0:40, :Tn], AF.Exp, [("ps", b), "bfc"], TK(0), bias=bfc[:, 0:1], scale=-1.0)
            act(cfa(0), cfa(0), AF.Ln, TK(0), TK(0), bias=onec[0:40, 0:1], scale=1.0)
            ts(cfa(1), cfa(0), -1.0, 0.0, ALU.mult, ALU.add, TK(0), TK(1))
            for (c0, n, slot) in segs:
                A("dve", lambda e, c0=c0, n=n, slot=slot: e.tensor_tensor_scan(
                    out=tmp[0:40, 2, c0:c0 + n], data0=onec[0:40, 0:1].to_broadcast([40, n]), data1=tmp[0:40, 1, c0:c0 + n],
                    initial=ccar[:, slot:slot + 1], op0=ALU.mult, op1=ALU.add), r=TK(1) + ["ccar"], w=TK(2))
                cp(ccar[:, slot:slot + 1], tmp[0:40, 2, c0 + n - 1:c0 + n], TK(2), ["ccar"])
            cp(chl[:, :Tn], cfa(2), TK(2), ["chl"])
            tt(cfa(3), cfa(2), chl[:, :Tn], ALU.subtract, TK(2) + ["chl"], TK(3))
            cp(chl[32:40, :Tn], tmp[32:40, 3, :Tn], TK(3), ["chl"])
            b = bankA()
            if prompt:
                for tb in range(nb):
                    tr(b, PS[b][:, tb * 16:tb * 16 + 8], tmp[0:8, 1, tb * 128:(tb + 1) * 128], ident[0:8, 0:8], r=TK(1) + ["ident"])
                    tr(b, PS[b][:, tb * 16 + 8:tb * 16 + 16], tmp[0:8, 2, tb * 128:(tb + 1) * 128], ident[0:8, 0:8], r=TK(2) + ["ident"])
                pv = PS[b][:, 0:nb * 16].rearrange("p (t x) -> p t x", x=16)
                lft = tmp[:, 9, 0:nb * 8].rearrange("p (t x) -> p t x", x=8)
                cp(lft, pv[:, :, 0:8], [("ps", b)], TK(9))
                ts(negc[:, ti * NBT:(ti + 1) * NBT, :], pv[:, :, 8:16], -1.0, 0.0, ALU.mult, ALU.add, [("ps", b)], [("negc", ti)])
                dma("sp", lfp[l, ti * TP:(ti + 1) * TP, :].rearrange("(t p) x -> p t x", p=128), lft, r=TK(9), w=[])
            else:
                for si in range(2):
                    tr(b, PS[b][0:16, si * 16:si * 16 + 8], tmp[0:8, 1, si * 16:(si + 1) * 16], ident[0:8, 0:8], r=TK(1) + ["ident"])
                    tr(b, PS[b][0:16, si * 16 + 8:si * 16 + 16], tmp[0:8, 2, si * 16:(si + 1) * 16], ident[0:8, 0:8], r=TK(2) + ["ident"])
                pv = PS[b][0:16, 0:32].rearrange("p (t x) -> p t x", x=16)
                lft = tmp[0:16, 9, 0:16].rearrange("p (t x) -> p t x", x=8)
                cp(lft, pv[:, :, 0:8], [("ps", b)], TK(9))
                ts(negcs[:, :, :], pv[:, :, 8:16], -1.0, 0.0, ALU.mult, ALU.add, [("ps", b)], ["negcs"])
                dma("sp", lfs[l].rearrange("(s p) x -> p s x", p=16), lft, r=TK(9), w=[])

            if UPTO < 2:
                return
            for ub_i in range(2):
                s, wv = wload(("u", ub_i), wview(w_in[l], 3072 + ub_i * 512, 512), KT, 512)
                for j in range(0 if os.environ.get("SKIPMM") else 4):
                    b = bankA()
                    for kt in range(KT):
                        mm(b, PS[b][:, :Tn], wv[:, kt, j * 128:(j + 1) * 128], aT[:, kt, :Tn], kt == 0, kt == KT - 1, r=WK(s) + [("aT", kt)])
                    c = ub_i * 4 + j
                    if os.environ.get("SKIPCP"):
                        continue
                    cp(uT[:, c, :Tn], PS[b][:, :Tn], [("ps", b)], [("uT", c)], eng="act")
                    cp(ub[:, c, :Tn], uT[:, c, :Tn], [("uT", c)], [("ub", c)])

            def part_qkv():
                jobs = [(which, hb, j) for which in range(2) for hb in range(2) for j in range(4)]
                wst = {}

                def q_s1(job):
                    which, hb, j = job
                    if j == 0:
                        wst[(which, hb)] = wload(("qk", which, hb), wview(w_in[l], which * 1024 + hb * 512, 512), KT, 512)
                    s, wv = wst[(which, hb)]
                    b = bankA()
                    for kt in range(KT):
                        mm(b, PS[b][:, :Tn], wv[:, kt, j * 128:(j + 1) * 128], aT[:, kt, :Tn], kt == 0, kt == KT - 1, r=WK(s) + [("aT", kt)])
                    return b

                def q_s3(job, b, rs):
                    which, hb, j = job
                    h = hb * 4 + j
                    if which == 0:
                        stt(qT[:, h, :Tn], PS[b][:, :Tn], vcol[:, 82:83], rstd[:, rs, :Tn], ALU.mult, ALU.mult,
                            r=[("ps", b), "vcol", ("rstd", rs)], w=[("qT", h)])
                        return
                    ks = nxt("p", 2)
                    kk = TBK(ks)
                    stt(TB(ks), PS[b][:, :Tn], vcol[:, 81:82], rstd[:, rs, :Tn], ALU.mult, ALU.mult,
                        r=[("ps", b), "vcol", ("rstd", rs)], w=kk)
                    if prompt:
                        cp(KTb[:, h, ti * TP:(ti + 1) * TP], TB(ks), kk, [("KT", h, ti)], eng="act")
                    else:
                        cp(kTs[:, h, :], TB(ks), kk, [("kTs", h)], eng="act")
                    b2 = 4
                    for tb in range(nb):
                        tr(b2, PS[b2][0:bw, TP + tb * 128:TP + (tb + 1) * 128], tmpB[:, ks, tb * 128:tb * 128 + bw], ident[:], r=kk + ["ident"])
                    kslot = h % 2
                    cp(kst[0:bw, kslot, 0:nb, :], PS[b2][0:bw, TP:TP + nb * 128].rearrange("p (t x) -> p t x", x=128),
                       [("ps", b2)], [("kst", kslot)], eng="act")
                    if prompt:
                        dma("sp", kp[l, ti * TP:(ti + 1) * TP, h * 128:(h + 1) * 128].rearrange("(t p) x -> p t x", p=128),
                            kst[:, kslot, :, :], r=[("kst", kslot)], w=[])
                    else:
                        dma("sp", kso[l, :, h * 128:(h + 1) * 128], kst[0:TS, kslot, 0, :], r=[("kst", kslot)], w=[])

                bnx = q_s1(jobs[0])
                for i_, job in enumerate(jobs):
                    bcur = bnx
                    if i_ + 1 < len(jobs):
                        bnx = q_s1(jobs[i_ + 1])
                    rs = rms_from_sq([(PS[bcur][:, :Tn], [("ps", bcur)])], Tn, 128, None)
                    q_s3(job, bcur, rs)
                for vb_i in range(2):
                    s, wv = wload(("v", vb_i), wview(w_in[l], 2048 + vb_i * 512, 512), KT, 512)
                    vblocks = [(tb * 128, 128, tb) for tb in range(nb)] if prompt else [(0, 16, 0), (16, 16, 1)]
                    for (c0, n, tb) in vblocks:
                        b = bankA()
                        for kt in range(KT):
                            mm(b, PS[b][0:n, :], aT[:, kt, c0:c0 + n], wv[:, kt, :], kt == 0, kt == KT - 1, r=WK(s) + [("aT", kt)])
                        vs_ = nxt("p", 2)
                        cp(vst[0:n, vs_, :], PS[b][0:n, :], [("ps", b)], [("vst", vs_)], eng="act")
                        if prompt:
                            cp(Vb[:, ti * NBT + tb, vb_i * 512:(vb_i + 1) * 512], vst[:, vs_, :], [("vst", vs_)], [("V", ti)])
                            r0 = ti * TP + c0
                            dma("sp", vp[l, r0:r0 + 128, vb_i * 512:(vb_i + 1) * 512], vst[:, vs_, :], r=[("vst", vs_)], w=[])
                        else:
                            cp(vbs[:, tb, vb_i * 512:(vb_i + 1) * 512], vst[0:16, vs_, :], [("vst", vs_)], ["vbs"])
                            dma("sp", vso[l, c0:c0 + 16, vb_i * 512:(vb_i + 1) * 512], vst[0:16, vs_, :], r=[("vst", vs_)], w=[])

            def part_s5():
                tsrc = tau if prompt else taus
                tabs = [(1, 2), (9, 10)]

                def tables(gp):
                    sn_i, cs_i = tabs[gp % 2]
                    if not cur["first"] and not os.environ.get("NOTABC"):
                        if prompt:
                            dma("sp", tmp[:, sn_i:sn_i + 2, :], tabsc[gp], r=[("tabsc", gp)], w=TK(sn_i, cs_i))
                        else:
                            for c0 in (0, 16):
                                dma("sp", tmp[:, sn_i:sn_i + 2, c0:c0 + 16], tabsc[gp, :, :, 0:16], r=[("tabsc", gp)], w=TK(sn_i, cs_i))
                        return
                    act(T(0), tsrc[:, :Tn], AF.Identity, ["tau", "thn"], TK(0), scale=thn[:, gp:gp + 1])
                    cp(tmpi[:, :Tn], T(0), TK(0), ["tmpi"])
                    tt(T(0), T(0), tmpi[:, :Tn], ALU.subtract, TK(0) + ["tmpi"], TK(0))
                    act(T(sn_i), T(0), AF.Sin, TK(0), TK(sn_i), scale=TWO_PI)
                    act(T(cs_i), T(0), AF.Sin, TK(0), TK(cs_i), scale=math.pi)
                    act(T(cs_i), T(cs_i), AF.Square, TK(cs_i), TK(cs_i))
                    act(T(cs_i), T(cs_i), AF.Identity, TK(cs_i) + ["onec"], TK(cs_i), bias=onec[:, 0:1], scale=-2.0)
                    if cur["first"] and not os.environ.get("NOTABC"):
                        dma("sp", tabsc[gp], tmp[:, sn_i:sn_i + 2, :], r=TK(sn_i, cs_i), w=[("tabsc", gp)])

                def bu(gp):
                    quad, j = gp // 4, gp % 4
                    bk = rot["banks"][gp % 2]
                    mm(bk, PS[bk][:, 0:Tn], Bc[32 * j:32 * j + 32, quad, 0, :], ub[32 * j:32 * j + 32, quad, :Tn], True, True,
                       r=["Bc", ("ub", quad)], tp=(32 * j, 0))
                    mm(bk, PS[bk][:, TP:TP + Tn], Bc[32 * j:32 * j + 32, quad, 1, :], ub[32 * j:32 * j + 32, quad, :Tn], True, True,
                       r=["Bc", ("ub", quad)], tp=(32 * j, 0))

                def body(gp):
                    quad, j = gp // 4, gp % 4
                    sn_i, cs_i = tabs[gp % 2]
                    bk = rot["banks"][gp % 2]
                    bre = PS[bk][:, 0:Tn]
                    bim = PS[bk][:, TP:TP + Tn]
                    BK = [("ps", bk)]
                    SK = ["smi16"]
                    inis = [(c0, n, slot, winit[:, slot, 0, gp:gp + 1], winit[:, slot, 1, gp:gp + 1]) for (c0, n, slot) in segs]
                    tt(T(3), T(cs_i), bre, ALU.mult, TK(cs_i) + BK, TK(3))
                    tt(T(5), T(sn_i), bim, ALU.mult, TK(sn_i) + BK, TK(5))
                    tt(T(4), T(cs_i), bim, ALU.mult, TK(cs_i) + BK, TK(4))
                    tt(T(8), T(sn_i), bre, ALU.mult, TK(sn_i) + BK, TK(8))
                    tt(T(3), T(3), T(5), ALU.add, TK(3, 5), TK(3))
                    tt(T(4), T(4), T(8), ALU.subtract, TK(4, 8), TK(4))
                    for (c0, n, slot, i0, i1) in inis:
                        for (dst, srcw, ini) in ((6, 3, i0), (7, 4, i1)):
                            A("dve", lambda e, dst=dst, srcw=srcw, ini=ini, c0=c0, n=n, gp=gp: e.tensor_tensor_scan(
                                out=tmp[:, dst, c0:c0 + n], data0=magT[:, gp:gp + 1].to_broadcast([128, n]),
                                data1=tmp[:, srcw, c0:c0 + n], initial=ini, op0=ALU.mult, op1=ALU.add),
                              r=TK(srcw) + ["winit", "magT"], w=TK(dst))
                    tt(T(8), T(cs_i), T(6), ALU.mult, TK(cs_i, 6), TK(8))
                    tt(T(5), T(sn_i), T(7), ALU.mult, TK(sn_i, 7), TK(5))
                    tt(T(3), T(cs_i), T(7), ALU.mult, TK(cs_i, 7), TK(3))
                    tt(T(4), T(sn_i), T(6), ALU.mult, TK(sn_i, 6), TK(4))
                    tt(T(8), T(8), T(5), ALU.subtract, TK(8, 5), TK(8))
                    tt(T(3), T(3), T(4), ALU.add, TK(3, 4), TK(3))
                    cp(xrb[:, 0, :Tn], T(8), TK(8), ["xrb0"], eng="act")
                    cp(xrb[:, 1, :Tn], T(3), TK(3), ["xrb1"], eng="act")
                    for (c0, n, slot, i0, i1) in inis:
                        cp(Xst[:, slot, 0, gp:gp + 1], tmp[:, 8, c0 + n - 1:c0 + n], TK(8), ["Xst"])
                        cp(Xst[:, slot, 1, gp:gp + 1], tmp[:, 3, c0 + n - 1:c0 + n], TK(3), ["Xst"])
                    by = 5
                    mm(by, PS[by][32 * j:32 * j + 32, :Tn], Cc[:, gp, 0, :], xrb[:, 0, :Tn], True, False, r=["Cc", "xrb0"], tp=(0, 32 * j))
                    mm(by, PS[by][32 * j:32 * j + 32, :Tn], Cc[:, gp, 1, :], xrb[:, 1, :Tn], False, True, r=["Cc", "xrb1"], tp=(0, 32 * j))
                    if j == 3:
                        stt(uT[:, quad, :Tn], uT[:, quad, :Tn], vcol[:, 64 + quad:65 + quad], PS[by][:, :Tn], ALU.mult, ALU.add,
                            r=[("uT", quad), "vcol", ("ps", by)], w=[("uT", quad)])
                        act(uT[:, quad, :Tn], uT[:, quad, :Tn], AF.Gelu_apprx_tanh, [("uT", quad)], [("uT", quad)])
                        cp(ub[:, quad, :Tn], uT[:, quad, :Tn], [("uT", quad)], [("ub", quad)])

                for (c0, n, slot) in segs:
                    xr_ = Xst[:, slot, 0, :]
                    xi_ = Xst[:, slot, 1, :]
                    wr0 = winit[:, slot, 0, :]
                    wi0 = winit[:, slot, 1, :]
                    WI = ["winit"]
                    tt(wr0, cth[:], xr_, ALU.mult, ["cth", "Xst"], WI)
                    tt(sm[:, 18, :], sth[:], xi_, ALU.mult, ["sth", "Xst"], ["smi18"])
                    tt(wi0, cth[:], xi_, ALU.mult, ["cth", "Xst"], WI)
                    tt(sm[:, 19, :], sth[:], xr_, ALU.mult, ["sth", "Xst"], ["smi19"])
                    tt(wr0, wr0, sm[:, 18, :], ALU.subtract, WI + ["smi18"], WI)
                    tt(wi0, wi0, sm[:, 19, :], ALU.add, WI + ["smi19"], WI)
                tables(0)
                bu(0)
                for gp in range(32):
                    if gp + 1 < 32:
                        tables(gp + 1)
                        bu(gp + 1)
                    body(gp)
            def part_glu():
                for nb2 in range(2):
                    s, wv = wload(("glu", nb2), wview(w_glu[l], nb2 * 512, 512), 8, 512)
                    for j in range(4):
                        n = nb2 * 4 + j
                        b = bankA()
                        for kq in range(8):
                            mm(b, PS[b][:, :Tn], wv[:, kq, j * 128:(j + 1) * 128], ub[:, kq, :Tn], kq == 0, kq == 7, r=WK(s) + [("ub", kq)])
                        act(T(0), PS[b][:, :Tn], AF.Sigmoid, [("ps", b), "vcol"], TK(0), bias=vcol[:, 72 + n:73 + n], scale=1.0)
                        tt(T(9), uT[:, n, :Tn], T(0), ALU.mult, [("uT", n)] + TK(0), TK(9))
                        s2 = nxt("s", 2)
                        act(sqb[:, s2, :Tn], T(9), AF.Square, TK(9), [("sqb", s2)])
                        mm(5, PS[5][:, :Tn], ones_bf[:], sqb[:, s2, :Tn], n == 0, n == 7, r=["ones_bf", ("sqb", s2)])
                        ts(mT(8 + n, Tn), T(9), vcol[:, 56 + n:57 + n], 0.0, ALU.mult, ALU.add, TK(9) + ["vcol"], [("aT", 8 + n)])
                rs_s = st.setdefault('x', {}).__setitem__('s', nxt("t", 3)) or st['x']['s']
                rstd_from(PS[5][:, :Tn], [("ps", 5)], rs_s, Tn, 1024)

            def part_attn():
                if prompt:
                    nkb = NBT * ti + NBT
                    for h in range(H):
                        bo = 6
                        br_ = 7
                        def s1(kb):
                            diag = kb >= NBT * ti
                            qs = (kb - NBT * ti) * 128 if diag else 0
                            bs = bankA()
                            ktile = kb // NBT
                            mm(bs, PS[bs][:, qs:TP], KTb[:, h, kb * 128:(kb + 1) * 128], qT[:, h, qs:TP], True, False, r=[("KT", h, ktile), ("qT", h)])
                            mm(bs, PS[bs][:, qs:TP], sel2[:, h, :], chl[:, qs:TP], False, not diag, r=["sel2", "chl"])
                            if diag:
                                mm(bs, PS[bs][:, qs:qs + 128], identb[:], negmask[:], False, True, r=["identb", "negmask"])
                            return bs, qs, ktile

                        def s2(kb, bs, qs, ktile):
                            pi = nxt("p", 3)
                            act(Pt[:, pi, qs:TP], PS[bs][:, qs:TP], AF.Exp, [("ps", bs), ("negc", ktile)], [("Pt", pi)], bias=negc[:, kb, h:h + 1], scale=1.0)
                            mm(bo, PS[bo][:, qs:TP], Vb[:, kb, h * 128:(h + 1) * 128], Pt[:, pi, qs:TP], kb == 0, kb == nkb - 1, r=[("V", ktile), ("Pt", pi)])
                            mm(br_, PS[br_][:, qs:TP], ones_bf[:], Pt[:, pi, qs:TP], kb == 0, kb == nkb - 1, r=["ones_bf", ("Pt", pi)])

                        nxt_ = s1(0)
                        for kb in range(nkb):
                            cur_ = nxt_
                            if kb + 1 < nkb:
                                nxt_ = s1(kb + 1)
                            s2(kb, *cur_)
                        attn_epilogue(l, h, Tn, bo, br_)
                else:
                    sample_attention(l)
                rs_a = st.setdefault('x', {}).__setitem__('a', nxt("t", 3)) or st['x']['a']
                rstd_from(PS[4][:, :Tn], [("ps", 4)], rs_a, Tn, 1024)

            if prompt and not os.environ.get("NOFORK"):
                main_l = pr.cur
                lb = []
                pr.cur = lb
                rot["banks"] = [2, 3]
                rot["sbanks"] = [4]
                part_qkv()
                part_attn()
                la = []
                pr.cur = la
                rot["banks"] = [0, 1]
                part_s5()
                rot["banks"] = [0, 1, 2, 3]
                rot["sbanks"] = [4, 5]
                pr.cur = main_l
                na, nb_ = len(la), len(lb)
                ia = ib = 0
                while ia < na or ib < nb_:
                    if ib >= nb_ or (ia < na and ia * nb_ <= ib * na):
                        main_l.append(la[ia]); ia += 1
                    else:
                        main_l.append(lb[ib]); ib += 1
                part_glu()
            else:
                part_qkv()
                part_s5()
                part_glu()
                part_attn()
            rs_a = st['x']['a']
            rs_s = st['x']['s']
            for nb4 in range(4):
                s, wv = wload(("wo", nb4), wview(w_out[l], nb4 * 512, 512), KT, 512)
                for j in range(4):
                    n = nb4 * 4 + j
                    b1 = bankA()
                    for kt in range(8):
                        mm(b1, PS[b1][:, :Tn], wv[:, kt, j * 128:(j + 1) * 128], mT(kt, Tn), kt == 0, kt == 7, r=WK(s) + [("aT", kt)])
                    b2 = bankA()
                    for kt in range(8, 16):
                        mm(b2, PS[b2][:, :Tn], wv[:, kt, j * 128:(j + 1) * 128], mT(kt, Tn), kt == 8, kt == 15, r=WK(s) + [("aT", kt)])
                    tt(T(0), PS[b1][:, :Tn], rstd[:, rs_a, :Tn], ALU.mult, [("ps", b1), ("rstd", rs_a)], TK(0))
                    tt(xT[:, n, :Tn], xT[:, n, :Tn], T(0), ALU.add, [("xT", n)] + TK(0), [("xT", n)])
                    tt(T(1), PS[b2][:, :Tn], rstd[:, rs_s, :Tn], ALU.mult, [("ps", b2), ("rstd", rs_s)], TK(1))
                    tt(xT[:, n, :Tn], xT[:, n, :Tn], T(1), ALU.add, [("xT", n)] + TK(1), [("xT", n)])

            if UPTO < 9:
                return
            norm_x(l, Tn, 16)
            akey = lambda jj: ("ub", jj) if jj < 8 else ("qT", jj - 8) if jj < 16 else ("Pt", jj - 16) if jj < 19 else "xrb0" if jj == 19 else "xrb1" if jj == 20 else "act21"
            for half in range(2):
                for blk in range(11):
                    c0 = (half * 11 + blk) * 256
                    gsrc = wview(w_gate[l], c0, 256)
                    usrc = wview(w_up[l], c0, 256)
                    s = wblock(("gu", half, blk), [(0, 8, 256, gsrc[:, 0:8, :]), (2048, 8, 256, gsrc[:, 8:16, :]),
                                                   (4096, 8, 256, usrc[:, 0:8, :]), (6144, 8, 256, usrc[:, 8:16, :])])
                    gv = wsl[s][:, 0:4096].rearrange("p (k n) -> p k n", n=256)
                    uv = wsl[s][:, 4096:8192].rearrange("p (k n) -> p k n", n=256)
                    for j in range(2):
                        jj = blk * 2 + j
                        bg = bankA()
                        for kt in range(KT):
                            mm(bg, PS[bg][:, :Tn], gv[:, kt, j * 128:(j + 1) * 128], aT[:, kt, :Tn], kt == 0, kt == KT - 1, r=WK(s) + [("aT", kt)])
                        bu_ = bankA()
                        for kt in range(KT):
                            mm(bu_, PS[bu_][:, :Tn], uv[:, kt, j * 128:(j + 1) * 128], aT[:, kt, :Tn], kt == 0, kt == KT - 1, r=WK(s) + [("aT", kt)])
                        t_ = nxt("p", 2)
                        act(T(t_), PS[bg][:, :Tn], AF.Silu, [("ps", bg)], TK(t_))
                        tt(actT[:, jj, :Tn], T(t_), PS[bu_][:, :Tn], ALU.mult, TK(t_) + [("ps", bu_)], [akey(jj)])
                for np_ in range(8):
                    src = w_down[l].rearrange("(k p) n -> p k n", p=128)[:, half * 22:(half + 1) * 22, np_ * 256:(np_ + 1) * 256]
                    s = wblock(("dn", half, np_), [(0, 11, 256, src[:, 0:11, :]), (11 * 256, 11, 256, src[:, 11:22, :])])
                    dv = wsl[s][:, 0:22 * 256].rearrange("p (k n) -> p k n", n=256)
                    for j in range(2):
                        n = np_ * 2 + j
                        b = bankA()
                        for jj in range(22):
                            mm(b, PS[b][:, :Tn], dv[:, jj, j * 128:(j + 1) * 128], actT[:, jj, :Tn], jj == 0, jj == 21, r=WK(s) + [akey(jj)])
                        tt(xT[:, n, :Tn], xT[:, n, :Tn], PS[b][:, :Tn], ALU.add, [("xT", n), ("ps", b)], [("xT", n)])

            if UPTO < 10:
                return
            norm_x(l, Tn, 32)
            for nb4 in range(4):
                s, wv = wload(("pg", nb4), wview(w_pg[l], nb4 * 512, 512), KT, 512)
                wps = nb4 % 2
                dma("pool", wppb[:, wps, :, :], w_pp[l].rearrange("(k p) n -> p k n", p=128)[:, :, nb4 * 512:(nb4 + 1) * 512], r=[], w=[("wpp", wps)])
                for j in range(4):
                    n = nb4 * 4 + j
                    bg = bankA()
                    for kt in range(KT):
                        mm(bg, PS[bg][:, :Tn], wv[:, kt, j * 128:(j + 1) * 128], aT[:, kt, :Tn], kt == 0, kt == KT - 1, r=WK(s) + [("aT", kt)])
                    bp = bankA()
                    for k2 in range(2):
                        mm(bp, PS[bp][:, :Tn], wppb[:, wps, k2, j * 128:(j + 1) * 128], pT[:, k2, :Tn], k2 == 0, k2 == 1, r=[("wpp", wps), "pT"])
                    t_ = nxt("p", 2)
                    act(T(t_), PS[bg][:, :Tn], AF.Sigmoid, [("ps", bg)], TK(t_))
                    tt(T(t_), T(t_), PS[bp][:, :Tn], ALU.mult, TK(t_) + [("ps", bp)], TK(t_))
                    tt(xT[:, n, :Tn], xT[:, n, :Tn], T(t_), ALU.add, [("xT", n)] + TK(t_), [("xT", n)])

            if UPTO < 11:
                return
            if l == 0 and NL > 1:
                for kt in range(KT):
                    dma("sp", hres[kt, :, tok0:tok0 + Tn], xT[:, kt, :Tn], r=[("xT", kt)], w=["hres"])
            else:
                dst = yp if prompt else ys
                for tb in range(nb):
                    for g in range(4):
                        b = bankA()
                        for j in range(4):
                            kt = 4 * g + j
                            tr(b, PS[b][0:bw, j * 128:(j + 1) * 128], xT[:, kt, tb * 128:tb * 128 + bw], ident[:], r=[("xT", kt), "ident"])
                        hf = g // 2
                        cp(stg[0:bw, hf, (g % 2) * 512:(g % 2 + 1) * 512], PS[b][0:bw, :], [("ps", b)], [("stg", hf)], eng="act" if g % 2 else "dve")
                    r0 = ti * TP + tb * 128 if prompt else 0
                    for hf in range(2):
                        dma("sp", dst[r0:r0 + bw, hf * 1024:(hf + 1) * 1024], stg[0:bw, hf, :], r=[("stg", hf)], w=[])

        def mT(kt, Tn):
            return aT[:, kt, :Tn]

        onec = sb("onec", [128, 1])
        ms(onec[:], 1.0, ["onec"])
        kst = sb("kst", [128, 2, NBT, 128])
        vst = sb("vst", [128, 2, 512])
        negcs = sb("negcs", [16, 2, 8])
        negcp = sb("negcp", [128, 16, 8])
        kTs = sb("kTs", [128, H, TS], BF16)
        vbs = sb("vbs", [16, 2, 1024], BF16)
        kc = sb("kc", [128, 1, 1024], BF16)
        vc = sb("vc", [128, 1, 1024], BF16)
        kTc = sb("kTc", [128, H, 128], BF16)
        cl_tok = sb("cl_tok", [128, 16, 8])

        def attn_epilogue(l, h, Tn, bo, br_):
            act(tmpB[:, 0, :Tn], PS[br_][:, :Tn], AF.Ln, r=[("ps", br_)], w=[("tmpB", 0)])
            act(tmpB[:, 0, :Tn], tmpB[:, 0, :Tn], AF.Exp, r=[("tmpB", 0)], w=[("tmpB", 0)], scale=-1.0)
            tt(tmpB[:, 1, :Tn], PS[bo][:, :Tn], tmpB[:, 0, :Tn], ALU.mult, [("ps", bo), ("tmpB", 0)], [("tmpB", 1)])
            s2 = nxt("s", 2)
            act(sqb[:, s2, :Tn], tmpB[:, 1, :Tn], AF.Square, [("tmpB", 1)], [("sqb", s2)])
            mm(4, PS[4][:, :Tn], ones_bf[:], sqb[:, s2, :Tn], h == 0, h == H - 1, r=["ones_bf", ("sqb", s2)])
            ts(mT(h, Tn), tmpB[:, 1, :Tn], vcol[:, 48 + h:49 + h], 0.0, ALU.mult, ALU.add, [("tmpB", 1), "vcol"], [("aT", h)])

        def sample_attention(l):
            Tn = TS
            for si in range(2):
                q0 = si * 16
                for hh in range(2):
                    dma("sp", cl_tok[:, hh * 8:(hh + 1) * 8, :], clf[l, si, hh * 1024:(hh + 1) * 1024, :].rearrange("(t p) x -> p t x", p=128), r=[], w=["cl_tok"])
                cpast = tmp[0:8, 0:8, :]
                cflat = lambda a, b_: tmp[0:8, a:b_, :]
                PK = [("tmp", x) for x in range(8)]
                for g in range(4):
                    b = bankA()
                    for j in range(4):
                        kb = g * 4 + j
                        tr(b, PS[b][0:8, j * 128:(j + 1) * 128], cl_tok[:, kb, :], ident[:], r=["cl_tok", "ident"])
                    cp(tmp[0:8, 2 * g:2 * g + 2, :], PS[b][0:8, :].rearrange("p (a t) -> p a t", t=TP), [("ps", b)], PK)
                for a in range(8):
                    ini = 0.0 if a == 0 else tmp[0:8, a - 1, TP - 1:TP]
                    A("dve", lambda e, a=a, ini=ini: e.tensor_tensor_scan(
                        out=tmp[0:8, a, :], data0=onec[0:8, 0:1].to_broadcast([8, TP]), data1=tmp[0:8, a, :],
                        initial=ini, op0=ALU.mult, op1=ALU.add), r=PK + ["onec"], w=PK)
                cp(sm[0:8, 17, 0:1], tmp[0:8, 7, TP - 1:TP], PK, ["smi17"])
                ts(tmp[0:8, 0:8, :], tmp[0:8, 0:8, :], -1.0, sm[0:8, 17, 0:1], ALU.mult, ALU.add, PK + ["smi17"], PK)
                b = bankA()
                for kb in range(16):
                    tr(b, PS[b][:, kb * 8:(kb + 1) * 8], tmp[0:8, kb // 2, (kb % 2) * 128:(kb % 2 + 1) * 128], ident[0:8, 0:8], r=PK + ["ident"])
                cp(negcp[:, :, :], PS[b][:, 0:128].rearrange("p (k x) -> p k x", x=8), [("ps", b)], ["negcp"])
                bo = 6
                br_ = 7
                for kb in range(17):
                    past = kb < 16
                    np_ = 128 if past else 16
                    if past:
                        ks = nxt("w", 2)
                        kcs = 0
                        dma("pool", kc[:, kcs, :], ck[l, si, kb * 128:(kb + 1) * 128, :], r=[], w=[("kc", kcs)])
                        dma("pool", vc[:, kcs, :], cv[l, si, kb * 128:(kb + 1) * 128, :], r=[], w=[("vc", kcs)])
                        for g in range(2):
                            b = bankA()
                            for j in range(4):
                                h = g * 4 + j
                                mm(b, PS[b][:, j * 128:(j + 1) * 128], kc[:, kcs, h * 128:(h + 1) * 128], identb[:], True, True, r=[("kc", kcs), "identb"])
                            cp(kTc[:, g * 4:g * 4 + 4, :], PS[b][:, :].rearrange("p (j t) -> p j t", t=128), [("ps", b)], [("kTc", g)], eng="act" if g else "dve")
                    bs = bankA()
                    for h in range(H):
                        if past:
                            mm(bs, PS[bs][:, h * 16:(h + 1) * 16], kTc[:, h, :], qT[:, h, q0:q0 + 16], True, False, r=[("kTc", h // 4), ("qT", h)])
                            mm(bs, PS[bs][:, h * 16:(h + 1) * 16], sel2[:, h, :], chl[:, q0:q0 + 16], False, True, r=["sel2", "chl"])
                        else:
                            mm(bs, PS[bs][0:16, h * 16:(h + 1) * 16], kTs[:, h, q0:q0 + 16], qT[:, h, q0:q0 + 16], True, False, r=[("kTs", h), ("qT", h)])
                            mm(bs, PS[bs][0:16, h * 16:(h + 1) * 16], sel2[:, h, 0:16], chl[:, q0:q0 + 16], False, False, r=["sel2", "chl"])
                            mm(bs, PS[bs][0:16, h * 16:(h + 1) * 16], identb[0:16, 0:16], negmask[0:16, 0:16], False, True, r=["identb", "negmask"])
                    pi = nxt("p", 3)
                    for h in range(H):
                        bias = negcp[:, kb, h:h + 1] if past else negcs[:, si, h:h + 1]
                        act(Pt[0:np_, pi, h * 16:(h + 1) * 16], PS[bs][0:np_, h * 16:(h + 1) * 16], AF.Exp, [("ps", bs), "negcp", "negcs"], [("Pt", pi)], bias=bias, scale=1.0)
                    for h in range(H):
                        vl = vc[:, 0, h * 128:(h + 1) * 128] if past else vbs[:, si, h * 128:(h + 1) * 128]
                        mm(bo, PS[bo][:, h * 16:(h + 1) * 16], vl, Pt[0:np_, pi, h * 16:(h + 1) * 16], kb == 0, kb == 16,
                           r=[("vc", 0), "vbs", ("Pt", pi)])
                    mm(br_, PS[br_][:, 0:128], ones_bf[0:np_, :], Pt[0:np_, pi, 0:128], kb == 0, kb == 16, r=["ones_bf", ("Pt", pi)])
                A("dve", lambda e: e.reciprocal(out=tmp[:, 8, 0:128], in_=PS[br_][:, 0:128]), r=[("ps", br_)], w=[("tmp", 8)])
                tt(tmp[:, 9, 0:128], PS[bo][:, 0:128], tmp[:, 8, 0:128], ALU.mult, [("ps", bo), ("tmp", 8)], [("tmp", 9)])
                s2 = nxt("s", 2)
                act(sqb[:, s2, 0:128], tmp[:, 9, 0:128], AF.Square, [("tmp", 9)], [("sqb", s2)])
                for h in range(H):
                    mm(4, PS[4][:, q0:q0 + 16], ones_bf[:], sqb[:, s2, h * 16:(h + 1) * 16], h == 0, h == H - 1, r=["ones_bf", ("sqb", s2)])
                    ts(aT[:, h, q0:q0 + 16], tmp[:, 9, h * 16:(h + 1) * 16], vcol[:, 48 + h:49 + h], 0.0, ALU.mult, ALU.add, [("tmp", 9), "vcol"], [("aT", h)])

        for l in range(NL):
            layer_setup(l)
            for ti in range(NT):
                do_tile(l, "p", ti)
            if SAMPLE:
                do_tile(l, "s", 0)
            outs = [(0, 0, srp[l]), (0, 1, sip[l])]
            if SAMPLE:
                outs += [(1, 0, srs[l, 0]), (1, 1, sis[l, 0]), (2, 0, srs[l, 1]), (2, 1, sis[l, 1])]
            for oi, (slot, ri, dst) in enumerate(outs):
                b = bankA()
                tr(b, PS[b][0:32, 0:128], Xst[:, slot, ri, :], ident[:], r=["Xst", "ident"])
                cp(stg[0:32, 0, oi * 128:(oi + 1) * 128], PS[b][0:32, 0:128], [("ps", b)], [("stg", 0)])
                dma("sp", dst, stg[0:32, 0, oi * 128:(oi + 1) * 128], r=[("stg", 0)], w=[])
        print('SBUF remaining', nc.sbuf_bytes_remaining)
        pr.finalize(es)
    return nc


_CACHE = {}
CFG = {}


def kernel(**inp):
    f = lambda a: np.ascontiguousarray(a, dtype=np.float32)
    nc = _CACHE.get("nc")
    if nc is None:
        nc = build(**{k: v for k, v in CFG.items() if k != 'NCORES'})
        _CACHE["nc"] = nc
    wnames = ["g_mix", "w_in", "b_f", "g_q", "g_k", "a_re", "a_im", "log_dt", "b_re", "b_im", "c_re", "c_im", "d_skip",
              "w_glu", "b_glu", "g_attn_out", "g_ssm_out", "w_out", "g_ffn", "w_gate", "w_up", "w_down", "g_ple",
              "w_ple_gate", "w_ple_proj"]
    shared = {n: f(inp[n]) for n in wnames}
    wi = shared["w_in"]
    shared["w_in"] = np.ascontiguousarray(np.concatenate([wi[..., :3072], wi[..., 3080:], wi[..., 3072:3080]], axis=-1))
    in_maps = []
    NCO = CFG.get('NCORES', 8)
    for c in range(NCO):
        m = dict(shared)
        m["xp"] = f(inp["x_prompt"][c])
        m["xs"] = f(inp["x_sample"][2 * c:2 * c + 2].reshape(TS, D))
        m["ck"] = f(inp["cache_k"][:, 2 * c:2 * c + 2].reshape(2, 2, SEQ, 1024))
        m["cv"] = f(inp["cache_v"][:, 2 * c:2 * c + 2].reshape(2, 2, SEQ, 1024))
        m["clf"] = f(inp["cache_logf"][:, 2 * c:2 * c + 2])
        m["sre"] = f(inp["state_ssm_re"][:, 2 * c:2 * c + 2])
        m["sim"] = f(inp["state_ssm_im"][:, 2 * c:2 * c + 2])
        m["pp"] = f(inp["p_prompt"][:, c])
        m["psm"] = f(inp["p_sample"][:, 2 * c:2 * c + 2].reshape(2, TS, 256))
        in_maps.append(m)
    res = run_bass_kernel_spmd(nc, in_maps, core_ids=list(range(NCO)))
    R = list(res.results)
    while len(R) < 8:
        R.append(R[0])
    cat = lambda k: np.stack([r[k] for r in R])
    y_prompt = cat("yp")
    y_sample = cat("ys").reshape(16, 16, D)
    k_prompt = cat("kp").transpose(1, 0, 2, 3).reshape(2, 8, SEQ, 8, 128)
    v_prompt = cat("vp").transpose(1, 0, 2, 3).reshape(2, 8, SEQ, 8, 128)
    logf_prompt = cat("lfp").transpose(1, 0, 2, 3)
    ssm_re_prompt = cat("srp").transpose(1, 0, 2, 3).reshape(2, 8, 64, 64)
    ssm_im_prompt = cat("sip").transpose(1, 0, 2, 3).reshape(2, 8, 64, 64)
    k_sample = cat("kso").transpose(1, 0, 2, 3).reshape(2, 16, 16, 8, 128)
    v_sample = cat("vso").transpose(1, 0, 2, 3).reshape(2, 16, 16, 8, 128)
    logf_sample = cat("lfs").transpose(1, 0, 2, 3).reshape(2, 16, 16, 8)
    ssm_re_sample = cat("srs").transpose(1, 0, 2, 3, 4).reshape(2, 16, 64, 64)
    ssm_im_sample = cat("sis").transpose(1, 0, 2, 3, 4).reshape(2, 16, 64, 64)
    return (y_prompt, y_sample, k_prompt, v_prompt, logf_prompt, ssm_re_prompt, ssm_im_prompt,
            k_sample, v_sample, logf_sample, ssm_re_sample, ssm_im_sample)
```

```python
import math
import os
import numpy as np
from contextlib import ExitStack
import concourse.bass as bass
import concourse.mybir as mybir
from concourse.bass_utils import run_bass_kernel_spmd

F32 = mybir.dt.float32
BF16 = mybir.dt.bfloat16
I32 = mybir.dt.int32
AF = mybir.ActivationFunctionType
ALU = mybir.AluOpType

D = 2048
KT = 16
H = 8
DFF = 5632
FT = 44
INC = 4104
SEQ = 2048
TP = 256
NBT = TP // 128
TS = 32
EPS = 1e-6
TWO_PI = 2.0 * math.pi
NRING = 16


class Op:
    __slots__ = ("eng", "fn", "deps", "dma", "stream", "seq", "waits", "mark", "val", "snap")

    def __init__(self, eng, fn, deps, dma):
        self.eng = eng
        self.fn = fn
        self.deps = deps
        self.dma = dma
        self.mark = False
        self.waits = []


class Prog:
    def __init__(self, nc):
        self.nc = nc
        self.ops = []
        self.lastw = {}
        self.readers = {}
        self.cur = []

    def add(self, eng, fn, r=(), w=(), dma=False):
        self.cur.append((eng, fn, tuple(r), tuple(w), dma))

    def _resolve(self):
        for (eng, fn, r, w, dma) in self.cur:
            idx = len(self.ops)
            deps = set()
            for k in r:
                lw = self.lastw.get(k)
                if lw is not None:
                    deps.add(lw)
            for k in w:
                lw = self.lastw.get(k)
                if lw is not None:
                    deps.add(lw)
                rd = self.readers.get(k)
                if rd:
                    deps.update(rd.values())
            sk = (eng, idx) if dma else eng
            for k in r:
                self.readers.setdefault(k, {})[sk] = idx
            for k in w:
                self.lastw[k] = idx
                self.readers[k] = {}
            self.ops.append(Op(eng, fn, deps, dma))

    def finalize(self, es):
        nc = self.nc
        self._resolve()
        ops = self.ops
        engs = ["pe", "act", "dve", "pool", "sp"]
        cnt = {e: 0 for e in engs}
        dcnt = {e: 0 for e in engs}
        for o in ops:
            if o.dma:
                n = dcnt[o.eng]
                dcnt[o.eng] += 1
                o.stream = (o.eng, n % NRING)
                o.seq = n // NRING + 1
            else:
                cnt[o.eng] += 1
                o.stream = o.eng
                o.seq = cnt[o.eng]
        clock = {e: {} for e in engs}
        for o in ops:
            ck = clock[o.eng]
            need = {}
            if o.dma and o.seq > 1:
                need[o.stream] = (o.seq - 1, None)
            for j in o.deps:
                p = ops[j]
                if p.stream == "pe" and o.eng == "pe" and not o.dma:
                    continue
                cur = need.get(p.stream)
                if cur is None or cur[0] < p.seq:
                    need[p.stream] = (p.seq, j)
            for s, (q, j) in need.items():
                if ck.get(s, 0) >= q:
                    continue
                o.waits.append((s, q, j))
                ck[s] = q
                if j is not None:
                    ops[j].mark = True
                    for s2, q2 in ops[j].snap.items():
                        if ck.get(s2, 0) < q2:
                            ck[s2] = q2
            o.snap = dict(ck)
        rank = {e: 0 for e in engs}
        for o in ops:
            if o.dma:
                o.val = 16 * o.seq
            elif o.mark:
                rank[o.eng] += 1
                o.val = rank[o.eng]
        sems = {}
        for e in ["pe", "act", "dve", "pool"]:
            sems[e] = es.enter_context(nc.semaphore("c_" + e))
        for e in ["pool", "sp"]:
            for i in range(NRING):
                sems[(e, i)] = es.enter_context(nc.semaphore("d_%s%d" % (e, i)))
        streams = {e: [o for o in ops if o.eng == e] for e in engs}
        final = {}
        for o in ops:
            if o.dma:
                final[o.stream] = max(final.get(o.stream, 0), o.val)

        def emit(e, eng):
            for o in streams[e]:
                for (s, q, j) in o.waits:
                    if j is not None:
                        v = ops[j].val
                    else:
                        v = 16 * q
                    eng.wait_ge(sems[s], v)
                ins = o.fn(eng)
                if o.dma:
                    ins.then_inc(sems[o.stream], 16)
                elif o.mark:
                    ins.then_inc(sems[o.eng], 1)
            if e == "sp":
                for s, v in final.items():
                    eng.wait_ge(sems[s], v)

        block = es.enter_context(nc.Block())

        @block.tensor
        def _(e):
            emit("pe", e)

        @block.scalar
        def _(e):
            emit("act", e)

        @block.vector
        def _(e):
            emit("dve", e)

        @block.gpsimd
        def _(e):
            emit("pool", e)

        @block.sync
        def _(e):
            emit("sp", e)


def build(NL=2, NT=SEQ // TP, SAMPLE=True, UPTO=99):
    nc = bass.Bass("TRN2", target_bir_lowering=False)

    def din(name, shape):
        return nc.dram_tensor(name, list(shape), F32, kind="ExternalInput").ap()

    def dout(name, shape):
        return nc.dram_tensor(name, list(shape), F32, kind="ExternalOutput").ap()

    xp = din("xp", [SEQ, D])
    xs = din("xs", [TS, D])
    ck = din("ck", [2, 2, SEQ, 1024])
    cv = din("cv", [2, 2, SEQ, 1024])
    clf = din("clf", [2, 2, SEQ, 8])
    sre = din("sre", [2, 2, 64, 64])
    sim = din("sim", [2, 2, 64, 64])
    pp = din("pp", [2, SEQ, 256])
    psm = din("psm", [2, TS, 256])
    g_mix = din("g_mix", [2, D])
    w_in = din("w_in", [2, D, INC])
    b_f = din("b_f", [2, 8])
    g_q = din("g_q", [2, 128])
    g_k = din("g_k", [2, 128])
    a_re = din("a_re", [2, 64, 64])
    a_im = din("a_im", [2, 64, 64])
    log_dt = din("log_dt", [2, 64])
    b_re = din("b_re", [2, 64, 64, 16])
    b_im = din("b_im", [2, 64, 64, 16])
    c_re = din("c_re", [2, 64, 16, 64])
    c_im = din("c_im", [2, 64, 16, 64])
    d_skip = din("d_skip", [2, 1024])
    w_glu = din("w_glu", [2, 1024, 1024])
    b_glu = din("b_glu", [2, 1024])
    g_ao = din("g_attn_out", [2, 1024])
    g_so = din("g_ssm_out", [2, 1024])
    w_out = din("w_out", [2, D, D])
    g_ffn = din("g_ffn", [2, D])
    w_gate = din("w_gate", [2, D, DFF])
    w_up = din("w_up", [2, D, DFF])
    w_down = din("w_down", [2, DFF, D])
    g_ple = din("g_ple", [2, D])
    w_pg = din("w_ple_gate", [2, D, D])
    w_pp = din("w_ple_proj", [2, 256, D])

    yp = dout("yp", [SEQ, D])
    ys = dout("ys", [TS, D])
    kp = dout("kp", [2, SEQ, 1024])
    vp = dout("vp", [2, SEQ, 1024])
    lfp = dout("lfp", [2, SEQ, 8])
    srp = dout("srp", [2, 32, 128])
    sip = dout("sip", [2, 32, 128])
    kso = dout("kso", [2, TS, 1024])
    vso = dout("vso", [2, TS, 1024])
    lfs = dout("lfs", [2, TS, 8])
    srs = dout("srs", [2, 2, 32, 128])
    sis = dout("sis", [2, 2, 32, 128])
    hres = nc.dram_tensor("hres", [KT, 128, SEQ + TS], F32, kind="Internal").ap()
    wsc = nc.dram_tensor("wsc", [2, 56, 128, 8192], BF16, kind="Internal").ap()
    tabsc = nc.dram_tensor("tabsc", [32, 128, 2, TP], F32, kind="Internal").ap()

    es = ExitStack()
    with es:
        def sb(name, shape, dt=F32):
            return es.enter_context(nc.sbuf_tensor(name, list(shape), dt))

        pr = Prog(nc)
        A = pr.add

        ident = sb("ident", [128, 128])
        ones_bf = sb("ones_bf", [128, 128], BF16)
        negmask = sb("negmask", [128, 128], BF16)
        identb = sb("identb", [128, 128], BF16)
        sel2 = sb("sel2", [64, 8, 128], BF16)
        tau = sb("tau", [128, TP])
        taus = sb("taus", [128, TS])
        KTb = sb("KTb", [128, H, SEQ], BF16)
        Vb = sb("Vb", [128, 16, 1024], BF16)
        negc = sb("negc", [128, 16, 8])
        vcol = sb("vcol", [128, 96])
        bfc = sb("bfc", [40, 1])
        wf2 = sb("wf2", [128, KT, 40], BF16)
        wppb = sb("wppb", [128, 2, 2, 512], BF16)
        magT = sb("magT", [128, 32])
        thn = sb("thn", [128, 32])
        cth = sb("cth", [128, 32])
        sth = sb("sth", [128, 32])
        winit = sb("winit", [128, 3, 2, 32])
        Bc = sb("Bc", [128, 8, 2, 128], BF16)
        Cc = sb("Cc", [128, 32, 2, 32], BF16)
        Xst = sb("Xst", [128, 3, 2, 32])
        ccar = sb("ccar", [40, 3])
        sm = sb("sm", [128, 24, 32])
        smi = sb("smi", [128, 32], I32)
        xT = sb("xT", [128, KT, TP])
        aT = sb("aT", [128, KT, TP], BF16)
        wsl = [sb("wsl%d" % i, [128, 8192], BF16) for i in range(2)]
        rstd = sb("rstd", [128, 3, TP])
        sqb = sb("sqb", [128, 2, TP], BF16)
        reg = sb("reg", [128, 8 * TP])
        uT = reg[:, :].rearrange("p (a t) -> p a t", t=TP)
        regb = sb("regb", [128, 22 * TP], BF16)
        ub = regb[:, 0:8 * TP].rearrange("p (a t) -> p a t", t=TP)
        qT = regb[:, 8 * TP:16 * TP].rearrange("p (a t) -> p a t", t=TP)
        Pt = regb[:, 16 * TP:19 * TP].rearrange("p (a t) -> p a t", t=TP)
        xrb = regb[:, 19 * TP:21 * TP].rearrange("p (a t) -> p a t", t=TP)
        actT = regb[:, :].rearrange("p (a t) -> p a t", t=TP)
        tmp = sb("tmp", [128, 11, TP])
        tmpi = sb("tmpi", [128, TP], I32)
        tmpB = sb("tmpB", [128, 2, TP])
        chl = sb("chl", [64, TP], BF16)
        stg = sb("stg", [128, 2, 1024])
        pT = sb("pT", [128, 2, TP], BF16)
        PS = [es.enter_context(nc.psum_tensor("ps%d" % i, [128, 512], F32)) for i in range(8)]

        rot = {"a": 0, "w": 0, "r": 0, "t": 0, "p": 0, "s": 0, "banks": [0, 1, 2, 3], "sbanks": [4, 5]}

        def bankA():
            lst = rot["banks"]
            rot["a"] = (rot["a"] + 1) % len(lst)
            return lst[rot["a"]]

        def nxt(name, n):
            rot[name] = (rot[name] + 1) % n
            return rot[name]

        def mm(b, out, lhsT, rhs, start, stop, r, tp=None, w=None):
            kw = {}
            if tp is not None:
                kw["tile_position"] = tp
            A("pe", lambda e: e.matmul(out, lhsT=lhsT, rhs=rhs, start=start, stop=stop, **kw),
              r=r, w=[("ps", b)] if w is None else w)

        def tr(b, out, in_, idn, r):
            A("pe", lambda e: e.transpose(out=out, in_=in_, identity=idn), r=r, w=[("ps", b)])

        def act(out, in_, func, r, w, bias=None, scale=None):
            kw = {}
            if bias is not None:
                kw["bias"] = bias
            if scale is not None:
                kw["scale"] = scale
            A("act", lambda e: e.activation(out=out, in_=in_, func=func, **kw), r=r, w=w)

        def tt(out, in0, in1, op, r, w, eng="dve"):
            A(eng, lambda e: e.tensor_tensor(out=out, in0=in0, in1=in1, op=op), r=r, w=w)

        def ts(out, in0, s1, s2, op0, op1, r, w, eng="dve"):
            A(eng, lambda e: e.tensor_scalar(out=out, in0=in0, scalar1=s1, scalar2=s2, op0=op0, op1=op1), r=r, w=w)

        def stt(out, in0, scalar, in1, op0, op1, r, w, eng="dve"):
            A(eng, lambda e: e.scalar_tensor_tensor(out=out, in0=in0, scalar=scalar, in1=in1, op0=op0, op1=op1), r=r, w=w)

        def cp(out, in_, r, w, eng="dve"):
            if eng == "act":
                A(eng, lambda e: e.activation(out=out, in_=in_, func=AF.Copy), r=r, w=w)
            else:
                A(eng, lambda e: e.tensor_copy(out=out, in_=in_), r=r, w=w)

        def ms(ap, val, w, eng="dve"):
            A(eng, lambda e: e.memset(ap, val), r=[], w=w)

        def dma(q, out, in_, r, w):
            A(q, lambda e: e.dma_start(out=out, in_=in_), r=r, w=w, dma=True)

        ms(ident[:], 0.0, ["ident"], eng="pool")
        A("pool", lambda e: e.affine_select(out=ident[:], in_=ident[:], compare_op=ALU.not_equal, fill=1.0,
                                            base=0, pattern=[[-1, 128]], channel_multiplier=1),
          r=["ident"], w=["ident"])
        ms(ones_bf[:], 1.0, ["ones_bf"])
        cp(identb[:], ident[:], ["ident"], ["identb"])
        ms(negmask[:], 0.0, ["negmask"], eng="pool")
        A("pool", lambda e: e.affine_select(out=negmask[:], in_=negmask[:], compare_op=ALU.is_ge, fill=-30000.0,
                                            base=0, pattern=[[1, 128]], channel_multiplier=-1),
          r=["negmask"], w=["negmask"])
        A("pool", lambda e: e.iota(tmpi[:], pattern=[[1, TP]], base=0, channel_multiplier=0), r=[], w=["tmpi"])
        cp(tau[:], tmpi[:], ["tmpi"], ["tau"])
        cp(taus[:, 0:16], tmpi[:, 0:16], ["tmpi"], ["tau"])
        cp(taus[:, 16:32], tmpi[:, 0:16], ["tmpi"], ["tau"])
        ms(sel2[:], 0.0, ["sel2"])
        ms(chl[:], 0.0, ["chl"])
        for h in range(H):
            cp(sel2[0:8, h, :], ident[0:8, h:h + 1].to_broadcast([8, 128]), ["ident", "sel2"], ["sel2"])
            cp(sel2[32:40, h, :], ident[32:40, 32 + h:33 + h].to_broadcast([8, 128]), ["ident", "sel2"], ["sel2"])
        ms(Xst[:], 0.0, ["Xst"])
        ms(ccar[:], 0.0, ["ccar"])

        def rstd_from(src, skeys, rs, Tn, nfeat):
            act(rstd[:, rs, :Tn], src, AF.Ln, r=skeys + ["epsc"], w=[("rstd", rs)], bias=epsc[:, 0:1], scale=1.0 / nfeat)
            act(rstd[:, rs, :Tn], rstd[:, rs, :Tn], AF.Exp, r=[("rstd", rs)], w=[("rstd", rs)], scale=-0.5)

        def rms_from_sq(sq_list, Tn, nfeat, rkeys):
            sbl = rot["sbanks"]
            rot["r"] = (rot["r"] + 1) % len(sbl)
            b = sbl[rot["r"]]
            n = len(sq_list)
            for i, (ap, keys) in enumerate(sq_list):
                s = nxt("s", 2)
                act(sqb[:, s, :Tn], ap, AF.Square, r=keys, w=[("sqb", s)])
                mm(b, PS[b][:, :Tn], ones_bf[:], sqb[:, s, :Tn], i == 0, i == n - 1, r=["ones_bf", ("sqb", s)])
            rs = nxt("t", 3)
            rstd_from(PS[b][:, :Tn], [("ps", b)], rs, Tn, nfeat)
            return rs

        epsc = sb("epsc", [128, 1])
        ms(epsc[:], EPS, ["epsc"])

        def norm_x(l, Tn, gofs):
            rs = rms_from_sq([(xT[:, kt, :Tn], [("xT", kt)]) for kt in range(KT)], Tn, D, None)
            for kt in range(KT):
                stt(aT[:, kt, :Tn], xT[:, kt, :Tn], vcol[:, gofs + kt:gofs + kt + 1], rstd[:, rs, :Tn], ALU.mult, ALU.mult,
                    r=[("xT", kt), "vcol", ("rstd", rs)], w=[("aT", kt)])

        bidc = {}
        cur = {"l": 0, "first": True}

        def wblock(name, parts):
            s = nxt("w", 2)
            l = cur["l"]
            bid = bidc.setdefault(name, len(bidc))
            total = max(c0 + nk * ncol for (c0, nk, ncol, _) in parts)
            if cur["first"] or os.environ.get("NOWSC"):
                for pi_, (c0, nk, ncol, src3) in enumerate(parts):
                    view = wsl[s][:, c0:c0 + nk * ncol].rearrange("p (k n) -> p k n", n=ncol)
                    dma("pool", view, src3, r=[], w=[("w", s, pi_)])
                if not os.environ.get("NOWSC"):
                    dma("sp", wsc[l, bid, :, 0:total], wsl[s][:, 0:total], r=WK(s), w=[("wsc", l, bid)])
            else:
                half = total // 2
                dma("pool", wsl[s][:, 0:half], wsc[l, bid, :, 0:half], r=[("wsc", l, bid)], w=[("w", s, 0)])
                dma("pool", wsl[s][:, half:total], wsc[l, bid, :, half:total], r=[("wsc", l, bid)], w=[("w", s, 1)])
            return s

        def wload(name, src3, nk, ncol):
            step = nk // 4
            parts = [(k0 * ncol, step, ncol, src3[:, k0:k0 + step, :]) for k0 in range(0, nk, step)]
            s = wblock(name, parts)
            return s, wsl[s][:, 0:nk * ncol].rearrange("p (k n) -> p k n", n=ncol)

        def WK(s):
            return [("w", s, i) for i in range(4)]

        def wview(w2d, c0, ncol):
            return w2d.rearrange("(k p) n -> p k n", p=128)[:, :, c0:c0 + ncol]

        def layer_setup(l):
            rows = [(g_mix, 0, 16), (g_ffn, 16, 16), (g_ple, 32, 16), (g_ao, 48, 8), (g_so, 56, 8),
                    (d_skip, 64, 8), (b_glu, 72, 8), (g_q, 80, 1), (g_k, 81, 1)]
            st = stg[:, 0, 0:128]
            for (t, r0, n) in rows:
                dma("sp", stg[r0:r0 + n, 0, 0:128], t[l].rearrange("(r c) -> r c", c=128), r=[], w=["stg"])
            b = bankA()
            tr(b, PS[b][:, 0:82], st[0:82, :], ident[0:82, 0:82], r=["stg", "ident"])
            cp(vcol[:, 0:82], PS[b][:, 0:82], [("ps", b)], ["vcol"])
            ts(vcol[:, 82:83], vcol[:, 80:81], 128.0 ** -0.5, 0.0, ALU.mult, ALU.add, ["vcol"], ["vcol"])
            ms(bfc[:], 0.0, ["bfc"])
            dma("sp", bfc[0:8, :], b_f[l].rearrange("(r c) -> r c", c=1), r=[], w=["bfc"])
            dma("sp", bfc[32:40, :], b_f[l].rearrange("(r c) -> r c", c=1), r=[], w=["bfc"])
            ts(bfc[:], bfc[:], -1.0, 0.0, ALU.mult, ALU.add, ["bfc"], ["bfc"])
            ms(wf2[:], 0.0, ["wf2"])
            wfsrc = wview(w_in[l], 4096, 8)
            dma("pool", wf2[:, :, 0:8], wfsrc, r=[], w=["wf2"])
            dma("pool", wf2[:, :, 32:40], wfsrc, r=[], w=["wf2"])
            sA = stg[:, 1, 0:128]
            dma("sp", stg[0:32, 1, 0:128], a_re[l].rearrange("(gp gl) p -> gp (gl p)", gl=2), r=[], w=["stgA"])
            dma("sp", stg[32:64, 1, 0:128], a_im[l].rearrange("(gp gl) p -> gp (gl p)", gl=2), r=[], w=["stgA"])
            dma("sp", stg[64:96, 1, 128:130], log_dt[l].rearrange("(gp gl) -> gp gl", gl=2), r=[], w=["stgA"])
            for gl in range(2):
                cp(stg[64:96, 1, gl * 64:(gl + 1) * 64], stg[64:96, 1, 128 + gl:129 + gl].to_broadcast([32, 64]),
                   ["stgA"], ["stgA"])
            b = bankA()
            tr(b, PS[b][:, 0:96], sA[0:96, :], ident[0:96, 0:96], r=["stgA", "ident"])
            S = lambda i: sm[:, i, :]
            K = ["sm"]
            cp(sm[:, 0:3, :], PS[b][:, 0:96].rearrange("p (a g) -> p a g", g=32), [("ps", b)], K)
            act(S(3), S(2), AF.Exp, K, K)
            tt(S(4), S(0), S(3), ALU.mult, K, K)
            act(magT[:], S(4), AF.Exp, K, ["magT"])
            tt(S(5), S(1), S(3), ALU.mult, K, K)
            ts(thn[:], S(5), 1.0 / TWO_PI, 0.0, ALU.mult, ALU.add, K, ["thn"])
            cp(smi[:], thn[:], ["thn"], ["smi"])
            tt(S(6), thn[:], smi[:], ALU.subtract, ["thn", "smi"], K)
            act(S(7), S(6), AF.Sin, K, K, scale=TWO_PI)
            act(S(8), S(6), AF.Sin, K, K, scale=math.pi)
            tt(S(9), S(8), S(8), ALU.mult, K, K)
            ts(S(9), S(9), -2.0, 1.0, ALU.mult, ALU.add, K, K)
            cp(cth[:], S(9), K, ["cth"])
            cp(sth[:], S(7), K, ["sth"])
            tt(S(10), magT[:], S(9), ALU.mult, K + ["magT"], K)
            tt(S(11), magT[:], S(7), ALU.mult, K + ["magT"], K)
            ts(S(10), S(10), -1.0, 0.0, ALU.add, ALU.add, K, K)
            tt(S(12), S(0), S(0), ALU.mult, K, K)
            tt(S(13), S(1), S(1), ALU.mult, K, K)
            tt(S(12), S(12), S(13), ALU.add, K, K)
            A("dve", lambda e: e.reciprocal(out=S(12), in_=S(12)), r=K, w=K)
            tt(S(13), S(10), S(0), ALU.mult, K, K)
            tt(S(14), S(11), S(1), ALU.mult, K, K)
            tt(S(13), S(13), S(14), ALU.add, K, K)
            tt(S(13), S(13), S(12), ALU.mult, K, K)
            tt(S(14), S(11), S(0), ALU.mult, K, K)
            tt(S(15), S(10), S(1), ALU.mult, K, K)
            tt(S(14), S(14), S(15), ALU.subtract, K, K)
            tt(S(14), S(14), S(12), ALU.mult, K, K)
            tmpf = tmp[:, :, :].rearrange("p a t -> p (a t)")
            v3 = lambda i: tmpf[:, i * 512:(i + 1) * 512].rearrange("p (g c) -> p g c", c=16)
            br, bi, t2, t3, t4 = v3(0), v3(1), v3(2), v3(3), v3(4)
            TA = [("tmp", x) for x in range(10)]
            for gl in range(2):
                dma("sp", br[gl * 64:(gl + 1) * 64], b_re[l].rearrange("(gp gl) p c -> gl p gp c", gl=2)[gl], r=[], w=TA)
                dma("sp", bi[gl * 64:(gl + 1) * 64], b_im[l].rearrange("(gp gl) p c -> gl p gp c", gl=2)[gl], r=[], w=TA)
            crb = sm[:, 13:14, :].rearrange("p a g -> p g a").to_broadcast([128, 32, 16])
            cib = sm[:, 14:15, :].rearrange("p a g -> p g a").to_broadcast([128, 32, 16])
            KK = TA + ["sm"]
            tt(t2, br, crb, ALU.mult, KK, TA)
            tt(t3, bi, cib, ALU.mult, KK, TA)
            tt(t2, t2, t3, ALU.subtract, TA, TA)
            tt(t3, bi, crb, ALU.mult, KK, TA)
            tt(t4, br, cib, ALU.mult, KK, TA)
            tt(t3, t3, t4, ALU.add, TA, TA)
            RG = ["regS"] + [("uT", x) for x in range(8)]
            srcB = reg[:, 0:2048].rearrange("p (q r x) -> p q r x", r=2, x=128)
            ms(reg[:, 0:2048], 0.0, RG)
            for ri, tsrc in enumerate((t2, t3)):
                for gl in range(2):
                    dst = srcB[gl * 64:(gl + 1) * 64, :, ri, :].rearrange("p q (j x) -> p q j x", x=32)[:, :, :, gl * 16:gl * 16 + 16]
                    cp(dst, tsrc[gl * 64:(gl + 1) * 64].rearrange("p (q j) c -> p q j c", j=4), TA + RG, RG)
            for q in range(8):
                b = bankA()
                for ri in range(2):
                    tr(b, PS[b][:, ri * 128:(ri + 1) * 128], srcB[:, q, ri, :], ident[:], r=RG + ["ident"])
                cp(Bc[:, q, :, :], PS[b][:, 0:256].rearrange("p (r x) -> p r x", x=128), [("ps", b)], ["Bc"])
            stC = reg[0:32, 0:2048].rearrange("p (g x) -> p g x", x=128)
            for ri, csrc in enumerate((c_re, c_im)):
                for g0 in range(0, 32, 16):
                    ms(reg[0:32, 0:2048], 0.0, RG)
                    for gl in range(2):
                        dma("sp", stC[gl * 16:(gl + 1) * 16, :, gl * 64:(gl + 1) * 64],
                            csrc[l].rearrange("(gp gl) c p -> gl c gp p", gl=2)[gl][:, g0:g0 + 16, :], r=[], w=RG)
                    b = bankA()
                    for gg in range(16):
                        tr(b, PS[b][:, gg * 32:(gg + 1) * 32], stC[:, gg, :], ident[0:32, 0:32], r=RG + ["ident"])
                    src = PS[b][:, :].rearrange("p (g x) -> p g x", x=32)
                    if ri == 0:
                        cp(Cc[:, g0:g0 + 16, 0, :], src, [("ps", b)], ["Cc"])
                    else:
                        ts(Cc[:, g0:g0 + 16, 1, :], src, -1.0, 0.0, ALU.mult, ALU.add, [("ps", b)], ["Cc"])
            if SAMPLE:
                for si in range(2):
                    dma("sp", stg[si * 64:si * 64 + 32, 1, 0:128], sre[l, si].rearrange("(gp gl) p -> gp (gl p)", gl=2), r=[], w=["stgA"])
                    dma("sp", stg[si * 64 + 32:si * 64 + 64, 1, 0:128], sim[l, si].rearrange("(gp gl) p -> gp (gl p)", gl=2), r=[], w=["stgA"])
                b = bankA()
                tr(b, PS[b][:, 0:128], sA[:, :], ident[:], r=["stgA", "ident"])
                cp(Xst[:, 1:3, :, :], PS[b][:, 0:128].rearrange("p (s r g) -> p s r g", r=2, g=32), [("ps", b)], ["Xst"])
            ms(Xst[:, 0, :, :], 0.0, ["Xst"])
            ms(ccar[:], 0.0, ["ccar"])

        def do_tile(l, kind, ti):
            prompt = kind == "p"
            Tn = TP if prompt else TS
            nb = NBT if prompt else 1
            bw = 128 if prompt else TS
            tok0 = ti * TP if prompt else SEQ
            segs = [(0, TP, 0)] if prompt else [(0, 16, 1), (16, 16, 2)]
            T = lambda i: tmp[:, i, :Tn]
            TK = lambda *i: [("tmp", x) for x in i]
            TB = lambda i: tmpB[:, i, :Tn]
            TBK = lambda i: [("tmpB", i)]
            st = {}
            cur["l"] = l
            cur["first"] = prompt and ti == 0

            if l == 0:
                src = xp if prompt else xs
                for tb in range(nb):
                    r0 = ti * TP + tb * 128 if prompt else 0
                    for hf in range(2):
                        dma("sp", stg[0:bw, hf, :], src[r0:r0 + bw, hf * 1024:(hf + 1) * 1024], r=[], w=[("stg", hf)])
                    for g in range(4):
                        b = bankA()
                        for j in range(4):
                            kt = 4 * g + j
                            tr(b, PS[b][:, j * bw:(j + 1) * bw], stg[0:bw, kt // 8, (kt % 8) * 128:(kt % 8 + 1) * 128],
                               ident[0:bw, 0:bw], r=[("stg", kt // 8), "ident"])
                        cp(xT[:, 4 * g:4 * g + 4, tb * 128:tb * 128 + bw], PS[b][:, 0:4 * bw].rearrange("p (j t) -> p j t", t=bw),
                           [("ps", b)], [("xT", 4 * g + j) for j in range(4)], eng="act" if g % 2 else "dve")
            else:
                for kt in range(KT):
                    dma("sp", xT[:, kt, :Tn], hres[kt, :, tok0:tok0 + Tn], r=["hres"], w=[("xT", kt)])
            psrc = pp if prompt else psm
            for tb in range(nb):
                r0 = ti * TP + tb * 128 if prompt else 0
                dma("sp", stg[0:bw, 1, 0:256], psrc[l, r0:r0 + bw, :], r=[], w=[("stg", 1)])
                b = bankA()
                for j in range(2):
                    tr(b, PS[b][:, j * bw:(j + 1) * bw], stg[0:bw, 1, j * 128:(j + 1) * 128], ident[0:bw, 0:bw], r=[("stg", 1), "ident"])
                cp(pT[:, :, tb * 128:tb * 128 + bw], PS[b][:, 0:2 * bw].rearrange("p (j t) -> p j t", t=bw), [("ps", b)], ["pT"])

            norm_x(l, Tn, 0)

            if UPTO < 1:
                return
            b = bankA()
            for kt in range(KT):
                mm(b, PS[b][0:40, :Tn], wf2[:, kt, :], aT[:, kt, :Tn], kt == 0, kt == KT - 1, r=["wf2", ("aT", kt)])
            cfa = lambda i: tmp[0:40, i, :Tn]
            act(cfa(0), PS[b][0:40, :Tn], AF.Exp, [("ps", b), "bfc"], TK(0), bias=bfc[:, 0:1], scale=-1.0)
            act(cfa(0), cfa(0), AF.Ln, TK(0), TK(0), bias=onec[0:40, 0:1], scale=1.0)
            ts(cfa(1), cfa(0), -1.0, 0.0, ALU.mult, ALU.add, TK(0), TK(1))
            for (c0, n, slot) in segs:
                A("dve", lambda e, c0=c0, n=n, slot=slot: e.tensor_tensor_scan(
                    out=tmp[0:40, 2, c0:c0 + n], data0=onec[0:40, 0:1].to_broadcast([40, n]), data1=tmp[0:40, 1, c0:c0 + n],
                    initial=ccar[:, slot:slot + 1], op0=ALU.mult, op1=ALU.add), r=TK(1) + ["ccar"], w=TK(2))
                cp(ccar[:, slot:slot + 1], tmp[0:40, 2, c0 + n - 1:c0 + n], TK(2), ["ccar"])
            cp(chl[0:40, :Tn], cfa(2), TK(2), ["chl"])
            tt(cfa(3), cfa(2), chl[0:40, :Tn], ALU.subtract, TK(2) + ["chl"], TK(3))
            cp(chl[32:40, :Tn], tmp[32:40, 3, :Tn], TK(3), ["chl"])
            b = bankA()
            if prompt:
                for tb in range(nb):
                    tr(b, PS[b][:, tb * 16:tb * 16 + 8], tmp[0:8, 1, tb * 128:(tb + 1) * 128], ident[0:8, 0:8], r=TK(1) + ["ident"])
                    tr(b, PS[b][:, tb * 16 + 8:tb * 16 + 16], tmp[0:8, 2, tb * 128:(tb + 1) * 128], ident[0:8, 0:8], r=TK(2) + ["ident"])
                pv = PS[b][:, 0:nb * 16].rearrange("p (t x) -> p t x", x=16)
                lft = tmp[:, 9, 0:nb * 8].rearrange("p (t x) -> p t x", x=8)
                cp(lft, pv[:, :, 0:8], [("ps", b)], TK(9))
                ts(negc[:, ti * NBT:(ti + 1) * NBT, :], pv[:, :, 8:16], -1.0, 0.0, ALU.mult, ALU.add, [("ps", b)], [("negc", ti)])
                dma("sp", lfp[l, ti * TP:(ti + 1) * TP, :].rearrange("(t p) x -> p t x", p=128), lft, r=TK(9), w=[])
            else:
                for si in range(2):
                    tr(b, PS[b][0:16, si * 16:si * 16 + 8], tmp[0:8, 1, si * 16:(si + 1) * 16], ident[0:8, 0:8], r=TK(1) + ["ident"])
                    tr(b, PS[b][0:16, si * 16 + 8:si * 16 + 16], tmp[0:8, 2, si * 16:(si + 1) * 16], ident[0:8, 0:8], r=TK(2) + ["ident"])
                pv = PS[b][0:16, 0:32].rearrange("p (t x) -> p t x", x=16)
                lft = tmp[0:16, 9, 0:16].rearrange("p (t x) -> p t x", x=8)
                cp(lft, pv[:, :, 0:8], [("ps", b)], TK(9))
                ts(negcs[:, :, :], pv[:, :, 8:16], -1.0, 0.0, ALU.mult, ALU.add, [("ps", b)], ["negcs"])
                dma("sp", lfs[l].rearrange("(s p) x -> p s x", p=16), lft, r=TK(9), w=[])

            if UPTO < 2:
                return
            for ub_i in range(2):
                s, wv = wload(("u", ub_i), wview(w_in[l], 3072 + ub_i * 512, 512), KT, 512)
                for j in range(0 if os.environ.get("SKIPMM") else 4):
                    b = bankA()
                    for kt in range(KT):
                        mm(b, PS[b][:, :Tn], wv[:, kt, j * 128:(j + 1) * 128], aT[:, kt, :Tn], kt == 0, kt == KT - 1, r=WK(s) + [("aT", kt)])
                    c = ub_i * 4 + j
                    if os.environ.get("SKIPCP"):
                        continue
                    cp(uT[:, c, :Tn], PS[b][:, :Tn], [("ps", b)], [("uT", c)], eng="act")
                    cp(ub[:, c, :Tn], uT[:, c, :Tn], [("uT", c)], [("ub", c)])

            def part_qkv():
                for which in range(2):
                    for hb in range(2):
                        s, wv = wload(("qk", which, hb), wview(w_in[l], which * 1024 + hb * 512, 512), KT, 512)
                        for j in range(4):
                            h = hb * 4 + j
                            b = bankA()
                            for kt in range(KT):
                                mm(b, PS[b][:, :Tn], wv[:, kt, j * 128:(j + 1) * 128], aT[:, kt, :Tn], kt == 0, kt == KT - 1, r=WK(s) + [("aT", kt)])
                            rs = rms_from_sq([(PS[b][:, :Tn], [("ps", b)])], Tn, 128, None)
                            if which == 0:
                                stt(qT[:, h, :Tn], PS[b][:, :Tn], vcol[:, 82:83], rstd[:, rs, :Tn], ALU.mult, ALU.mult,
                                    r=[("ps", b), "vcol", ("rstd", rs)], w=[("qT", h)])
                            else:
                                ks = nxt("p", 2)
                                kk = TBK(ks)
                                stt(TB(ks), PS[b][:, :Tn], vcol[:, 81:82], rstd[:, rs, :Tn], ALU.mult, ALU.mult,
                                    r=[("ps", b), "vcol", ("rstd", rs)], w=kk)
                                if prompt:
                                    cp(KTb[:, h, ti * TP:(ti + 1) * TP], TB(ks), kk, [("KT", h, ti)], eng="act")
                                else:
                                    cp(kTs[:, h, :], TB(ks), kk, [("kTs", h)], eng="act")
                                b2 = bankA()
                                for tb in range(nb):
                                    tr(b2, PS[b2][0:bw, tb * 128:(tb + 1) * 128], tmpB[:, ks, tb * 128:tb * 128 + bw], ident[:], r=kk + ["ident"])
                                kslot = h % 2
                                cp(kst[0:bw, kslot, 0:nb, :], PS[b2][0:bw, 0:nb * 128].rearrange("p (t x) -> p t x", x=128),
                                   [("ps", b2)], [("kst", kslot)], eng="act")
                                if prompt:
                                    dma("sp", kp[l, ti * TP:(ti + 1) * TP, h * 128:(h + 1) * 128].rearrange("(t p) x -> p t x", p=128),
                                        kst[:, kslot, :, :], r=[("kst", kslot)], w=[])
                                else:
                                    dma("sp", kso[l, :, h * 128:(h + 1) * 128], kst[0:TS, kslot, 0, :], r=[("kst", kslot)], w=[])
                for vb_i in range(2):
                    s, wv = wload(("v", vb_i), wview(w_in[l], 2048 + vb_i * 512, 512), KT, 512)
                    vblocks = [(tb * 128, 128, tb) for tb in range(nb)] if prompt else [(0, 16, 0), (16, 16, 1)]
                    for (c0, n, tb) in vblocks:
                        b = bankA()
                        for kt in range(KT):
                            mm(b, PS[b][0:n, :], aT[:, kt, c0:c0 + n], wv[:, kt, :], kt == 0, kt == KT - 1, r=WK(s) + [("aT", kt)])
                        vs_ = nxt("p", 2)
                        cp(vst[0:n, vs_, :], PS[b][0:n, :], [("ps", b)], [("vst", vs_)], eng="act")
                        if prompt:
                            cp(Vb[:, ti * NBT + tb, vb_i * 512:(vb_i + 1) * 512], vst[:, vs_, :], [("vst", vs_)], [("V", ti)])
                            r0 = ti * TP + c0
                            dma("sp", vp[l, r0:r0 + 128, vb_i * 512:(vb_i + 1) * 512], vst[:, vs_, :], r=[("vst", vs_)], w=[])
                        else:
                            cp(vbs[:, tb, vb_i * 512:(vb_i + 1) * 512], vst[0:16, vs_, :], [("vst", vs_)], ["vbs"])
                            dma("sp", vso[l, c0:c0 + 16, vb_i * 512:(vb_i + 1) * 512], vst[0:16, vs_, :], r=[("vst", vs_)], w=[])

            def part_s5():
                tsrc = tau if prompt else taus
                tabs = [(1, 2), (9, 10)]

                def tables(gp):
                    sn_i, cs_i = tabs[gp % 2]
                    if not cur["first"] and not os.environ.get("NOTABC"):
                        if prompt:
                            dma("sp", tmp[:, sn_i:sn_i + 2, :], tabsc[gp], r=[("tabsc", gp)], w=TK(sn_i, cs_i))
                        else:
                            for c0 in (0, 16):
                                dma("sp", tmp[:, sn_i:sn_i + 2, c0:c0 + 16], tabsc[gp, :, :, 0:16], r=[("tabsc", gp)], w=TK(sn_i, cs_i))
                        return
                    act(T(0), tsrc[:, :Tn], AF.Identity, ["tau", "thn"], TK(0), scale=thn[:, gp:gp + 1])
                    cp(tmpi[:, :Tn], T(0), TK(0), ["tmpi"])
                    tt(T(0), T(0), tmpi[:, :Tn], ALU.subtract, TK(0) + ["tmpi"], TK(0))
                    act(T(sn_i), T(0), AF.Sin, TK(0), TK(sn_i), scale=TWO_PI)
                    act(T(cs_i), T(0), AF.Sin, TK(0), TK(cs_i), scale=math.pi)
                    act(T(cs_i), T(cs_i), AF.Square, TK(cs_i), TK(cs_i))
                    act(T(cs_i), T(cs_i), AF.Identity, TK(cs_i) + ["onec"], TK(cs_i), bias=onec[:, 0:1], scale=-2.0)
                    if cur["first"] and not os.environ.get("NOTABC"):
                        dma("sp", tabsc[gp], tmp[:, sn_i:sn_i + 2, :], r=TK(sn_i, cs_i), w=[("tabsc", gp)])

                def bu(gp):
                    quad, j = gp // 4, gp % 4
                    bk = rot["banks"][gp % 2]
                    mm(bk, PS[bk][:, 0:Tn], Bc[32 * j:32 * j + 32, quad, 0, :], ub[32 * j:32 * j + 32, quad, :Tn], True, True,
                       r=["Bc", ("ub", quad)], tp=(32 * j, 0))
                    mm(bk, PS[bk][:, TP:TP + Tn], Bc[32 * j:32 * j + 32, quad, 1, :], ub[32 * j:32 * j + 32, quad, :Tn], True, True,
                       r=["Bc", ("ub", quad)], tp=(32 * j, 0))

                def body(gp):
                    quad, j = gp // 4, gp % 4
                    sn_i, cs_i = tabs[gp % 2]
                    bk = rot["banks"][gp % 2]
                    bre = PS[bk][:, 0:Tn]
                    bim = PS[bk][:, TP:TP + Tn]
                    BK = [("ps", bk)]
                    SK = ["smi16"]
                    inis = [(c0, n, slot, winit[:, slot, 0, gp:gp + 1], winit[:, slot, 1, gp:gp + 1]) for (c0, n, slot) in segs]
                    tt(T(3), T(cs_i), bre, ALU.mult, TK(cs_i) + BK, TK(3))
                    tt(T(5), T(sn_i), bim, ALU.mult, TK(sn_i) + BK, TK(5))
                    tt(T(4), T(cs_i), bim, ALU.mult, TK(cs_i) + BK, TK(4))
                    tt(T(8), T(sn_i), bre, ALU.mult, TK(sn_i) + BK, TK(8))
                    tt(T(3), T(3), T(5), ALU.add, TK(3, 5), TK(3))
                    tt(T(4), T(4), T(8), ALU.subtract, TK(4, 8), TK(4))
                    for (c0, n, slot, i0, i1) in inis:
                        for (dst, srcw, ini) in ((6, 3, i0), (7, 4, i1)):
                            A("dve", lambda e, dst=dst, srcw=srcw, ini=ini, c0=c0, n=n, gp=gp: e.tensor_tensor_scan(
                                out=tmp[:, dst, c0:c0 + n], data0=magT[:, gp:gp + 1].to_broadcast([128, n]),
                                data1=tmp[:, srcw, c0:c0 + n], initial=ini, op0=ALU.mult, op1=ALU.add),
                              r=TK(srcw) + ["winit", "magT"], w=TK(dst))
                    tt(T(8), T(cs_i), T(6), ALU.mult, TK(cs_i, 6), TK(8))
                    tt(T(5), T(sn_i), T(7), ALU.mult, TK(sn_i, 7), TK(5))
                    tt(T(3), T(cs_i), T(7), ALU.mult, TK(cs_i, 7), TK(3))
                    tt(T(4), T(sn_i), T(6), ALU.mult, TK(sn_i, 6), TK(4))
                    tt(T(8), T(8), T(5), ALU.subtract, TK(8, 5), TK(8))
                    tt(T(3), T(3), T(4), ALU.add, TK(3, 4), TK(3))
                    cp(xrb[:, 0, :Tn], T(8), TK(8), ["xrb0"], eng="act")
                    cp(xrb[:, 1, :Tn], T(3), TK(3), ["xrb1"], eng="act")
                    for (c0, n, slot, i0, i1) in inis:
                        cp(Xst[:, slot, 0, gp:gp + 1], tmp[:, 8, c0 + n - 1:c0 + n], TK(8), ["Xst"])
                        cp(Xst[:, slot, 1, gp:gp + 1], tmp[:, 3, c0 + n - 1:c0 + n], TK(3), ["Xst"])
                    by = 5
                    mm(by, PS[by][32 * j:32 * j + 32, :Tn], Cc[:, gp, 0, :], xrb[:, 0, :Tn], True, False, r=["Cc", "xrb0"], tp=(0, 32 * j))
                    mm(by, PS[by][32 * j:32 * j + 32, :Tn], Cc[:, gp, 1, :], xrb[:, 1, :Tn], False, True, r=["Cc", "xrb1"], tp=(0, 32 * j))
                    if j == 3:
                        stt(uT[:, quad, :Tn], uT[:, quad, :Tn], vcol[:, 64 + quad:65 + quad], PS[by][:, :Tn], ALU.mult, ALU.add,
                            r=[("uT", quad), "vcol", ("ps", by)], w=[("uT", quad)])
                        act(uT[:, quad, :Tn], uT[:, quad, :Tn], AF.Gelu_apprx_tanh, [("uT", quad)], [("uT", quad)])
                        cp(ub[:, quad, :Tn], uT[:, quad, :Tn], [("uT", quad)], [("ub", quad)])

                for (c0, n, slot) in segs:
                    xr_ = Xst[:, slot, 0, :]
                    xi_ = Xst[:, slot, 1, :]
                    wr0 = winit[:, slot, 0, :]
                    wi0 = winit[:, slot, 1, :]
                    WI = ["winit"]
                    tt(wr0, cth[:], xr_, ALU.mult, ["cth", "Xst"], WI)
                    tt(sm[:, 18, :], sth[:], xi_, ALU.mult, ["sth", "Xst"], ["smi18"])
                    tt(wi0, cth[:], xi_, ALU.mult, ["cth", "Xst"], WI)
                    tt(sm[:, 19, :], sth[:], xr_, ALU.mult, ["sth", "Xst"], ["smi19"])
                    tt(wr0, wr0, sm[:, 18, :], ALU.subtract, WI + ["smi18"], WI)
                    tt(wi0, wi0, sm[:, 19, :], ALU.add, WI + ["smi19"], WI)
                tables(0)
                bu(0)
                for gp in range(32):
                    if gp + 1 < 32:
                        tables(gp + 1)
                        bu(gp + 1)
                    body(gp)
            def part_glu():
                for nb2 in range(2):
                    s, wv = wload(("glu", nb2), wview(w_glu[l], nb2 * 512, 512), 8, 512)
                    for j in range(4):
                        n = nb2 * 4 + j
                        b = bankA()
                        for kq in range(8):
                            mm(b, PS[b][:, :Tn], wv[:, kq, j * 128:(j + 1) * 128], ub[:, kq, :Tn], kq == 0, kq == 7, r=WK(s) + [("ub", kq)])
                        act(T(0), PS[b][:, :Tn], AF.Sigmoid, [("ps", b), "vcol"], TK(0), bias=vcol[:, 72 + n:73 + n], scale=1.0)
                        tt(T(9), uT[:, n, :Tn], T(0), ALU.mult, [("uT", n)] + TK(0), TK(9))
                        s2 = nxt("s", 2)
                        act(sqb[:, s2, :Tn], T(9), AF.Square, TK(9), [("sqb", s2)])
                        mm(5, PS[5][:, :Tn], ones_bf[:], sqb[:, s2, :Tn], n == 0, n == 7, r=["ones_bf", ("sqb", s2)])
                        ts(mT(8 + n, Tn), T(9), vcol[:, 56 + n:57 + n], 0.0, ALU.mult, ALU.add, TK(9) + ["vcol"], [("aT", 8 + n)])
                rs_s = st.setdefault('x', {}).__setitem__('s', nxt("t", 3)) or st['x']['s']
                rstd_from(PS[5][:, :Tn], [("ps", 5)], rs_s, Tn, 1024)

            def part_attn():
                if prompt:
                    nkb = NBT * ti + NBT
                    for h in range(H):
                        bo = 6
                        br_ = 7
                        def s1(kb):
                            diag = kb >= NBT * ti
                            qs = (kb - NBT * ti) * 128 if diag else 0
                            bs = bankA()
                            ktile = kb // NBT
                            mm(bs, PS[bs][:, qs:TP], KTb[:, h, kb * 128:(kb + 1) * 128], qT[:, h, qs:TP], True, False, r=[("KT", h, ktile), ("qT", h)])
                            mm(bs, PS[bs][:, qs:TP], sel2[:, h, :], chl[:, qs:TP], False, not diag, r=["sel2", "chl"])
                            if diag:
                                mm(bs, PS[bs][:, qs:qs + 128], identb[:], negmask[:], False, True, r=["identb", "negmask"])
                            return bs, qs, ktile

                        def s2(kb, bs, qs, ktile):
                            pi = nxt("p", 3)
                            act(Pt[:, pi, qs:TP], PS[bs][:, qs:TP], AF.Exp, [("ps", bs), ("negc", ktile)], [("Pt", pi)], bias=negc[:, kb, h:h + 1], scale=1.0)
                            mm(bo, PS[bo][:, qs:TP], Vb[:, kb, h * 128:(h + 1) * 128], Pt[:, pi, qs:TP], kb == 0, kb == nkb - 1, r=[("V", ktile), ("Pt", pi)])
                            mm(br_, PS[br_][:, qs:TP], ones_bf[:], Pt[:, pi, qs:TP], kb == 0, kb == nkb - 1, r=["ones_bf", ("Pt", pi)])

                        nxt_ = s1(0)
                        for kb in range(nkb):
                            cur_ = nxt_
                            if kb + 1 < nkb:
                                nxt_ = s1(kb + 1)
                            s2(kb, *cur_)
                        attn_epilogue(l, h, Tn, bo, br_)
                else:
                    sample_attention(l)
                rs_a = st.setdefault('x', {}).__setitem__('a', nxt("t", 3)) or st['x']['a']
                rstd_from(PS[4][:, :Tn], [("ps", 4)], rs_a, Tn, 1024)

            if (prompt or os.environ.get("FORKS", "1") == "1") and not os.environ.get("NOFORK"):
                main_l = pr.cur
                lb = []
                pr.cur = lb
                rot["banks"] = [2, 3]
                rot["sbanks"] = [4]
                part_qkv()
                part_attn()
                la = []
                pr.cur = la
                rot["banks"] = [0, 1]
                part_s5()
                rot["banks"] = [0, 1, 2, 3]
                rot["sbanks"] = [4, 5]
                pr.cur = main_l
                na, nb_ = len(la), len(lb)
                ia = ib = 0
                while ia < na or ib < nb_:
                    if ib >= nb_ or (ia < na and ia * nb_ <= ib * na):
                        main_l.append(la[ia]); ia += 1
                    else:
                        main_l.append(lb[ib]); ib += 1
                part_glu()
            else:
                part_qkv()
                part_s5()
                part_glu()
                part_attn()
            rs_a = st['x']['a']
            rs_s = st['x']['s']
            for nb4 in range(4):
                s, wv = wload(("wo", nb4), wview(w_out[l], nb4 * 512, 512), KT, 512)
                for j in range(4):
                    n = nb4 * 4 + j
                    b1 = bankA()
                    for kt in range(8):
                        mm(b1, PS[b1][:, :Tn], wv[:, kt, j * 128:(j + 1) * 128], mT(kt, Tn), kt == 0, kt == 7, r=WK(s) + [("aT", kt)])
                    b2 = bankA()
                    for kt in range(8, 16):
                        mm(b2, PS[b2][:, :Tn], wv[:, kt, j * 128:(j + 1) * 128], mT(kt, Tn), kt == 8, kt == 15, r=WK(s) + [("aT", kt)])
                    tt(T(0), PS[b1][:, :Tn], rstd[:, rs_a, :Tn], ALU.mult, [("ps", b1), ("rstd", rs_a)], TK(0))
                    tt(xT[:, n, :Tn], xT[:, n, :Tn], T(0), ALU.add, [("xT", n)] + TK(0), [("xT", n)])
                    tt(T(1), PS[b2][:, :Tn], rstd[:, rs_s, :Tn], ALU.mult, [("ps", b2), ("rstd", rs_s)], TK(1))
                    tt(xT[:, n, :Tn], xT[:, n, :Tn], T(1), ALU.add, [("xT", n)] + TK(1), [("xT", n)])

            if UPTO < 9:
                return
            norm_x(l, Tn, 16)
            akey = lambda jj: ("ub", jj) if jj < 8 else ("qT", jj - 8) if jj < 16 else ("Pt", jj - 16) if jj < 19 else "xrb0" if jj == 19 else "xrb1" if jj == 20 else "act21"
            for half in range(2):
                for blk in range(11):
                    c0 = (half * 11 + blk) * 256
                    gsrc = wview(w_gate[l], c0, 256)
                    usrc = wview(w_up[l], c0, 256)
                    s = wblock(("gu", half, blk), [(0, 8, 256, gsrc[:, 0:8, :]), (2048, 8, 256, gsrc[:, 8:16, :]),
                                                   (4096, 8, 256, usrc[:, 0:8, :]), (6144, 8, 256, usrc[:, 8:16, :])])
                    gv = wsl[s][:, 0:4096].rearrange("p (k n) -> p k n", n=256)
                    uv = wsl[s][:, 4096:8192].rearrange("p (k n) -> p k n", n=256)
                    for j in range(2):
                        jj = blk * 2 + j
                        bg = bankA()
                        for kt in range(KT):
                            mm(bg, PS[bg][:, :Tn], gv[:, kt, j * 128:(j + 1) * 128], aT[:, kt, :Tn], kt == 0, kt == KT - 1, r=WK(s) + [("aT", kt)])
                        bu_ = bankA()
                        for kt in range(KT):
                            mm(bu_, PS[bu_][:, :Tn], uv[:, kt, j * 128:(j + 1) * 128], aT[:, kt, :Tn], kt == 0, kt == KT - 1, r=WK(s) + [("aT", kt)])
                        t_ = nxt("p", 2)
                        act(T(t_), PS[bg][:, :Tn], AF.Silu, [("ps", bg)], TK(t_))
                        tt(actT[:, jj, :Tn], T(t_), PS[bu_][:, :Tn], ALU.mult, TK(t_) + [("ps", bu_)], [akey(jj)])
                for np_ in range(8):
                    src = w_down[l].rearrange("(k p) n -> p k n", p=128)[:, half * 22:(half + 1) * 22, np_ * 256:(np_ + 1) * 256]
                    s = wblock(("dn", half, np_), [(0, 11, 256, src[:, 0:11, :]), (11 * 256, 11, 256, src[:, 11:22, :])])
                    dv = wsl[s][:, 0:22 * 256].rearrange("p (k n) -> p k n", n=256)
                    for j in range(2):
                        n = np_ * 2 + j
                        b = bankA()
                        for jj in range(22):
                            mm(b, PS[b][:, :Tn], dv[:, jj, j * 128:(j + 1) * 128], actT[:, jj, :Tn], jj == 0, jj == 21, r=WK(s) + [akey(jj)])
                        tt(xT[:, n, :Tn], xT[:, n, :Tn], PS[b][:, :Tn], ALU.add, [("xT", n), ("ps", b)], [("xT", n)])

            if UPTO < 10:
                return
            norm_x(l, Tn, 32)
            for nb4 in range(4):
                s, wv = wload(("pg", nb4), wview(w_pg[l], nb4 * 512, 512), KT, 512)
                wps = nb4 % 2
                dma("pool", wppb[:, wps, :, :], w_pp[l].rearrange("(k p) n -> p k n", p=128)[:, :, nb4 * 512:(nb4 + 1) * 512], r=[], w=[("wpp", wps)])
                for j in range(4):
                    n = nb4 * 4 + j
                    bg = bankA()
                    for kt in range(KT):
                        mm(bg, PS[bg][:, :Tn], wv[:, kt, j * 128:(j + 1) * 128], aT[:, kt, :Tn], kt == 0, kt == KT - 1, r=WK(s) + [("aT", kt)])
                    bp = bankA()
                    for k2 in range(2):
                        mm(bp, PS[bp][:, :Tn], wppb[:, wps, k2, j * 128:(j + 1) * 128], pT[:, k2, :Tn], k2 == 0, k2 == 1, r=[("wpp", wps), "pT"])
                    t_ = nxt("p", 2)
                    act(T(t_), PS[bg][:, :Tn], AF.Sigmoid, [("ps", bg)], TK(t_))
                    tt(T(t_), T(t_), PS[bp][:, :Tn], ALU.mult, TK(t_) + [("ps", bp)], TK(t_))
                    tt(xT[:, n, :Tn], xT[:, n, :Tn], T(t_), ALU.add, [("xT", n)] + TK(t_), [("xT", n)])

            if UPTO < 11:
                return
            if l == 0 and NL > 1:
                for kt in range(KT):
                    dma("sp", hres[kt, :, tok0:tok0 + Tn], xT[:, kt, :Tn], r=[("xT", kt)], w=["hres"])
            else:
                dst = yp if prompt else ys
                for tb in range(nb):
                    for g in range(4):
                        b = bankA()
                        for j in range(4):
                            kt = 4 * g + j
                            tr(b, PS[b][0:bw, j * 128:(j + 1) * 128], xT[:, kt, tb * 128:tb * 128 + bw], ident[:], r=[("xT", kt), "ident"])
                        hf = g // 2
                        cp(stg[0:bw, hf, (g % 2) * 512:(g % 2 + 1) * 512], PS[b][0:bw, :], [("ps", b)], [("stg", hf)], eng="act" if g % 2 else "dve")
                    r0 = ti * TP + tb * 128 if prompt else 0
                    for hf in range(2):
                        dma("sp", dst[r0:r0 + bw, hf * 1024:(hf + 1) * 1024], stg[0:bw, hf, :], r=[("stg", hf)], w=[])

        def mT(kt, Tn):
            return aT[:, kt, :Tn]

        onec = sb("onec", [128, 1])
        ms(onec[:], 1.0, ["onec"])
        kst = sb("kst", [128, 2, NBT, 128])
        vst = sb("vst", [128, 2, 512])
        negcs = sb("negcs", [16, 2, 8])
        negcp = sb("negcp", [128, 16, 8])
        kTs = sb("kTs", [128, H, TS], BF16)
        vbs = sb("vbs", [16, 2, 1024], BF16)
        kc = sb("kc", [128, 1, 1024], BF16)
        vc = sb("vc", [128, 1, 1024], BF16)
        kTc = sb("kTc", [128, H, 128], BF16)
        cl_tok = sb("cl_tok", [128, 16, 8])

        def attn_epilogue(l, h, Tn, bo, br_):
            act(tmpB[:, 0, :Tn], PS[br_][:, :Tn], AF.Ln, r=[("ps", br_)], w=[("tmpB", 0)])
            act(tmpB[:, 0, :Tn], tmpB[:, 0, :Tn], AF.Exp, r=[("tmpB", 0)], w=[("tmpB", 0)], scale=-1.0)
            tt(tmpB[:, 1, :Tn], PS[bo][:, :Tn], tmpB[:, 0, :Tn], ALU.mult, [("ps", bo), ("tmpB", 0)], [("tmpB", 1)])
            s2 = nxt("s", 2)
            act(sqb[:, s2, :Tn], tmpB[:, 1, :Tn], AF.Square, [("tmpB", 1)], [("sqb", s2)])
            mm(4, PS[4][:, :Tn], ones_bf[:], sqb[:, s2, :Tn], h == 0, h == H - 1, r=["ones_bf", ("sqb", s2)])
            ts(mT(h, Tn), tmpB[:, 1, :Tn], vcol[:, 48 + h:49 + h], 0.0, ALU.mult, ALU.add, [("tmpB", 1), "vcol"], [("aT", h)])

        def sample_attention(l):
            Tn = TS
            for si in range(2):
                q0 = si * 16
                for hh in range(2):
                    dma("sp", cl_tok[:, hh * 8:(hh + 1) * 8, :], clf[l, si, hh * 1024:(hh + 1) * 1024, :].rearrange("(t p) x -> p t x", p=128), r=[], w=["cl_tok"])
                cpt = stg[0:8, :, :].rearrange("p a t -> p (a t)").rearrange("p (c t) -> p c t", t=TP)
                PK = [("stg", 0), ("stg", 1)]
                for g in range(4):
                    b = bankA()
                    for j in range(4):
                        kb = g * 4 + j
                        tr(b, PS[b][0:8, j * 128:(j + 1) * 128], cl_tok[:, kb, :], ident[:], r=["cl_tok", "ident"])
                    cp(cpt[:, 2 * g:2 * g + 2, :], PS[b][0:8, :].rearrange("p (a t) -> p a t", t=TP), [("ps", b)], PK)
                for a in range(8):
                    ini = 0.0 if a == 0 else cpt[:, a - 1, TP - 1:TP]
                    A("dve", lambda e, a=a, ini=ini: e.tensor_tensor_scan(
                        out=cpt[:, a, :], data0=onec[0:8, 0:1].to_broadcast([8, TP]), data1=cpt[:, a, :],
                        initial=ini, op0=ALU.mult, op1=ALU.add), r=PK + ["onec"], w=PK)
                cp(sm[0:8, 17, 0:1], cpt[:, 7, TP - 1:TP], PK, ["smi17"])
                ts(cpt[:, 0:8, :], cpt[:, 0:8, :], -1.0, sm[0:8, 17, 0:1], ALU.mult, ALU.add, PK + ["smi17"], PK)
                b = bankA()
                for kb in range(16):
                    tr(b, PS[b][:, kb * 8:(kb + 1) * 8], cpt[:, kb // 2, (kb % 2) * 128:(kb % 2 + 1) * 128], ident[0:8, 0:8], r=PK + ["ident"])
                cp(negcp[:, :, :], PS[b][:, 0:128].rearrange("p (k x) -> p k x", x=8), [("ps", b)], ["negcp"])
                bo = 6
                br_ = 7
                for kb in range(17):
                    past = kb < 16
                    np_ = 128 if past else 16
                    if past:
                        ks = nxt("w", 2)
                        kcs = 0
                        dma("pool", kc[:, kcs, :], ck[l, si, kb * 128:(kb + 1) * 128, :], r=[], w=[("kc", kcs)])
                        dma("pool", vc[:, kcs, :], cv[l, si, kb * 128:(kb + 1) * 128, :], r=[], w=[("vc", kcs)])
                        for g in range(2):
                            b = bankA()
                            for j in range(4):
                                h = g * 4 + j
                                mm(b, PS[b][:, j * 128:(j + 1) * 128], kc[:, kcs, h * 128:(h + 1) * 128], identb[:], True, True, r=[("kc", kcs), "identb"])
                            cp(kTc[:, g * 4:g * 4 + 4, :], PS[b][:, :].rearrange("p (j t) -> p j t", t=128), [("ps", b)], [("kTc", g)], eng="act" if g else "dve")
                    bs = bankA()
                    for h in range(H):
                        if past:
                            mm(bs, PS[bs][:, h * 16:(h + 1) * 16], kTc[:, h, :], qT[:, h, q0:q0 + 16], True, False, r=[("kTc", h // 4), ("qT", h)])
                            mm(bs, PS[bs][:, h * 16:(h + 1) * 16], sel2[:, h, :], chl[:, q0:q0 + 16], False, True, r=["sel2", "chl"])
                        else:
                            mm(bs, PS[bs][0:16, h * 16:(h + 1) * 16], kTs[:, h, q0:q0 + 16], qT[:, h, q0:q0 + 16], True, False, r=[("kTs", h), ("qT", h)])
                            mm(bs, PS[bs][0:16, h * 16:(h + 1) * 16], sel2[:, h, 0:16], chl[:, q0:q0 + 16], False, False, r=["sel2", "chl"])
                            mm(bs, PS[bs][0:16, h * 16:(h + 1) * 16], identb[0:16, 0:16], negmask[0:16, 0:16], False, True, r=["identb", "negmask"])
                    pi = nxt("p", 3)
                    for h in range(H):
                        bias = negcp[:, kb, h:h + 1] if past else negcs[:, si, h:h + 1]
                        act(Pt[0:np_, pi, h * 16:(h + 1) * 16], PS[bs][0:np_, h * 16:(h + 1) * 16], AF.Exp, [("ps", bs), "negcp", "negcs"], [("Pt", pi)], bias=bias, scale=1.0)
                    for h in range(H):
                        vl = vc[:, 0, h * 128:(h + 1) * 128] if past else vbs[:, si, h * 128:(h + 1) * 128]
                        mm(bo, PS[bo][:, h * 16:(h + 1) * 16], vl, Pt[0:np_, pi, h * 16:(h + 1) * 16], kb == 0, kb == 16,
                           r=[("vc", 0), "vbs", ("Pt", pi)])
                    mm(br_, PS[br_][:, 0:128], ones_bf[0:np_, :], Pt[0:np_, pi, 0:128], kb == 0, kb == 16, r=["ones_bf", ("Pt", pi)])
                A("dve", lambda e: e.reciprocal(out=tmpB[:, 0, 0:128], in_=PS[br_][:, 0:128]), r=[("ps", br_)], w=[("tmpB", 0)])
                tt(tmpB[:, 1, 0:128], PS[bo][:, 0:128], tmpB[:, 0, 0:128], ALU.mult, [("ps", bo), ("tmpB", 0)], [("tmpB", 1)])
                s2 = nxt("s", 2)
                act(sqb[:, s2, 0:128], tmpB[:, 1, 0:128], AF.Square, [("tmpB", 1)], [("sqb", s2)])
                for h in range(H):
                    mm(4, PS[4][:, q0:q0 + 16], ones_bf[:], sqb[:, s2, h * 16:(h + 1) * 16], h == 0, h == H - 1, r=["ones_bf", ("sqb", s2)])
                    ts(aT[:, h, q0:q0 + 16], tmpB[:, 1, h * 16:(h + 1) * 16], vcol[:, 48 + h:49 + h], 0.0, ALU.mult, ALU.add, [("tmpB", 1), "vcol"], [("aT", h)])

        for l in range(NL):
            layer_setup(l)
            for ti in range(NT):
                do_tile(l, "p", ti)
            if SAMPLE:
                do_tile(l, "s", 0)
            outs = [(0, 0, srp[l]), (0, 1, sip[l])]
            if SAMPLE:
                outs += [(1, 0, srs[l, 0]), (1, 1, sis[l, 0]), (2, 0, srs[l, 1]), (2, 1, sis[l, 1])]
            for oi, (slot, ri, dst) in enumerate(outs):
                b = bankA()
                tr(b, PS[b][0:32, 0:128], Xst[:, slot, ri, :], ident[:], r=["Xst", "ident"])
                cp(stg[0:32, 0, oi * 128:(oi + 1) * 128], PS[b][0:32, 0:128], [("ps", b)], [("stg", 0)])
                dma("sp", dst, stg[0:32, 0, oi * 128:(oi + 1) * 128], r=[("stg", 0)], w=[])
        print('SBUF remaining', nc.sbuf_bytes_remaining)
        pr.finalize(es)
    return nc


_CACHE = {}
CFG = {}


def kernel(**inp):
    f = lambda a: np.ascontiguousarray(a, dtype=np.float32)
    nc = _CACHE.get("nc")
    if nc is None:
        nc = build(**{k: v for k, v in CFG.items() if k != 'NCORES'})
        _CACHE["nc"] = nc
    wnames = ["g_mix", "w_in", "b_f", "g_q", "g_k", "a_re", "a_im", "log_dt", "b_re", "b_im", "c_re", "c_im", "d_skip",
              "w_glu", "b_glu", "g_attn_out", "g_ssm_out", "w_out", "g_ffn", "w_gate", "w_up", "w_down", "g_ple",
              "w_ple_gate", "w_ple_proj"]
    shared = {n: f(inp[n]) for n in wnames}
    wi = shared["w_in"]
    shared["w_in"] = np.ascontiguousarray(np.concatenate([wi[..., :3072], wi[..., 3080:], wi[..., 3072:3080]], axis=-1))
    in_maps = []
    NCO = CFG.get('NCORES', 8)
    for c in range(NCO):
        m = dict(shared)
        m["xp"] = f(inp["x_prompt"][c])
        m["xs"] = f(inp["x_sample"][2 * c:2 * c + 2].reshape(TS, D))
        m["ck"] = f(inp["cache_k"][:, 2 * c:2 * c + 2].reshape(2, 2, SEQ, 1024))
        m["cv"] = f(inp["cache_v"][:, 2 * c:2 * c + 2].reshape(2, 2, SEQ, 1024))
        m["clf"] = f(inp["cache_logf"][:, 2 * c:2 * c + 2])
        m["sre"] = f(inp["state_ssm_re"][:, 2 * c:2 * c + 2])
        m["sim"] = f(inp["state_ssm_im"][:, 2 * c:2 * c + 2])
        m["pp"] = f(inp["p_prompt"][:, c])
        m["psm"] = f(inp["p_sample"][:, 2 * c:2 * c + 2].reshape(2, TS, 256))
        in_maps.append(m)
    res = run_bass_kernel_spmd(nc, in_maps, core_ids=list(range(NCO)))
    R = list(res.results)
    while len(R) < 8:
        R.append(R[0])
    cat = lambda k: np.stack([r[k] for r in R])
    y_prompt = cat("yp")
    y_sample = cat("ys").reshape(16, 16, D)
    k_prompt = cat("kp").transpose(1, 0, 2, 3).reshape(2, 8, SEQ, 8, 128)
    v_prompt = cat("vp").transpose(1, 0, 2, 3).reshape(2, 8, SEQ, 8, 128)
    logf_prompt = cat("lfp").transpose(1, 0, 2, 3)
    ssm_re_prompt = cat("srp").transpose(1, 0, 2, 3).reshape(2, 8, 64, 64)
    ssm_im_prompt = cat("sip").transpose(1, 0, 2, 3).reshape(2, 8, 64, 64)
    k_sample = cat("kso").transpose(1, 0, 2, 3).reshape(2, 16, 16, 8, 128)
    v_sample = cat("vso").transpose(1, 0, 2, 3).reshape(2, 16, 16, 8, 128)
    logf_sample = cat("lfs").transpose(1, 0, 2, 3).reshape(2, 16, 16, 8)
    ssm_re_sample = cat("srs").transpose(1, 0, 2, 3, 4).reshape(2, 16, 64, 64)
    ssm_im_sample = cat("sis").transpose(1, 0, 2, 3, 4).reshape(2, 16, 64, 64)
    return (y_prompt, y_sample, k_prompt, v_prompt, logf_prompt, ssm_re_prompt, ssm_im_prompt,
            k_sample, v_sample, logf_sample, ssm_re_sample, ssm_im_sample)
```
